# Optimizing a Trainium2 kernel written in Bass

```python
import jax, jax.numpy as jnp
from jax import lax
import numpy as np

D_MODEL = 2048
BATCH = 2
SEQ = 4096
DEPTH = 2
DEC_BATCH = 32
DEC_SEQ = 8
PAST_LEN = 16384
PAGE_SIZE = 128

N_MIXERS = 2
N_CONV_LAYERS = (DEPTH + N_MIXERS - 1) // N_MIXERS
N_ATTN_LAYERS = DEPTH // N_MIXERS
CONV_WIDTH = 3
HEAD_DIM = 64
N_HEADS = D_MODEL // HEAD_DIM
N_KV_HEADS = N_HEADS // 8
GROUP = N_HEADS // N_KV_HEADS
WINDOW = 128
BLOCK = WINDOW
D_FF = 4 * D_MODEL
ROPE_THETA = 10000.0
EPS = 1e-6

kernel_name = "hybrid_shortconv_swa_sink_step"


def rms_norm(x, g):
    xf = x.astype(jnp.float32)
    y = xf * lax.rsqrt(jnp.mean(xf * xf, axis=-1, keepdims=True) + EPS)
    return (y * g.astype(jnp.float32)).astype(x.dtype)


def rope(x, pos):
    half = HEAD_DIM // 2
    inv = ROPE_THETA ** (-jnp.arange(half, dtype=jnp.float32) / half)
    ang = pos.astype(jnp.float32)[:, None] * inv[None, :]
    cos = jnp.cos(ang)[None, :, None, :]
    sin = jnp.sin(ang)[None, :, None, :]
    xf = x.astype(jnp.float32)
    x1, x2 = xf[..., :half], xf[..., half:]
    return jnp.concatenate([x1 * cos - x2 * sin, x2 * cos + x1 * sin], axis=-1).astype(x.dtype)


def short_conv_mixer(h, conv_past, w_in, w_conv, w_out):
    T = h.shape[1]
    b_gate, c_gate, v = jnp.split(h @ w_in, 3, axis=-1)
    u = c_gate * v
    up = jnp.concatenate([conv_past.astype(u.dtype), u], axis=1)
    conv = w_conv[0] * up[:, 0:T]
    for kk in range(1, CONV_WIDTH):
        conv = conv + w_conv[kk] * up[:, kk:kk + T]
    y = (b_gate * conv) @ w_out
    return y, up[:, -(CONV_WIDTH - 1):]


def attn_project(h, pos, w_qkv, q_gain, k_gain):
    B, T, _ = h.shape
    qkv = h @ w_qkv
    q, k, v = jnp.split(qkv, [N_HEADS * HEAD_DIM, (N_HEADS + N_KV_HEADS) * HEAD_DIM], axis=-1)
    q = q.reshape(B, T, N_HEADS, HEAD_DIM)
    k = k.reshape(B, T, N_KV_HEADS, HEAD_DIM)
    v = v.reshape(B, T, N_KV_HEADS, HEAD_DIM)
    q = rope(rms_norm(q, q_gain), pos)
    k = rope(rms_norm(k, k_gain), pos)
    return q, k, v


def sink_softmax(s, sinks):
    sk = sinks.astype(jnp.float32).reshape(N_KV_HEADS, GROUP, 1, 1)
    m = jnp.maximum(jnp.max(s, axis=-1, keepdims=True), sk)
    p = jnp.exp(s - m)
    return p / (jnp.sum(p, axis=-1, keepdims=True) + jnp.exp(sk - m))


def swa_prompt(q, k, v, sinks):
    B, T = q.shape[:2]
    nb = T // BLOCK
    scale = HEAD_DIM ** -0.5
    qb = q.reshape(B, nb, BLOCK, N_KV_HEADS, GROUP, HEAD_DIM)
    kb = k.reshape(B, nb, BLOCK, N_KV_HEADS, HEAD_DIM)
    vb = v.reshape(B, nb, BLOCK, N_KV_HEADS, HEAD_DIM)
    kpad = jnp.zeros_like(kb[:, :1])
    vpad = jnp.zeros_like(vb[:, :1])
    k2 = jnp.concatenate([jnp.concatenate([kpad, kb[:, :-1]], axis=1), kb], axis=2)
    v2 = jnp.concatenate([jnp.concatenate([vpad, vb[:, :-1]], axis=1), vb], axis=2)
    s = jnp.einsum('bnqkgd,bnskd->bnkgqs', qb, k2, preferred_element_type=jnp.float32) * scale
    blk = jnp.arange(nb, dtype=jnp.int32)[:, None]
    qpos = blk * BLOCK + jnp.arange(BLOCK, dtype=jnp.int32)[None, :]
    kpos = (blk - 1) * BLOCK + jnp.arange(2 * BLOCK, dtype=jnp.int32)[None, :]
    diff = qpos[:, :, None] - kpos[:, None, :]
    mask = (diff >= 0) & (diff < WINDOW) & (kpos[:, None, :] >= 0)
    s = jnp.where(mask[None, :, None, None], s, -jnp.inf)
    p = sink_softmax(s, sinks)
    o = jnp.einsum('bnkgqs,bnskd->bnqkgd', p.astype(v.dtype), v2)
    return o.reshape(B, T, N_HEADS * HEAD_DIM)


def swa_sample(q, k_new, v_new, k_buf, v_buf, sinks, past_len):
    B, T = q.shape[:2]
    W = k_buf.shape[1]
    scale = HEAD_DIM ** -0.5
    k_all = jnp.concatenate([k_buf.astype(k_new.dtype), k_new], axis=1)
    v_all = jnp.concatenate([v_buf.astype(v_new.dtype), v_new], axis=1)
    qg = q.reshape(B, T, N_KV_HEADS, GROUP, HEAD_DIM)
    s = jnp.einsum('btkgd,bskd->bkgts', qg, k_all, preferred_element_type=jnp.float32) * scale
    qpos = past_len + jnp.arange(T, dtype=jnp.int32)
    kpos = past_len - W + jnp.arange(W + T, dtype=jnp.int32)
    diff = qpos[:, None] - kpos[None, :]
    mask = (diff >= 0) & (diff < WINDOW)
    s = jnp.where(mask[None, None, None], s, -jnp.inf)
    p = sink_softmax(s, sinks)
    o = jnp.einsum('bkgts,bskd->btkgd', p.astype(v_all.dtype), v_all)
    return o.reshape(B, T, N_HEADS * HEAD_DIM), k_all[:, -W:], v_all[:, -W:]


def sq_relu_mlp(h, w_up, w_down):
    a = jax.nn.relu(h @ w_up)
    return (a * a) @ w_down


def setup_inputs(seed: int = 0) -> dict:
    key = jax.random.key(seed)
    ks = jax.random.split(key, 20)
    f32 = jnp.float32
    D = D_MODEL
    W = min(WINDOW, PAST_LEN)
    qkv_cols = (N_HEADS + 2 * N_KV_HEADS) * HEAD_DIM
    nrm = lambda k, shape, s: jax.random.normal(k, shape, f32) * s
    return {
        "x_prompt": nrm(ks[0], (BATCH, SEQ, D), 1.0),
        "x_sample": nrm(ks[1], (DEC_BATCH, DEC_SEQ, D), 1.0),
        "state_conv": nrm(ks[2], (N_CONV_LAYERS, DEC_BATCH, CONV_WIDTH - 1, D), 1.0),
        "cache_k_win": nrm(ks[3], (N_ATTN_LAYERS, DEC_BATCH, W, N_KV_HEADS, HEAD_DIM), 1.0),
        "cache_v_win": nrm(ks[4], (N_ATTN_LAYERS, DEC_BATCH, W, N_KV_HEADS, HEAD_DIM), 1.0),
        "ln_mix": 1.0 + nrm(ks[5], (DEPTH, D), 0.02),
        "ln_mlp": 1.0 + nrm(ks[6], (DEPTH, D), 0.02),
        "w_conv_in": nrm(ks[7], (N_CONV_LAYERS, D, 3 * D), D ** -0.5),
        "w_conv": nrm(ks[8], (N_CONV_LAYERS, CONV_WIDTH, D), CONV_WIDTH ** -0.5),
        "w_conv_out": nrm(ks[9], (N_CONV_LAYERS, D, D), D ** -0.5),
        "w_qkv": nrm(ks[10], (N_ATTN_LAYERS, D, qkv_cols), D ** -0.5),
        "w_attn_out": nrm(ks[11], (N_ATTN_LAYERS, N_HEADS * HEAD_DIM, D), (N_HEADS * HEAD_DIM) ** -0.5),
        "q_norm": 1.0 + nrm(ks[12], (N_ATTN_LAYERS, HEAD_DIM), 0.02),
        "k_norm": 1.0 + nrm(ks[13], (N_ATTN_LAYERS, HEAD_DIM), 0.02),
        "sinks": nrm(ks[14], (N_ATTN_LAYERS, N_HEADS), 1.0),
        "w_up": nrm(ks[15], (DEPTH, D, D_FF), D ** -0.5),
        "w_down": nrm(ks[16], (DEPTH, D_FF, D), D_FF ** -0.5),
    }


def reference(x_prompt, x_sample, state_conv, cache_k_win, cache_v_win, ln_mix, ln_mlp,
              w_conv_in, w_conv, w_conv_out, w_qkv, w_attn_out, q_norm, k_norm, sinks,
              w_up, w_down):
    yp, ys = x_prompt, x_sample
    Bp, Tp, D = x_prompt.shape
    Ts = x_sample.shape[1]
    pos_p = jnp.arange(Tp, dtype=jnp.int32)
    pos_s = PAST_LEN + jnp.arange(Ts, dtype=jnp.int32)
    conv_p, conv_s, kp, vp, kss, vss = [], [], [], [], [], []
    for i in range(DEPTH):
        j = i // N_MIXERS
        hp = rms_norm(yp, ln_mix[i])
        hs = rms_norm(ys, ln_mix[i])
        if i % N_MIXERS == 0:
            zero_past = jnp.zeros((Bp, CONV_WIDTH - 1, D), hp.dtype)
            op, sp = short_conv_mixer(hp, zero_past, w_conv_in[j], w_conv[j], w_conv_out[j])
            os_, ss = short_conv_mixer(hs, state_conv[j], w_conv_in[j], w_conv[j], w_conv_out[j])
            conv_p.append(sp)
            conv_s.append(ss)
        else:
            q, k, v = attn_project(hp, pos_p, w_qkv[j], q_norm[j], k_norm[j])
            op = swa_prompt(q, k, v, sinks[j]) @ w_attn_out[j]
            kp.append(k[:, -WINDOW:])
            vp.append(v[:, -WINDOW:])
            q2, k2, v2 = attn_project(hs, pos_s, w_qkv[j], q_norm[j], k_norm[j])
            o2, kb, vb = swa_sample(q2, k2, v2, cache_k_win[j], cache_v_win[j], sinks[j], PAST_LEN)
            os_ = o2 @ w_attn_out[j]
            kss.append(kb)
            vss.append(vb)
        yp = yp + op
        ys = ys + os_
        yp = yp + sq_relu_mlp(rms_norm(yp, ln_mlp[i]), w_up[i], w_down[i])
        ys = ys + sq_relu_mlp(rms_norm(ys, ln_mlp[i]), w_up[i], w_down[i])
    return (yp, ys, jnp.stack(conv_p), jnp.stack(conv_s), jnp.stack(kp), jnp.stack(vp), jnp.stack(kss), jnp.stack(vss))
```

```python
import numpy as np
from contextlib import ExitStack
import concourse.bass as bass
import concourse.mybir as mybir
from concourse.bass_utils import run_bass_kernel_spmd

F32 = mybir.dt.float32
BF16 = mybir.dt.bfloat16
AF = mybir.ActivationFunctionType
ALU = mybir.AluOpType
AX = mybir.AxisListType

D = 2048
DFF = 8192
NT = 1186
EPS = 1e-6
NCORES = 8
PAST = 16384


class _Op:
    __slots__ = ("eng", "fn", "deps", "signal", "key", "tick", "is_dma")

    def __init__(self, eng, fn, key, is_dma):
        self.eng = eng
        self.fn = fn
        self.deps = []
        self.signal = is_dma
        self.key = key
        self.tick = 0
        self.is_dma = is_dma


class Sched:
    ENGS = ("pe", "act", "dve", "pool", "sp")

    def __init__(self, nc, same_engine_sync=("act", "dve", "pool")):
        self.nc = nc
        self.streams = {e: [] for e in self.ENGS}
        self.last_writer = {}
        self.readers = {}
        self.same_sync = set(same_engine_sync)
        self.dma_counts = {}
        self.out_dmas = []

    def _add(self, op, reads, writes):
        deps = {}
        for r in reads:
            w = self.last_writer.get(r)
            if w is not None:
                deps[id(w)] = w
        for r in writes:
            w = self.last_writer.get(r)
            if w is not None:
                deps[id(w)] = w
            for rd in self.readers.get(r, ()):
                deps[id(rd)] = rd
        op.deps = list(deps.values())
        for r in reads:
            self.readers.setdefault(r, []).append(op)
        for r in writes:
            self.last_writer[r] = op
            self.readers[r] = []
        self.streams[op.eng].append(op)
        return op

    def op(self, eng, fn, reads=(), writes=()):
        return self._add(_Op(eng, fn, eng, False), reads, writes)

    def dma(self, eng, fn, slot, reads=(), writes=(), is_output=False):
        op = _Op(eng, fn, ("dma", slot), True)
        c = self.dma_counts.get(slot, 0) + 16
        self.dma_counts[slot] = c
        op.tick = c
        self._add(op, reads, writes)
        self.out_dmas.append(op)
        return op

    def finalize(self):
        fin = _Op("sp", None, "sp", False)
        fin.deps = list(self.out_dmas)
        self.streams["sp"].append(fin)
        for e in self.ENGS:
            for op in self.streams[e]:
                for d in op.deps:
                    if d.is_dma:
                        continue
                    if d.eng == op.eng and not op.is_dma and d.eng not in self.same_sync:
                        continue
                    d.signal = True
        for e in self.ENGS:
            t = 0
            for op in self.streams[e]:
                if op.is_dma:
                    continue
                if op.signal:
                    t += 1
                    op.tick = t

    def emit(self, stack):
        nc = self.nc
        self.finalize()
        sems = {}
        import os
        for i in range(int(os.environ.get("SEM_PAD", "0"))):
            stack.enter_context(nc.semaphore("pad%d" % i))
        for e in self.ENGS:
            sems[e] = stack.enter_context(nc.semaphore("s_" + e))
        for i, slot in enumerate(self.dma_counts):
            sems[("dma", slot)] = stack.enter_context(nc.semaphore("d%d" % i))
        block = stack.enter_context(nc.Block())
        sched = self

        def run(ename, eng):
            waited = {}
            for op in sched.streams[ename]:
                need = {}
                for d in op.deps:
                    if (not d.is_dma) and d.eng == ename and (not op.is_dma) and ename not in sched.same_sync:
                        continue
                    k = d.key
                    if need.get(k, 0) < d.tick:
                        need[k] = d.tick
                for k, t in need.items():
                    if waited.get(k, 0) >= t:
                        continue
                    eng.wait_ge(sems[k], t)
                    waited[k] = t
                if op.fn is None:
                    continue
                ins = op.fn(eng)
                if op.is_dma:
                    ins.then_inc(sems[op.key], 16)
                elif op.signal:
                    ins.then_inc(sems[ename], 1)

        @block.tensor
        def _(e):
            run("pe", e)

        @block.scalar
        def _(e):
            run("act", e)

        @block.vector
        def _(e):
            run("dve", e)

        @block.gpsimd
        def _(e):
            run("pool", e)

        @block.sync
        def _(e):
            run("sp", e)


UNIT = 2048
NUNITS = 6


class WStream:
    def __init__(self, S, ring):
        self.S = S
        self.ring = ring
        self.pieces = []
        self.views = {}
        self.next_issue = 0
        self.ptr = 0
        self.free = [True] * NUNITS
        self.units_of = {}
        self.hold_keys = []

    def add(self, src, nunits, a, b):
        self.pieces.append((src, nunits, a, b))
        return len(self.pieces) - 1

    def _try_issue(self):
        if self.next_issue >= len(self.pieces):
            return False
        src, nu, a, b = self.pieces[self.next_issue]
        p = self.ptr
        if p + nu > NUNITS:
            p = 0
        if not all(self.free[p:p + nu]):
            return False
        for u in range(p, p + nu):
            self.free[u] = False
        pid = self.next_issue
        self.units_of[pid] = (p, nu)
        view = self.ring[:, p * UNIT:p * UNIT + a * b].rearrange("p (a b) -> p a b", b=b)
        keys = [("ring", u) for u in range(p, p + nu)]
        self.views[pid] = (view, keys)
        extra = self.hold_keys if (2 <= pid < 6) else []
        self.S.dma("pool", lambda e, view=view, src=src: e.dma_start(out=view, in_=src),
                   ("ring", p), reads=extra, writes=keys)
        self.ptr = p + nu
        self.next_issue += 1
        return True

    def get(self, pid):
        while self._try_issue():
            pass
        assert pid in self.views, "ring deadlock: piece %d not issued" % pid
        return self.views[pid]

    def done(self, pid):
        p, nu = self.units_of[pid]
        for u in range(p, p + nu):
            self.free[u] = True
        while self._try_issue():
            pass


class _Stop(Exception):
    pass


def build_program(skip_mlp1=False, stage=99):
    nc = bass.Bass("TRN2", target_bir_lowering=False)

    def din(name, shape):
        return nc.dram_tensor(name, shape, F32, kind="ExternalInput").ap()

    def dout(name, shape):
        return nc.dram_tensor(name, shape, F32, kind="ExternalOutput").ap()

    xtok = din("xtok", [NT, D])
    swc = din("swc", [11, D])
    ck = din("ck", [4, 128, 256])
    cv = din("cv", [4, 128, 256])
    ln = din("ln", [4, D])
    w_in = din("w_in", [D, 3 * D])
    w_out = din("w_out", [D, D])
    w_qkv = din("w_qkv", [D, 2560])
    w_ao = din("w_ao", [D, D])
    qn = din("qn", [1, 64])
    kn = din("kn", [1, 64])
    sinks = din("sinks", [1, 32])
    w_up = din("w_up", [2, D, DFF])
    w_dn = din("w_dn", [2, DFF, D])
    cos_t = din("cos_t", [128, 10, 64])
    sin_t = din("sin_t", [128, 10, 64])
    masks = din("masks", [128, 3, 128])
    mask_s = din("mask_s", [32, 32])

    y_o = dout("y", [1056, D])
    ncv_o = dout("ncv", [10, D])
    kp_o = dout("kp", [128, 256])
    vp_o = dout("vp", [128, 256])
    ks_o = dout("ks", [4, 128, 256])
    vs_o = dout("vs", [4, 128, 256])

    with ExitStack() as st:
        def sb(name, shape, dt):
            return st.enter_context(nc.sbuf_tensor(name, shape, dt))

        X = sb("X", [128, 10, D], F32)
        HT = sb("HT", [128, 16 * NT], BF16)
        ACT_T = sb("ACT_T", [128, 8 * NT], BF16)
        RING = sb("RING", [128, NUNITS * UNIT], BF16)
        TMP = sb("TMP", [128, 3600], F32)
        ATT = sb("ATT", [128, 4352], F32)
        identf = sb("identf", [128, 128], F32)
        ident = sb("ident", [128, 128], BF16)
        ones64 = sb("ones64", [128, 64], BF16)
        maskb = sb("maskb", [128, 3, 128], BF16)
        mask01 = sb("mask01", [128, 3, 128], BF16)
        msb = sb("msb", [32, 32], BF16)
        COS = sb("COS", [128, 10, 64], F32)
        SIN = sb("SIN", [128, 10, 64], F32)
        SWT = sb("SWT", [128, 16, 11], F32)
        UK = sb("UK", [128, 16, 10], F32)
        SS = sb("SS", [128, 16], F32)
        GQ = sb("GQ", [128, 64], F32)
        GK = sb("GK", [128, 64], F32)
        SK = sb("SK", [128, 32], F32)
        SE0 = sb("SE0", [128, 32], F32)
        SE = sb("SE", [128, 16], F32)
        SM = sb("SM", [128, 8], F32)
        EPSB = sb("EPSB", [128, 1], F32)
        S4T = sb("S4T", [128, 8], F32)
        JUNK = sb("JUNK", [128, 256], BF16)
        ps = st.enter_context(nc.psum_tensor("ps", [128, 8, 512], F32))
        psb = ps.bitcast(BF16)

        HT3 = HT[:].rearrange("p (c n) -> p c n", n=NT)
        AT3 = ACT_T[:].rearrange("p (c n) -> p c n", n=NT)
        T0 = TMP[:, 0:1200]
        T1 = TMP[:, 1200:2400]
        T2 = TMP[:, 2400:3600]
        GB = TMP[:, 0:2048]
        KT2 = TMP[:].bitcast(BF16)[:, 0:4 * 1696].rearrange("p (k n) -> p k n", n=1696)
        ALLT = ["T0", "T1", "T1h", "T1s", "T1p", "T2", "T2s"]
        ATTb = ATT[:].bitcast(BF16)
        V = ATTb[:, 0:2560].rearrange("p (t n) -> p t n", n=256)
        Vc = ATTb[:, 2560:3584].rearrange("p (b n) -> p b n", n=256)
        XS = [ATT[:, 1792:2048], ATT[:, 2048:2304]]
        AB = [ATT[:, 2304:2560], ATT[:, 2560:2816]]
        BB = [ATT[:, 2816:3072], ATT[:, 3072:3328]]
        RD = ATT[:, 1792:2304]
        RDK = ["XS0", "XS1"]
        QRB = [ATTb[:, 6656:6912], ATTb[:, 6912:7168]]
        KDUP = ATTb[:, 7168:7680]
        KRBS = [ATT[:, 3840:4096], ATT[:, 4096:4352]]
        CKB = ATTb[:, 4608:5632].rearrange("p (b n) -> p b n", n=256)
        CKBK = ["AB0", "AB1"]
        X0b = X[:, 0, :].bitcast(BF16)
        PT = [X0b[:, 0:2048], X0b[:, 2048:4096]]
        PTK = [[("X", 0, 0), ("X", 0, 1)], [("X", 0, 2), ("X", 0, 3)]]
        HB_ATT = ([ATTb[:, 0:2048], ATTb[:, 2048:4096], ATTb[:, 4096:6144]],
                  [[("V", t) for t in range(8)], [("V", 8), ("V", 9), "VC", "XS0"], ["XS1", "AB0", "AB1", "BB0"]])
        HB_X0 = (PT, PTK)

        S = Sched(nc)
        W = WStream(S, RING[:])
        W.hold_keys = [("X", 9, 0)]

        def XK(t):
            return [("X", t, g) for g in range(4)]

        HK = [("H", t) for t in range(10)]

        def wv(src, c0, ncol):
            return src[:, c0:c0 + ncol].rearrange("(k p) n -> p k n", p=128)

        def wr(src, r0, nk, c0, ncol):
            return src[r0:r0 + nk * 128, c0:c0 + ncol].rearrange("(k p) n -> p k n", p=128)

        P_conv = []
        for gi in range(2):
            up = []
            for j in range(8):
                J = gi * 8 + j
                up.append((W.add(wv(w_in, 2048 + J * 128, 128), 1, 16, 128),
                           W.add(wv(w_in, 4096 + J * 128, 128), 1, 16, 128),
                           W.add(wv(w_in, J * 128, 128), 1, 16, 128)))
            dn = [W.add(wr(w_out, gi * 1024, 8, ng * 512, 512), 2, 8, 512) for ng in range(4)]
            P_conv.append((up, dn))

        def plan_mlp(l):
            out = []
            for gi in range(8):
                up = [W.add(wv(w_up[l], (gi * 8 + 2 * pi) * 128, 256), 2, 16, 256) for pi in range(4)]
                dn = [W.add(wr(w_dn[l], gi * 1024, 8, ng * 512, 512), 2, 8, 512) for ng in range(4)]
                out.append((up, dn))
            return out

        P_mlp0 = plan_mlp(0)
        P_k = W.add(wv(w_qkv, 2048, 256), 2, 16, 256)
        P_v = W.add(wv(w_qkv, 2304, 256), 2, 16, 256)
        P_att = []
        for g in range(4):
            qp = [W.add(wv(w_qkv, g * 512 + qh * 256, 256), 2, 16, 256) for qh in range(2)]
            ao = [W.add(wr(w_ao, g * 512, 4, hf * 1024, 1024), 2, 4, 1024) for hf in range(2)]
            P_att.append((qp, ao))
        P_mlp1 = plan_mlp(1)

        for t in range(10):
            rows = 128 if t < 9 else 34
            S.dma("sp", lambda e, t=t, rows=rows: e.dma_start(out=X[0:rows, t, :], in_=xtok[t * 128:t * 128 + rows, :]),
                  ("x", t), writes=XK(t))
        S.dma("sp", lambda e: e.dma_start(out=X[64:75, 9, :], in_=swc), "swc", writes=["X9hi"])
        S.dma("sp", lambda e: e.dma_start(out=COS[:], in_=cos_t), "cos", writes=["COS"])
        S.dma("sp", lambda e: e.dma_start(out=SIN[:], in_=sin_t), "sin", writes=["SIN"])
        S.dma("sp", lambda e: e.dma_start(out=GQ[:], in_=qn.partition_broadcast(128)), "gq", writes=["GQ"])
        S.dma("sp", lambda e: e.dma_start(out=GK[:], in_=kn.partition_broadcast(128)), "gk", writes=["GK"])
        S.dma("sp", lambda e: e.dma_start(out=SK[:], in_=sinks.partition_broadcast(128)), "sk", writes=["SK"])
        S.dma("pool", lambda e: e.dma_start(out=maskb[:], in_=masks), "maskb", writes=["maskb"])
        S.dma("pool", lambda e: e.dma_start(out=msb[:], in_=mask_s), "msb", writes=["msb"])
        S.op("pool", lambda e: e.memset(identf[:], 1.0), writes=["identf"])
        S.op("pool", lambda e: e.affine_select(out=identf[:], in_=identf[:], pattern=[[-1, 128]],
                                               compare_op=ALU.is_equal, fill=0.0, base=0, channel_multiplier=1),
             reads=["identf"], writes=["identf"])
        S.op("dve", lambda e: e.tensor_copy(out=ident[:], in_=identf[:]), reads=["identf"], writes=["ident"])
        S.op("dve", lambda e: e.tensor_scalar(out=mask01[:], in0=maskb[:], scalar1=0.0, scalar2=None, op0=ALU.is_equal),
             reads=["maskb"], writes=["mask01"])
        S.op("dve", lambda e: e.memset(ones64[:], 1.0), writes=["ones64"])
        S.op("dve", lambda e: e.memset(EPSB[:], EPS), writes=["EPSB"])

        def swt_mm(e):
            for c in range(16):
                ins = e.matmul(ps[:, 0, c * 11:(c + 1) * 11], lhsT=X[64:75, 9, c * 128:(c + 1) * 128],
                               rhs=identf[64:75, 64:75], start=True, stop=True)
            return ins
        S.op("pe", swt_mm, reads=["X9hi", "identf"], writes=[("ps", 0)])
        S.op("act", lambda e: e.activation(out=SWT[:].rearrange("p c n -> p (c n)"), in_=ps[:, 0, 0:176], func=AF.Copy),
             reads=[("ps", 0)], writes=["SWT"])

        norm_cnt = [0]

        class NormPipe:
            def __init__(self, li, tiles, rows9, hbsel):
                Hb, HbK = hbsel
                S.dma("sp", lambda e: e.dma_start(out=GB, in_=ln[li:li + 1, :].partition_broadcast(128)),
                      "gb", writes=["T0", "T1", "T1h"])
                self.info = []
                for t in tiles:
                    i = norm_cnt[0]
                    norm_cnt[0] += 1
                    self.info.append((t, 128 if t < 9 else rows9, Hb[i % len(Hb)], HbK[i % len(Hb)], i % 16, (i % 2) * 2))
                self.nbuf = len(Hb)
                self.idx = {t: k for k, t in enumerate(tiles)}
                self.na = 0
                self.nb = 0

            def _a(self, t, rows, hb, hk, col, b0):
                S.op("act", lambda e: e.activation(
                    out=hb[0:rows, :], in_=X[0:rows, t, :], func=AF.Square, accum_out=SS[0:rows, col:col + 1]),
                    reads=XK(t), writes=hk + [("SS", col)])
                S.op("act", lambda e: e.activation(
                    out=SS[0:rows, col:col + 1], in_=SS[0:rows, col:col + 1], func=AF.Sqrt,
                    bias=EPSB[0:rows, :], scale=1.0 / D), reads=[("SS", col), "EPSB"], writes=[("SS", col)])
                S.op("dve", lambda e: e.reciprocal(out=SS[0:rows, col:col + 1], in_=SS[0:rows, col:col + 1]),
                     reads=[("SS", col)], writes=[("SS", col)])
                S.op("dve", lambda e: e.scalar_tensor_tensor(
                    out=hb[0:rows, :], in0=X[0:rows, t, :], scalar=SS[0:rows, col:col + 1], in1=GB[0:rows, :],
                    op0=ALU.mult, op1=ALU.mult), reads=XK(t) + [("SS", col), "T0", "T1", "T1h"], writes=hk)

            def _b(self, t, rows, hb, hk, col, b0):
                def tr(e):
                    for c in range(16):
                        ins = e.transpose(out=psb[:, b0 + c // 8, (c % 8) * 128:(c % 8) * 128 + rows],
                                          in_=hb[0:rows, c * 128:(c + 1) * 128], identity=ident[0:rows, 0:rows])
                    return ins
                S.op("pe", tr, reads=hk + ["ident"], writes=[("ps", b0), ("ps", b0 + 1)])
                for half in range(2):
                    src = psb[:, b0 + half, :].rearrange("p (c n) -> p c n", n=128)[:, :, 0:rows]
                    dst = HT3[:, half * 8:(half + 1) * 8, t * 128:t * 128 + rows]
                    if half == 0:
                        S.op("act", lambda e, src=src, dst=dst: e.activation(out=dst, in_=src, func=AF.Copy),
                             reads=[("ps", b0 + half)], writes=[("H", t, half)])
                    else:
                        S.op("dve", lambda e, src=src, dst=dst: e.tensor_copy(out=dst, in_=src),
                             reads=[("ps", b0 + half)], writes=[("H", t, half)])

            def tile_ready(self, t):
                if t not in self.idx:
                    return
                assert self.idx[t] == self.na
                if self.nbuf >= 3:
                    self._a(*self.info[self.na])
                    self.na += 1
                    if self.na >= 3:
                        self._b(*self.info[self.nb])
                        self.nb += 1
                else:
                    if self.na >= 2:
                        self._b(*self.info[self.nb])
                        self.nb += 1
                    self._a(*self.info[self.na])
                    self.na += 1

            def finish(self):
                n = len(self.info)
                while self.nb < n:
                    if self.na < n and (self.na - self.nb) < 2:
                        self._a(*self.info[self.na])
                        self.na += 1
                        continue
                    self._b(*self.info[self.nb])
                    self.nb += 1

        def HKall(tiles):
            return [("H", t, h) for t in tiles for h in range(2)]

        def fm_mm(Wv, c_lo, bank0, tgs):
            def f(e):
                for k in range(16):
                    for gi, (c0, c1) in enumerate(tgs):
                        ins = e.matmul(ps[:, bank0 + gi, 0:c1 - c0], lhsT=Wv[:, k, c_lo:c_lo + 128],
                                       rhs=HT3[:, k, c0:c1], start=(k == 0), stop=(k == 15))
                return ins
            return f

        def flat(bank0, n):
            return ps[:, bank0:bank0 + 3, :].rearrange("p b n -> p (b n)")[:, 0:n]

        def BK(bank0):
            return [("ps", bank0), ("ps", bank0 + 1), ("ps", bank0 + 2)]

        dn_cnt = [0]

        def down_phase(pieces, nk, tiles, colbase, rows9, akeys, hook=None):
            def one(Wv, wk, ngi, t):
                rows = 128 if t < 9 else rows9
                lc = t * 128 - colbase
                bank = 6 + (dn_cnt[0] % 2)
                dn_cnt[0] += 1
                xg = ngi

                def mm(e):
                    for c in range(nk):
                        ins = e.matmul(ps[0:rows, bank, :], lhsT=AT3[:, c, lc:lc + rows],
                                       rhs=Wv[:, c, :], start=(c == 0), stop=(c == nk - 1))
                    return ins
                S.op("pe", mm, reads=wk + akeys, writes=[("ps", bank)])
                S.op("dve", lambda e: e.tensor_tensor(
                    out=X[0:rows, t, xg * 512:(xg + 1) * 512], in0=X[0:rows, t, xg * 512:(xg + 1) * 512],
                    in1=ps[0:rows, bank, :], op=ALU.add), reads=[("ps", bank), ("X", t, xg)], writes=[("X", t, xg)])

            first = pieces if hook is None else pieces[:-2]
            for ngi, pid in enumerate(first):
                Wv, wk = W.get(pid)
                for t in tiles:
                    one(Wv, wk, ngi, t)
                W.done(pid)
            if hook is not None:
                n0 = len(pieces) - 2
                Wa_, wka_ = W.get(pieces[n0])
                Wb_, wkb_ = W.get(pieces[n0 + 1])
                for t in tiles:
                    one(Wa_, wka_, n0, t)
                    one(Wb_, wkb_, n0 + 1, t)
                    hook(t)
                W.done(pieces[n0])
                W.done(pieces[n0 + 1])

        def ckpt(k):
            if k > stage:
                raise _Stop()

        def record():
            TG0 = [(0, 512), (512, 1024), (1024, NT)]
            ALL10 = list(range(10))
            NormPipe(0, ALL10, 34, HB_ATT).finish()
            ckpt(2)
            HA = HKall(ALL10)
            ubuf = T1
            ubs = T1[:, 1154:1194].rearrange("p (b n) -> p b n", n=10)
            t1 = T2
            csb = T0
            setc = [0]

            def nextset():
                s = (setc[0] % 2) * 3
                setc[0] += 1
                return s

            for gi in range(2):
                up, dn = P_conv[gi]
                for j in range(8):
                    J = gi * 8 + j
                    pc, pv, pb = up[j]
                    w0 = SWT[:, J, 8:9]
                    w1 = SWT[:, J, 9:10]
                    w2 = SWT[:, J, 10:11]
                    Wc, wkc = W.get(pc)
                    sA = nextset()
                    S.op("pe", fm_mm(Wc, 0, sA, TG0), reads=wkc + HA, writes=BK(sA))
                    W.done(pc)
                    S.op("act", lambda e, sA=sA: e.activation(out=csb[:, 0:NT], in_=flat(sA, NT), func=AF.Copy),
                         reads=BK(sA), writes=["T0"])
                    Wvv, wkv = W.get(pv)
                    sB = nextset()
                    S.op("pe", fm_mm(Wvv, 0, sB, TG0), reads=wkv + HA, writes=BK(sB))
                    W.done(pv)
                    S.op("dve", lambda e, sB=sB: e.tensor_tensor(out=ubuf[:, 2:1154], in0=csb[:, 0:1152], in1=flat(sB, NT)[:, 0:1152], op=ALU.mult),
                         reads=BK(sB) + ["T0"], writes=["T1"])
                    S.op("dve", lambda e, sB=sB: e.tensor_tensor(
                        out=ubs[:, :, 2:10], in0=csb[:, 1152:1184].rearrange("p (b n) -> p b n", n=8),
                        in1=flat(sB, NT)[:, 1152:1184].rearrange("p (b n) -> p b n", n=8), op=ALU.mult),
                        reads=BK(sB) + ["T0"], writes=["T1s"])
                    S.op("dve", lambda e, sB=sB: e.tensor_tensor(out=ubuf[:, 0:2], in0=csb[:, 1184:1186], in1=flat(sB, NT)[:, 1184:1186], op=ALU.mult),
                         reads=BK(sB) + ["T0"], writes=["T1h"])
                    S.op("pool", lambda e, J=J: e.tensor_copy(out=ubs[:, :, 0:2], in_=SWT[:, J, 0:8].rearrange("p (b n) -> p b n", n=2)),
                         reads=["SWT"], writes=["T1p"])
                    S.op("pool", lambda e, J=J: e.tensor_copy(out=UK[:, J, 0:2], in_=ubuf[:, 1152:1154]), reads=["T1"], writes=[("UK", J, 0)])
                    S.op("pool", lambda e, J=J: e.tensor_copy(out=UK[:, J, 2:10].rearrange("p (b n) -> p b n", n=2), in_=ubs[:, :, 8:10]),
                         reads=["T1s"], writes=[("UK", J, 1)])
                    UR = ["T1", "T1h"]
                    S.op("dve", lambda e, w0=w0: e.tensor_scalar(out=t1[:, 0:1152], in0=ubuf[:, 0:1152], scalar1=w0, scalar2=None, op0=ALU.mult),
                         reads=UR + ["SWT"], writes=["T2"])
                    S.op("dve", lambda e, w1=w1: e.scalar_tensor_tensor(out=t1[:, 0:1152], in0=ubuf[:, 1:1153], scalar=w1, in1=t1[:, 0:1152], op0=ALU.mult, op1=ALU.add),
                         reads=UR + ["T2"], writes=["T2"])
                    S.op("dve", lambda e, w2=w2: e.scalar_tensor_tensor(out=t1[:, 0:1152], in0=ubuf[:, 2:1154], scalar=w2, in1=t1[:, 0:1152], op0=ALU.mult, op1=ALU.add),
                         reads=UR + ["T2"], writes=["T2"])
                    t1s = t1[:, 1152:1184].rearrange("p (b n) -> p b n", n=8)
                    USR = ["T1s", "T1p"]
                    S.op("dve", lambda e, w0=w0, t1s=t1s: e.tensor_scalar(out=t1s, in0=ubs[:, :, 0:8], scalar1=w0, scalar2=None, op0=ALU.mult),
                         reads=USR + ["SWT"], writes=["T2s"])
                    S.op("dve", lambda e, w1=w1, t1s=t1s: e.scalar_tensor_tensor(out=t1s, in0=ubs[:, :, 1:9], scalar=w1, in1=t1s, op0=ALU.mult, op1=ALU.add),
                         reads=USR + ["T2s"], writes=["T2s"])
                    S.op("dve", lambda e, w2=w2, t1s=t1s: e.scalar_tensor_tensor(out=t1s, in0=ubs[:, :, 2:10], scalar=w2, in1=t1s, op0=ALU.mult, op1=ALU.add),
                         reads=USR + ["T2s"], writes=["T2s"])
                    Wb, wkb = W.get(pb)
                    sC = nextset()
                    S.op("pe", fm_mm(Wb, 0, sC, TG0), reads=wkb + HA, writes=BK(sC))
                    W.done(pb)
                    S.op("dve", lambda e, sC=sC, j=j: e.tensor_tensor(out=AT3[:, j, 0:1184], in0=flat(sC, NT)[:, 0:1184], in1=t1[:, 0:1184], op=ALU.mult),
                         reads=BK(sC) + ["T2", "T2s"], writes=[("A", j)])
                if gi == 0:
                    down_phase(dn, 8, ALL10, 0, 32, [("A", c) for c in range(8)])
                else:
                    def uk_mm(e):
                        for c in range(16):
                            ins = e.matmul(ps[0:10, c // 4, (c % 4) * 128:(c % 4 + 1) * 128], lhsT=UK[:, c, :], rhs=identf[:, :],
                                           start=True, stop=True)
                        return ins
                    S.op("pe", uk_mm, reads=[("UK", J, h) for J in range(16) for h in range(2)] + ["identf"],
                         writes=[("ps", b) for b in range(4)])
                    S.op("act", lambda e: e.activation(out=TMP[0:10, 0:2048], in_=ps[0:10, 0:4, :].rearrange("p b n -> p (b n)"), func=AF.Copy),
                         reads=[("ps", b) for b in range(4)], writes=["T0", "T1", "T1h"])
                    S.dma("sp", lambda e: e.dma_start(out=ncv_o, in_=TMP[0:10, 0:2048]), "ncv", reads=["T0", "T1", "T1h"], is_output=True)


                    NP1 = NormPipe(1, ALL10, 32, HB_ATT)
                    down_phase(dn, 8, ALL10, 0, 32, [("A", c) for c in range(8)], hook=NP1.tile_ready)
            ckpt(3)

            def mlp(NP, plan, tiles, colbase, rows9, tgs, next_norm=None):
                NP.finish()
                NPn = None
                HA_ = HKall(tiles)
                ncols = tgs[-1][1] - tgs[0][0]
                rcnt = 0
                for gi in range(8):
                    up, dn = plan[gi]
                    for pi in range(4):
                        Wv, wk = W.get(up[pi])
                        for cl in range(2):
                            c = pi * 2 + cl
                            sA = nextset()
                            S.op("pe", fm_mm(Wv, cl * 128, sA, tgs), reads=wk + HA_, writes=BK(sA))
                            rt = (T0, T1)[rcnt % 2]
                            rk = (["T0"], ["T1", "T1h", "T1s", "T1p"])[rcnt % 2]
                            rcnt += 1
                            S.op("act", lambda e, sA=sA, rt=rt: e.activation(out=rt[:, 0:ncols], in_=flat(sA, ncols), func=AF.Relu),
                                 reads=BK(sA), writes=rk)
                            S.op("pool", lambda e, rt=rt, c=c: e.tensor_tensor(out=AT3[:, c, 0:ncols], in0=rt[:, 0:ncols], in1=rt[:, 0:ncols], op=ALU.mult),
                                 reads=rk, writes=[("A", c)])
                        W.done(up[pi])
                    if gi == 7 and next_norm is not None:
                        NPn = NormPipe(*next_norm)
                        down_phase(dn, 8, tiles, colbase, rows9, [("A", c) for c in range(8)], hook=NPn.tile_ready)
                    else:
                        down_phase(dn, 8, tiles, colbase, rows9, [("A", c) for c in range(8)])
                return NPn

            ckpt(4)
            NP2 = mlp(NP1, P_mlp0, ALL10, 0, 32, TG0, next_norm=(2, ALL10, 32, HB_ATT))
            ckpt(5)

            NP2.finish()
            HA1 = HKall(ALL10)

            S.op("dve", lambda e: e.tensor_reduce(out=SM[:, 0:1], in_=GQ[:], axis=AX.X, op=ALU.max, apply_absolute_value=True), reads=["GQ"], writes=["SM0"])
            S.op("dve", lambda e: e.tensor_reduce(out=SM[:, 1:2], in_=GK[:], axis=AX.X, op=ALU.max, apply_absolute_value=True), reads=["GK"], writes=["SM1"])
            S.op("dve", lambda e: e.tensor_reduce(out=SM[:, 2:3], in_=SK[:], axis=AX.X, op=ALU.max), reads=["SK"], writes=["SM2"])
            S.op("dve", lambda e: e.tensor_tensor(out=SM[:, 3:4], in0=SM[:, 0:1], in1=SM[:, 1:2], op=ALU.mult), reads=["SM0", "SM1"], writes=["SM3"])
            S.op("dve", lambda e: e.scalar_tensor_tensor(out=SM[:, 4:5], in0=SM[:, 3:4], scalar=8.0, in1=SM[:, 2:3], op0=ALU.mult, op1=ALU.max),
                 reads=["SM3", "SM2"], writes=["SM4"])
            S.op("dve", lambda e: e.tensor_scalar(out=SM[:, 5:6], in0=SM[:, 4:5], scalar1=-1.0, scalar2=None, op0=ALU.mult), reads=["SM4"], writes=["NEGM"])
            NEGM = SM[:, 5:6]
            S.op("act", lambda e: e.activation(out=SE0[:], in_=SK[:], func=AF.Exp, bias=NEGM, scale=1.0), reads=["SK", "NEGM"], writes=["SE0"])
            SE0v = SE0[:].rearrange("p (a b) -> p a b", b=2)
            S.op("dve", lambda e: e.tensor_copy(out=SE[0:64, :], in_=SE0v[0:64, :, 0]), reads=["SE0"], writes=["SEa"])
            S.op("dve", lambda e: e.tensor_copy(out=SE[64:128, :], in_=SE0v[64:128, :, 1]), reads=["SE0"], writes=["SEb"])

            ckpt(5.1)
            S.dma("pool", lambda e: e.dma_start(out=CKB, in_=ck.rearrange("b k n -> k b n")), "ckb", writes=CKBK)
            S.dma("pool", lambda e: e.dma_start(out=Vc, in_=cv.rearrange("b k n -> k b n")), "vc", writes=["VC"])
            S.dma("sp", lambda e: e.dma_start(out=ks_o[:, 0:120, :], in_=ck[:, 8:128, :]), "ksw", is_output=True)
            S.dma("sp", lambda e: e.dma_start(out=vs_o[:, 0:120, :], in_=cv[:, 8:128, :]), "vsw", is_output=True)
            ckpt(5.2)
            kd4 = KDUP.rearrange("p (k d n) -> p k d n", d=2, n=64)
            trc = [0]

            def ktrans(src_rows, rows, dstcols):
                bank = 4 + (trc[0] % 2)
                trc[0] += 1

                def tr(e):
                    for kv in range(4):
                        ins = e.transpose(out=psb[:, bank, kv * 128:kv * 128 + rows], in_=KDUP[0:rows, kv * 128:(kv + 1) * 128],
                                          identity=ident[0:rows, 0:rows])
                    return ins
                S.op("pe", tr, reads=["KDUP", "ident"], writes=[("ps", bank)])
                S.op("act", lambda e: e.activation(
                    out=KT2[:, :, dstcols:dstcols + rows],
                    in_=psb[:, bank, 0:512].rearrange("p (k n) -> p k n", n=128)[:, :, 0:rows], func=AF.Copy),
                    reads=[("ps", bank)] + ALLT, writes=[("KT2", dstcols)])

            for b in range(4):
                S.op("dve", lambda e, b=b: e.tensor_copy(
                    out=kd4, in_=CKB[:, b, :].rearrange("p (k n) -> p k n", n=64).unsqueeze(2).broadcast_to([128, 4, 2, 64])),
                    reads=CKBK, writes=["KDUP"])
                ktrans(None, 128, 1184 + b * 128)

            ckpt(5.3)
            def build_tables(G, gkey):
                S.op("dve", lambda e: e.tensor_tensor(out=COS[:], in0=COS[:], in1=G[:].unsqueeze(1).broadcast_to([128, 10, 64]), op=ALU.mult),
                     reads=["COS", gkey], writes=["COS"])
                S.op("dve", lambda e: e.tensor_tensor(out=SIN[:, :, 0:32], in0=SIN[:, :, 0:32],
                                                      in1=G[:, 32:64].unsqueeze(1).broadcast_to([128, 10, 32]), op=ALU.mult),
                     reads=["SIN", gkey], writes=["SIN"])
                S.op("dve", lambda e: e.tensor_tensor(out=SIN[:, :, 32:64], in0=SIN[:, :, 32:64],
                                                      in1=G[:, 0:32].unsqueeze(1).broadcast_to([128, 10, 32]), op=ALU.mult),
                     reads=["SIN", gkey], writes=["SIN"])

            def qk_chain(bank, rows, t, cb, out_ap, out_keys):
                xs = XS[cb][0:rows, :]
                xs3 = xs.rearrange("p (h d) -> p h d", d=64)
                a3 = AB[cb][0:rows, :].rearrange("p (h d) -> p h d", d=64)
                b3 = BB[cb][0:rows, :].rearrange("p (h d) -> p h d", d=64)
                s4 = S4T[0:rows, cb * 4:cb * 4 + 4]
                xk, ak, bk, sk_ = "XS%d" % cb, "AB%d" % cb, "BB%d" % cb, "S4%d" % cb
                S.op("act", lambda e: e.activation(out=xs, in_=ps[0:rows, bank, 0:256], func=AF.Copy), reads=[("ps", bank)], writes=[xk])
                for h in range(4):
                    S.op("act", lambda e, h=h: e.activation(out=JUNK[0:rows, h * 64:(h + 1) * 64], in_=xs[:, h * 64:(h + 1) * 64],
                                                            func=AF.Square, accum_out=s4[:, h:h + 1]),
                         reads=[xk], writes=[("J", h), (sk_, h)])
                S.op("act", lambda e: e.activation(out=s4, in_=s4, func=AF.Sqrt, bias=EPSB[0:rows, :], scale=1.0 / 64),
                     reads=[(sk_, h) for h in range(4)] + ["EPSB"], writes=[sk_])
                S.op("dve", lambda e: e.tensor_tensor(out=a3, in0=xs3, in1=COS[0:rows, t, :].unsqueeze(1).broadcast_to([rows, 4, 64]), op=ALU.mult),
                     reads=[xk, "COS"], writes=[ak])
                S.op("dve", lambda e: e.tensor_tensor(out=b3[:, :, 0:32], in0=xs3[:, :, 32:64],
                                                      in1=SIN[0:rows, t, 0:32].unsqueeze(1).broadcast_to([rows, 4, 32]), op=ALU.mult),
                     reads=[xk, "SIN"], writes=[bk + "a"])
                S.op("dve", lambda e: e.tensor_tensor(out=b3[:, :, 32:64], in0=xs3[:, :, 0:32],
                                                      in1=SIN[0:rows, t, 32:64].unsqueeze(1).broadcast_to([rows, 4, 32]), op=ALU.mult),
                     reads=[xk, "SIN"], writes=[bk + "b"])
                S.op("dve", lambda e: e.reciprocal(out=s4, in_=s4), reads=[sk_], writes=[sk_])
                S.op("pool", lambda e: e.tensor_tensor(out=AB[cb][0:rows, :], in0=AB[cb][0:rows, :], in1=BB[cb][0:rows, :], op=ALU.add),
                     reads=[ak, bk + "a", bk + "b"], writes=[ak])
                S.op("pool", lambda e: e.tensor_tensor(out=out_ap, in0=a3, in1=s4.unsqueeze(2).broadcast_to([rows, 4, 64]), op=ALU.mult),
                     reads=[ak, sk_], writes=out_keys)

            pj = [0]

            def tm_proj(Wv, wk, t, rows):
                bank = 6 + (pj[0] % 2)
                pj[0] += 1

                def mm(e):
                    for k in range(16):
                        ins = e.matmul(ps[0:rows, bank, 0:256], lhsT=HT3[:, k, t * 128:t * 128 + rows], rhs=Wv[:, k, :],
                                       start=(k == 0), stop=(k == 15))
                    return ins
                S.op("pe", mm, reads=wk + [("H", t, 0), ("H", t, 1)], writes=[("ps", bank)])
                return bank

            def proj_loop(Wv, wk, tiles, post_a, post_b=None):
                n = len(tiles)
                rws = [128 if t < 9 else 32 for t in tiles]
                banks = [None] * n
                banks[0] = tm_proj(Wv, wk, tiles[0], rws[0])
                if n > 1:
                    banks[1] = tm_proj(Wv, wk, tiles[1], rws[1])
                post_a(0, tiles[0], rws[0], banks[0])
                for i in range(n):
                    if i + 2 < n:
                        banks[i + 2] = tm_proj(Wv, wk, tiles[i + 2], rws[i + 2])
                    if i + 1 < n:
                        post_a(i + 1, tiles[i + 1], rws[i + 1], banks[i + 1])
                    if post_b is not None:
                        post_b(i, tiles[i], rws[i])

            build_tables(GK, "GK")
            Wk_, wkk = W.get(P_k)

            def k_post_a(i, t, rows, bank):
                KRB = KRBS[i % 2]
                qk_chain(bank, rows, t, i % 2, KRB[0:rows, :].rearrange("p (h d) -> p h d", d=64), ["KRB%d" % (i % 2)])

            def k_post_b(i, t, rows):
                KRB = KRBS[i % 2]
                kk = "KRB%d" % (i % 2)
                S.op("act", lambda e: e.activation(
                    out=kd4[0:rows], in_=KRB[0:rows, :].rearrange("p (k n) -> p k n", n=64).unsqueeze(2).broadcast_to([rows, 4, 2, 64]),
                    func=AF.Copy), reads=[kk], writes=["KDUP"])
                if t == 8:
                    S.dma("sp", lambda e: e.dma_start(out=kp_o, in_=KRB[:, :]), "kp", reads=[kk], is_output=True)
                if t == 9:
                    for b in range(4):
                        S.dma("sp", lambda e, b=b: e.dma_start(out=ks_o[b, 120:128, :], in_=KRB[b * 8:(b + 1) * 8, :]),
                              ("ksn", b), reads=[kk], is_output=True)
                ktrans(None, rows, t * 128)
            proj_loop(Wk_, wkk, ALL10, k_post_a, k_post_b)
            W.done(P_k)
            ckpt(5.4)
            S.dma("sp", lambda e: e.dma_start(out=COS[:], in_=cos_t), "cos", writes=["COS"])
            S.dma("sp", lambda e: e.dma_start(out=SIN[:], in_=sin_t), "sin", writes=["SIN"])
            build_tables(GQ, "GQ")
            Wv_, wkv_ = W.get(P_v)

            def v_post(i, t, rows, bank):
                S.op("act", lambda e: e.activation(out=V[0:rows, t, :], in_=ps[0:rows, bank, 0:256], func=AF.Copy),
                     reads=[("ps", bank)], writes=[("V", t)])
                if t >= 8:
                    S.op("dve", lambda e: e.tensor_copy(out=KRBS[0][0:rows, :], in_=ps[0:rows, bank, 0:256]),
                         reads=[("ps", bank), ("V", t)], writes=["KRB0"])
                    if t == 8:
                        S.dma("sp", lambda e: e.dma_start(out=vp_o, in_=KRBS[0][:, :]), "vp", reads=["KRB0"], is_output=True)
                    else:
                        for b in range(4):
                            S.dma("sp", lambda e, b=b: e.dma_start(out=vs_o[b, 120:128, :], in_=KRBS[0][b * 8:(b + 1) * 8, :]),
                                  ("vsn", b), reads=["KRB0"], is_output=True)
            proj_loop(Wv_, wkv_, ALL10, v_post)
            W.done(P_v)

            ckpt(6)
            QT3 = AT3[:, 0:4, :]
            OT3 = AT3[:, 4:8, :]
            KT2all = [("KT2", c) for c in [t * 128 for t in range(10)] + [1184 + b * 128 for b in range(4)]] + ALLT
            ptc = [0]
            TILES1 = list(range(1, 10))
            for g in range(4):
                qp, ao = P_att[g]
                for qh in range(2):
                    Wq, wkq = W.get(qp[qh])

                    def q_post_a(i, t, rows, bank):
                        cb = i % 2
                        qk_chain(bank, rows, t, cb, QRB[cb][0:rows, :].rearrange("p (h d) -> p h d", d=64), ["QRB%d" % cb])

                    def q_post_b(i, t, rows, qh=qh):
                        cb = i % 2
                        qb = QRB[cb]
                        qbk = "QRB%d" % cb
                        tb = 4 + (trc[0] % 2)
                        trc[0] += 1

                        def tr(e):
                            for pr in range(2):
                                ins = e.transpose(out=psb[:, tb, pr * 128:pr * 128 + rows], in_=qb[0:rows, pr * 128:(pr + 1) * 128],
                                                  identity=ident[0:rows, 0:rows])
                            return ins
                        S.op("pe", tr, reads=[qbk, "ident"], writes=[("ps", tb)])
                        lc = t * 128 - 128
                        S.op("act", lambda e: e.activation(
                            out=QT3[:, qh * 2:qh * 2 + 2, lc:lc + rows],
                            in_=psb[:, tb, 0:256].rearrange("p (k n) -> p k n", n=128)[:, :, 0:rows], func=AF.Copy),
                            reads=[("ps", tb)], writes=[("A", qh * 2), ("A", qh * 2 + 1)])
                    proj_loop(Wq, wkq, TILES1, q_post_a, q_post_b)
                    W.done(qp[qh])
                QK_ = [("A", c) for c in range(4)]
                OK_ = [("A", c) for c in range(4, 8)]
                bufs = {}
                for n in range(1, 9):
                    bufs[n] = ptc[0] % 2
                    ptc[0] += 1

                def score_exp(n, kbi, g=g):
                    buf = bufs[n]
                    P4 = PT[buf].rearrange("p (k r n) -> p k r n", k=2, r=2)
                    lc = (n - 1) * 128
                    kt = (n - 1, n)[kbi]
                    mi = kbi if (kbi == 1 or n > 1) else 2

                    def st(e):
                        for par in range(2):
                            bank = kbi * 2 + par
                            ph = slice(par * 64, par * 64 + 64)
                            ins = e.matmul(ps[:, bank, :], lhsT=KT2[ph, g, kt * 128:(kt + 1) * 128], rhs=QT3[ph, :, lc:lc + 128],
                                           start=True, stop=True)
                        return ins
                    S.op("pe", st, reads=KT2all + QK_, writes=[("ps", kbi * 2), ("ps", kbi * 2 + 1)])
                    S.op("act", lambda e: e.activation(
                        out=P4[:, kbi], in_=ps[:, kbi * 2:kbi * 2 + 2, :], func=AF.Exp, bias=NEGM, scale=0.125),
                        reads=[("ps", kbi * 2), ("ps", kbi * 2 + 1), "NEGM"], writes=[PTK[buf][kbi]])
                    pm = P4[:, kbi].rearrange("p r (a q) -> p (r a) q", q=128)
                    S.op("pool" if kbi == 0 else "dve", lambda e: e.tensor_tensor(
                        out=pm, in0=pm, in1=mask01[:, mi, :].unsqueeze(1).broadcast_to([128, 8, 128]), op=ALU.mult),
                        reads=[PTK[buf][kbi], "mask01"], writes=[PTK[buf][kbi]])

                def pv_norm(n, g=g):
                    buf = bufs[n]
                    P4 = PT[buf].rearrange("p (k r n) -> p k r n", k=2, r=2)
                    lc = (n - 1) * 128
                    bo = 4 + 2 * (n % 2)
                    bd = bo + 1

                    def pv(e):
                        for par in range(2):
                            ph = slice(par * 64, par * 64 + 64)
                            for kbi, kt in enumerate((n - 1, n)):
                                e.matmul(ps[ph, bo, :], lhsT=V[:, kt, g * 64:(g + 1) * 64], rhs=P4[:, kbi, par, :],
                                         start=(kbi == 0), stop=(kbi == 1), tile_position=(0, par * 64))
                        for par in range(2):
                            ph = slice(par * 64, par * 64 + 64)
                            for kbi in range(2):
                                ins = e.matmul(ps[ph, bd, :], lhsT=ones64[:, :], rhs=P4[:, kbi, par, :],
                                               start=(kbi == 0), stop=(kbi == 1), tile_position=(0, par * 64))
                        return ins
                    S.op("pe", pv, reads=PTK[buf] + [("V", n - 1), ("V", n), "ones64"], writes=[("ps", bo), ("ps", bd)])
                    rd3 = RD.rearrange("p (a q) -> p a q", q=128)
                    S.op("dve", lambda e: e.tensor_tensor(
                        out=rd3, in0=ps[:, bd, :].rearrange("p (a q) -> p a q", q=128),
                        in1=SE[:, 4 * g:4 * g + 4].unsqueeze(2).broadcast_to([128, 4, 128]), op=ALU.add),
                        reads=[("ps", bd), "SEa", "SEb"], writes=RDK)
                    S.op("act", lambda e: e.activation(out=RD, in_=RD, func=AF.Ln), reads=RDK, writes=RDK)
                    S.op("act", lambda e: e.activation(out=RD, in_=RD, func=AF.Exp, scale=-1.0), reads=RDK, writes=RDK)
                    S.op("dve", lambda e: e.tensor_tensor(
                        out=OT3[:, :, lc:lc + 128], in0=ps[:, bo, :].rearrange("p (a q) -> p a q", q=128), in1=rd3, op=ALU.mult),
                        reads=[("ps", bo)] + RDK, writes=OK_)

                score_exp(1, 0)
                score_exp(1, 1)
                for n in range(1, 9):
                    if n + 1 <= 8:
                        score_exp(n + 1, 0)
                    pv_norm(n)
                    if n + 1 <= 8:
                        score_exp(n + 1, 1)
                buf = ptc[0] % 2
                ptc[0] += 1
                PTc = PT[buf][:, 0:256].rearrange("p (r n) -> p r n", r=2)
                PTn = PT[buf][0:32, 256:512].rearrange("p (r n) -> p r n", r=2)

                def s_mm(e, g=g):
                    for par in range(2):
                        ph = slice(par * 64, par * 64 + 64)
                        e.matmul(ps[:, par, 0:128], lhsT=ident[:, :],
                                 rhs=maskb[:, 0, 0:8].unsqueeze(1).broadcast_to([128, 16, 8]), start=True, stop=False)
                        for b in range(4):
                            e.matmul(ps[:, par, b * 32:(b + 1) * 32], lhsT=KT2[ph, g, 1184 + b * 128:1184 + (b + 1) * 128],
                                     rhs=QT3[ph, :, 1024 + b * 8:1024 + (b + 1) * 8], start=False, stop=(b == 3))
                    for par in range(2):
                        ph = slice(par * 64, par * 64 + 64)
                        e.matmul(ps[0:32, 2 + par, 0:128], lhsT=KT2[ph, g, 1152:1184],
                                 rhs=QT3[ph, :, 1024:1056].rearrange("p a (b t) -> p b a t", t=8), start=True, stop=False)
                        ins = e.matmul(ps[0:32, 2 + par, 0:128], lhsT=ident[0:32, 0:32],
                                       rhs=msb[:, :].rearrange("p (b t) -> p b t", t=8).unsqueeze(2).broadcast_to([32, 4, 4, 8]),
                                       start=False, stop=True)
                    return ins
                S.op("pe", s_mm, reads=KT2all + QK_ + ["ident", "maskb", "msb"], writes=[("ps", b) for b in range(4)])
                S.op("act", lambda e, PTc=PTc: e.activation(out=PTc, in_=ps[:, 0:2, 0:128], func=AF.Exp, bias=NEGM, scale=0.125),
                     reads=[("ps", 0), ("ps", 1), "NEGM"], writes=PTK[buf])
                S.op("act", lambda e, PTn=PTn: e.activation(out=PTn, in_=ps[0:32, 2:4, 0:128], func=AF.Exp, bias=NEGM[0:32, :], scale=0.125),
                     reads=[("ps", 2), ("ps", 3), "NEGM"], writes=PTK[buf])

                def s_pv(e, PTc=PTc, PTn=PTn, g=g):
                    for bank, use_v in ((4, True), (5, False)):
                        for par in range(2):
                            ph = slice(par * 64, par * 64 + 64)
                            lhs_n = V[0:32, 9, g * 64:(g + 1) * 64] if use_v else ones64[0:32, :]
                            e.matmul(ps[ph, bank, 0:128], lhsT=lhs_n, rhs=PTn[:, par, :], start=True, stop=False,
                                     tile_position=(0, par * 64))
                            for b in range(4):
                                lhs_c = Vc[:, b, g * 64:(g + 1) * 64] if use_v else ones64[:, :]
                                ins = e.matmul(ps[ph, bank, b * 32:(b + 1) * 32],
                                               lhsT=lhs_c, rhs=PTc[:, par, b * 32:(b + 1) * 32],
                                               start=False, stop=(b == 3), tile_position=(0, par * 64))
                    return ins
                S.op("pe", s_pv, reads=PTK[buf] + [("V", 9), "VC", "ones64"], writes=[("ps", 4), ("ps", 5)])
                rds = RD[:, 0:128].rearrange("p (b a t) -> p b a t", b=4, t=8)
                S.op("dve", lambda e, rds=rds, g=g: e.tensor_tensor(
                    out=rds, in0=ps[:, 5, 0:128].rearrange("p (b a t) -> p b a t", b=4, t=8),
                    in1=SE[:, 4 * g:4 * g + 4].unsqueeze(1).unsqueeze(3).broadcast_to([128, 4, 4, 8]), op=ALU.add),
                    reads=[("ps", 5), "SEa", "SEb"], writes=RDK)
                S.op("dve", lambda e: e.reciprocal(out=RD[:, 0:128], in_=RD[:, 0:128]), reads=RDK, writes=RDK)
                S.op("dve", lambda e, rds=rds: e.tensor_tensor(
                    out=OT3[:, :, 1024:1056].rearrange("p a (b t) -> p b a t", t=8),
                    in0=ps[:, 4, 0:128].rearrange("p (b a t) -> p b a t", b=4, t=8), in1=rds, op=ALU.mult),
                    reads=[("ps", 4)] + RDK, writes=OK_)
                if g == 3:
                    NP3 = NormPipe(3, TILES1, 32, HB_X0)
                for hf in range(2):
                    Wa, wka = W.get(ao[hf])
                    for t in TILES1:
                        rows = 128 if t < 9 else 32
                        lc = t * 128 - 128
                        for sub in range(2):
                            bank = 6 + (dn_cnt[0] % 2)
                            dn_cnt[0] += 1
                            xg = hf * 2 + sub

                            def mm(e, Wa=Wa, rows=rows, lc=lc, bank=bank, sub=sub):
                                for c in range(4):
                                    ins = e.matmul(ps[0:rows, bank, :], lhsT=OT3[:, c, lc:lc + rows],
                                                   rhs=Wa[:, c, sub * 512:(sub + 1) * 512], start=(c == 0), stop=(c == 3))
                                return ins
                            S.op("pe", mm, reads=wka + OK_, writes=[("ps", bank)])
                            S.op("dve", lambda e, t=t, rows=rows, bank=bank, xg=xg: e.tensor_tensor(
                                out=X[0:rows, t, xg * 512:(xg + 1) * 512], in0=X[0:rows, t, xg * 512:(xg + 1) * 512],
                                in1=ps[0:rows, bank, :], op=ALU.add), reads=[("ps", bank), ("X", t, xg)], writes=[("X", t, xg)])
                        if g == 3 and hf == 1:
                            NP3.tile_ready(t)
                    W.done(ao[hf])

            ckpt(7)
            TG1 = [(128, 640), (640, 1152), (1152, 1184)]
            if not skip_mlp1:
                mlp(NP3, P_mlp1, TILES1, 128, 32, TG1)

        try:
            record()
        except _Stop:
            pass
        TILES1 = list(range(1, 10))
        for g4 in range(4):
            for t in TILES1:
                rows = 128 if t < 9 else 32
                S.dma("sp", lambda e, t=t, rows=rows, g4=g4: e.dma_start(
                    out=y_o[(t - 1) * 128:(t - 1) * 128 + rows, g4 * 512:(g4 + 1) * 512], in_=X[0:rows, t, g4 * 512:(g4 + 1) * 512]),
                    ("x", t), reads=[("X", t, g4)], is_output=True)

        with nc.allow_low_precision("bf16 matmul operands with fp32 PSUM accumulation"):
            S.emit(st)
    return nc


_PROGRAM = None


def _rope_cos_sin(pos):
    half = 32
    try:
        import jax
        import jax.numpy as jnp
        cpu = jax.devices("cpu")[0]
        with jax.default_device(cpu):
            inv = 10000.0 ** (-jnp.arange(half, dtype=jnp.float32) / half)
            ang = jnp.asarray(pos, dtype=jnp.float32)[..., None] * inv
            return np.asarray(jnp.cos(ang), dtype=np.float32), np.asarray(jnp.sin(ang), dtype=np.float32)
    except Exception:
        inv = (np.float32(10000.0) ** (-(np.arange(half, dtype=np.float32)) / np.float32(half))).astype(np.float32)
        ang = (np.asarray(pos, np.float32)[..., None] * inv).astype(np.float32)
        return np.cos(ang).astype(np.float32), np.sin(ang).astype(np.float32)


def _tables(s, first):
    pos = np.zeros((128, 10), np.float32)
    r = np.arange(128)
    for t in range(9):
        pos[:, t] = s - 128 + 128 * t + r
    pos[:32, 9] = PAST + (r[:32] % 8)
    c, sn = _rope_cos_sin(pos)
    cos2 = np.concatenate([c, c], axis=-1)
    sinm = np.concatenate([-sn, sn], axis=-1)
    j = np.arange(128)[:, None]
    i = np.arange(128)[None, :]
    masks = np.zeros((128, 3, 128), np.float32)
    NEG = -30000.0
    masks[:, 0, :] = np.where(j > i, 0.0, NEG)
    masks[:, 1, :] = np.where(j <= i, 0.0, NEG)
    masks[:, 2, :] = NEG if first else np.where(j > i, 0.0, NEG)
    jj = np.arange(32)[:, None]
    qq = np.arange(32)[None, :]
    mask_s = np.where((jj // 8 == qq // 8) & ((jj % 8) <= (qq % 8)), 0.0, NEG).astype(np.float32)
    return np.ascontiguousarray(cos2), np.ascontiguousarray(sinm), masks, mask_s


def make_in_maps(x_prompt, x_sample, state_conv, cache_k_win, cache_v_win, ln_mix, ln_mlp,
                 w_conv_in, w_conv, w_conv_out, w_qkv, w_attn_out, q_norm, k_norm, sinks, w_up, w_down):
    f = lambda a: np.ascontiguousarray(np.asarray(a, dtype=np.float32))
    x_prompt, x_sample, state_conv = f(x_prompt), f(x_sample), f(state_conv)
    cache_k_win, cache_v_win = f(cache_k_win), f(cache_v_win)
    shared = {
        "ln": f(np.stack([np.asarray(ln_mix)[0], np.asarray(ln_mlp)[0], np.asarray(ln_mix)[1], np.asarray(ln_mlp)[1]])),
        "w_in": f(np.asarray(w_conv_in)[0]),
        "w_out": f(np.asarray(w_conv_out)[0]),
        "w_qkv": f(np.asarray(w_qkv)[0]),
        "w_ao": f(np.asarray(w_attn_out)[0]),
        "qn": f(np.asarray(q_norm)[0:1]),
        "kn": f(np.asarray(k_norm)[0:1]),
        "sinks": f(np.asarray(sinks)[0:1]),
        "w_up": f(w_up),
        "w_dn": f(w_down),
    }
    wc = f(np.asarray(w_conv)[0])
    in_maps = []
    for c in range(NCORES):
        bi, qi = c // 4, c % 4
        s = qi * 1024
        xt = np.zeros((NT, D), np.float32)
        if qi > 0:
            xt[0:128] = x_prompt[bi, s - 128:s]
            xt[1184:1186] = x_prompt[bi, s - 130:s - 128]
        xt[128:1152] = x_prompt[bi, s:s + 1024]
        xt[1152:1184] = x_sample[4 * c:4 * c + 4].reshape(32, D)
        swc = np.concatenate([state_conv[0, 4 * c:4 * c + 4].reshape(8, D), wc], axis=0)
        cos2, sinm, masks, mask_s = _tables(s, qi == 0)
        m = dict(shared)
        m.update({
            "xtok": xt, "swc": np.ascontiguousarray(swc),
            "ck": np.ascontiguousarray(cache_k_win[0, 4 * c:4 * c + 4].reshape(4, 128, 256)),
            "cv": np.ascontiguousarray(cache_v_win[0, 4 * c:4 * c + 4].reshape(4, 128, 256)),
            "cos_t": cos2, "sin_t": sinm, "masks": masks, "mask_s": mask_s,
        })
        in_maps.append(m)
    return in_maps


def assemble(R):
    y_prompt = np.zeros((2, 4096, D), np.float32)
    y_sample = np.zeros((32, 8, D), np.float32)
    ncp = np.zeros((1, 2, 2, D), np.float32)
    ncs = np.zeros((1, 32, 2, D), np.float32)
    kp = np.zeros((1, 2, 128, 4, 64), np.float32)
    vp = np.zeros((1, 2, 128, 4, 64), np.float32)
    ks = np.zeros((1, 32, 128, 4, 64), np.float32)
    vs = np.zeros((1, 32, 128, 4, 64), np.float32)
    for c in range(NCORES):
        if R[c] is None:
            continue
        bi, qi = c // 4, c % 4
        s = qi * 1024
        r = R[c]
        y_prompt[bi, s:s + 1024] = r["y"][0:1024]
        y_sample[4 * c:4 * c + 4] = r["y"][1024:1056].reshape(4, 8, D)
        ncs[0, 4 * c:4 * c + 4] = r["ncv"][2:10].reshape(4, 2, D)
        ks[0, 4 * c:4 * c + 4] = r["ks"].reshape(4, 128, 4, 64)
        vs[0, 4 * c:4 * c + 4] = r["vs"].reshape(4, 128, 4, 64)
        if qi == 3:
            ncp[0, bi] = r["ncv"][0:2]
            kp[0, bi] = r["kp"].reshape(128, 4, 64)
            vp[0, bi] = r["vp"].reshape(128, 4, 64)
    return (y_prompt, y_sample, ncp, ncs, kp, vp, ks, vs)


def kernel(**inputs):
    global _PROGRAM
    if _PROGRAM is None:
        _PROGRAM = build_program()
    in_maps = make_in_maps(**inputs)
    res = run_bass_kernel_spmd(_PROGRAM, in_maps, core_ids=list(range(NCORES)))
    return assemble(res.results)
```

```python
import numpy as np
from contextlib import ExitStack
import concourse.bass as bass
import concourse.mybir as mybir
from concourse.bass_utils import run_bass_kernel_spmd

F32 = mybir.dt.float32
BF16 = mybir.dt.bfloat16
AF = mybir.ActivationFunctionType
ALU = mybir.AluOpType
AX = mybir.AxisListType

D = 2048
DFF = 8192
NT = 1186
EPS = 1e-6
NCORES = 8
PAST = 16384


class _Op:
    __slots__ = ("eng", "fn", "deps", "signal", "key", "tick", "is_dma")

    def __init__(self, eng, fn, key, is_dma):
        self.eng = eng
        self.fn = fn
        self.deps = []
        self.signal = is_dma
        self.key = key
        self.tick = 0
        self.is_dma = is_dma


class Sched:
    ENGS = ("pe", "act", "dve", "pool", "sp")

    def __init__(self, nc, same_engine_sync=("act", "dve", "pool")):
        self.nc = nc
        self.streams = {e: [] for e in self.ENGS}
        self.last_writer = {}
        self.readers = {}
        self.same_sync = set(same_engine_sync)
        self.dma_counts = {}
        self.out_dmas = []

    def _add(self, op, reads, writes):
        deps = {}
        for r in reads:
            w = self.last_writer.get(r)
            if w is not None:
                deps[id(w)] = w
        for r in writes:
            w = self.last_writer.get(r)
            if w is not None:
                deps[id(w)] = w
            for rd in self.readers.get(r, ()):
                deps[id(rd)] = rd
        op.deps = list(deps.values())
        for r in reads:
            self.readers.setdefault(r, []).append(op)
        for r in writes:
            self.last_writer[r] = op
            self.readers[r] = []
        self.streams[op.eng].append(op)
        return op

    def op(self, eng, fn, reads=(), writes=()):
        return self._add(_Op(eng, fn, eng, False), reads, writes)

    def dma(self, eng, fn, slot, reads=(), writes=(), is_output=False):
        op = _Op(eng, fn, ("dma", slot), True)
        c = self.dma_counts.get(slot, 0) + 16
        self.dma_counts[slot] = c
        op.tick = c
        self._add(op, reads, writes)
        self.out_dmas.append(op)
        return op

    def finalize(self):
        fin = _Op("sp", None, "sp", False)
        fin.deps = list(self.out_dmas)
        self.streams["sp"].append(fin)
        for e in self.ENGS:
            for op in self.streams[e]:
                for d in op.deps:
                    if d.is_dma:
                        continue
                    if d.eng == op.eng and not op.is_dma and d.eng not in self.same_sync:
                        continue
                    d.signal = True
        for e in self.ENGS:
            t = 0
            for op in self.streams[e]:
                if op.is_dma:
                    continue
                if op.signal:
                    t += 1
                    op.tick = t

    def emit(self, stack):
        nc = self.nc
        self.finalize()
        sems = {}
        import os
        for i in range(int(os.environ.get("SEM_PAD", "0"))):
            stack.enter_context(nc.semaphore("pad%d" % i))
        for e in self.ENGS:
            sems[e] = stack.enter_context(nc.semaphore("s_" + e))
        for i, slot in enumerate(self.dma_counts):
            sems[("dma", slot)] = stack.enter_context(nc.semaphore("d%d" % i))
        block = stack.enter_context(nc.Block())
        sched = self

        def run(ename, eng):
            waited = {}
            for op in sched.streams[ename]:
                need = {}
                for d in op.deps:
                    if (not d.is_dma) and d.eng == ename and (not op.is_dma) and ename not in sched.same_sync:
                        continue
                    k = d.key
                    if need.get(k, 0) < d.tick:
                        need[k] = d.tick
                for k, t in need.items():
                    if waited.get(k, 0) >= t:
                        continue
                    eng.wait_ge(sems[k], t)
                    waited[k] = t
                if op.fn is None:
                    continue
                ins = op.fn(eng)
                if op.is_dma:
                    ins.then_inc(sems[op.key], 16)
                elif op.signal:
                    ins.then_inc(sems[ename], 1)

        @block.tensor
        def _(e):
            run("pe", e)

        @block.scalar
        def _(e):
            run("act", e)

        @block.vector
        def _(e):
            run("dve", e)

        @block.gpsimd
        def _(e):
            run("pool", e)

        @block.sync
        def _(e):
            run("sp", e)


UNIT = 2048
NUNITS = 6


class WStream:
    def __init__(self, S, ring):
        self.S = S
        self.ring = ring
        self.pieces = []
        self.views = {}
        self.next_issue = 0
        self.ptr = 0
        self.free = [True] * NUNITS
        self.units_of = {}
        self.hold_keys = []

    def add(self, src, nunits, a, b):
        self.pieces.append((src, nunits, a, b))
        return len(self.pieces) - 1

    def _try_issue(self):
        if self.next_issue >= len(self.pieces):
            return False
        src, nu, a, b = self.pieces[self.next_issue]
        p = self.ptr
        if p + nu > NUNITS:
            p = 0
        if not all(self.free[p:p + nu]):
            return False
        for u in range(p, p + nu):
            self.free[u] = False
        pid = self.next_issue
        self.units_of[pid] = (p, nu)
        view = self.ring[:, p * UNIT:p * UNIT + a * b].rearrange("p (a b) -> p a b", b=b)
        keys = [("ring", u) for u in range(p, p + nu)]
        self.views[pid] = (view, keys)
        extra = self.hold_keys if (2 <= pid < 6) else []
        self.S.dma("pool", lambda e, view=view, src=src: e.dma_start(out=view, in_=src),
                   ("ring", p), reads=extra, writes=keys)
        self.ptr = p + nu
        self.next_issue += 1
        return True

    def get(self, pid):
        while self._try_issue():
            pass
        assert pid in self.views, "ring deadlock: piece %d not issued" % pid
        return self.views[pid]

    def done(self, pid):
        p, nu = self.units_of[pid]
        for u in range(p, p + nu):
            self.free[u] = True
        while self._try_issue():
            pass


class _Stop(Exception):
    pass


def build_program(skip_mlp1=False, stage=99):
    nc = bass.Bass("TRN2", target_bir_lowering=False)

    def din(name, shape):
        return nc.dram_tensor(name, shape, F32, kind="ExternalInput").ap()

    def dout(name, shape):
        return nc.dram_tensor(name, shape, F32, kind="ExternalOutput").ap()

    xtok = din("xtok", [NT, D])
    swc = din("swc", [11, D])
    ck = din("ck", [4, 128, 256])
    cv = din("cv", [4, 128, 256])
    ln = din("ln", [4, D])
    w_in = din("w_in", [D, 3 * D])
    w_out = din("w_out", [D, D])
    w_qkv = din("w_qkv", [D, 2560])
    w_ao = din("w_ao", [D, D])
    qn = din("qn", [1, 64])
    kn = din("kn", [1, 64])
    sinks = din("sinks", [1, 32])
    w_up = din("w_up", [2, D, DFF])
    w_dn = din("w_dn", [2, DFF, D])
    cos_t = din("cos_t", [128, 10, 64])
    sin_t = din("sin_t", [128, 10, 64])
    masks = din("masks", [128, 3, 128])
    mask_s = din("mask_s", [32, 32])

    y_o = dout("y", [1056, D])
    ncv_o = dout("ncv", [10, D])
    kp_o = dout("kp", [128, 256])
    vp_o = dout("vp", [128, 256])
    ks_o = dout("ks", [4, 128, 256])
    vs_o = dout("vs", [4, 128, 256])

    with ExitStack() as st:
        def sb(name, shape, dt):
            return st.enter_context(nc.sbuf_tensor(name, shape, dt))

        X = sb("X", [128, 10, D], F32)
        HT = sb("HT", [128, 16 * NT], BF16)
        ACT_T = sb("ACT_T", [128, 8 * NT], BF16)
        RING = sb("RING", [128, NUNITS * UNIT], BF16)
        TMP = sb("TMP", [128, 3600], F32)
        ATT = sb("ATT", [128, 4352], F32)
        identf = sb("identf", [128, 128], F32)
        ident = sb("ident", [128, 128], BF16)
        ones64 = sb("ones64", [128, 64], BF16)
        maskb = sb("maskb", [128, 3, 128], BF16)
        mask01 = sb("mask01", [128, 3, 128], BF16)
        msb = sb("msb", [32, 32], BF16)
        COS = sb("COS", [128, 10, 64], F32)
        SIN = sb("SIN", [128, 10, 64], F32)
        SWT = sb("SWT", [128, 16, 11], F32)
        UK = sb("UK", [128, 16, 10], F32)
        SS = sb("SS", [128, 16], F32)
        GQ = sb("GQ", [128, 64], F32)
        GK = sb("GK", [128, 64], F32)
        SK = sb("SK", [128, 32], F32)
        SE0 = sb("SE0", [128, 32], F32)
        SE = sb("SE", [128, 16], F32)
        SM = sb("SM", [128, 8], F32)
        EPSB = sb("EPSB", [128, 1], F32)
        S4T = sb("S4T", [128, 8], F32)
        JUNK = sb("JUNK", [128, 256], BF16)
        ps = st.enter_context(nc.psum_tensor("ps", [128, 8, 512], F32))
        psb = ps.bitcast(BF16)

        HT3 = HT[:].rearrange("p (c n) -> p c n", n=NT)
        AT3 = ACT_T[:].rearrange("p (c n) -> p c n", n=NT)
        T0 = TMP[:, 0:1200]
        T1 = TMP[:, 1200:2400]
        T2 = TMP[:, 2400:3600]
        GB = TMP[:, 0:2048]
        KT2 = TMP[:].bitcast(BF16)[:, 0:4 * 1696].rearrange("p (k n) -> p k n", n=1696)
        ALLT = ["T0", "T1", "T1h", "T1s", "T1p", "T2", "T2s"]
        ATTb = ATT[:].bitcast(BF16)
        V = ATTb[:, 0:2560].rearrange("p (t n) -> p t n", n=256)
        Vc = ATTb[:, 2560:3584].rearrange("p (b n) -> p b n", n=256)
        XS = [ATT[:, 1792:2048], ATT[:, 2048:2304]]
        AB = [ATT[:, 2304:2560], ATT[:, 2560:2816]]
        BB = [ATT[:, 2816:3072], ATT[:, 3072:3328]]
        RD = ATT[:, 1792:2304]
        RDK = ["XS0", "XS1"]
        QRB = [ATTb[:, 6656:6912], ATTb[:, 6912:7168]]
        KDUP = ATTb[:, 7168:7680]
        KRBS = [ATT[:, 3840:4096], ATT[:, 4096:4352]]
        CKB = ATTb[:, 4608:5632].rearrange("p (b n) -> p b n", n=256)
        CKBK = ["AB0", "AB1"]
        X0b = X[:, 0, :].bitcast(BF16)
        PT = [X0b[:, 0:2048], X0b[:, 2048:4096]]
        PTK = [[("X", 0, 0), ("X", 0, 1)], [("X", 0, 2), ("X", 0, 3)]]
        HB_ATT = ([ATTb[:, 0:2048], ATTb[:, 2048:4096], ATTb[:, 4096:6144]],
                  [[("V", t) for t in range(8)], [("V", 8), ("V", 9), "VC", "XS0"], ["XS1", "AB0", "AB1", "BB0"]])
        HB_X0 = (PT, PTK)

        S = Sched(nc)
        W = WStream(S, RING[:])
        W.hold_keys = [("X", 9, 0)]

        def XK(t):
            return [("X", t, g) for g in range(4)]

        HK = [("H", t) for t in range(10)]

        def wv(src, c0, ncol):
            return src[:, c0:c0 + ncol].rearrange("(k p) n -> p k n", p=128)

        def wr(src, r0, nk, c0, ncol):
            return src[r0:r0 + nk * 128, c0:c0 + ncol].rearrange("(k p) n -> p k n", p=128)

        P_conv = []
        for gi in range(2):
            up = []
            for j in range(8):
                J = gi * 8 + j
                up.append((W.add(wv(w_in, 2048 + J * 128, 128), 1, 16, 128),
                           W.add(wv(w_in, 4096 + J * 128, 128), 1, 16, 128),
                           W.add(wv(w_in, J * 128, 128), 1, 16, 128)))
            dn = [W.add(wr(w_out, gi * 1024, 8, ng * 512, 512), 2, 8, 512) for ng in range(4)]
            P_conv.append((up, dn))

        def plan_mlp(l):
            out = []
            for gi in range(8):
                up = [W.add(wv(w_up[l], (gi * 8 + 2 * pi) * 128, 256), 2, 16, 256) for pi in range(4)]
                dn = [W.add(wr(w_dn[l], gi * 1024, 8, ng * 512, 512), 2, 8, 512) for ng in range(4)]
                out.append((up, dn))
            return out

        P_mlp0 = plan_mlp(0)
        P_k = W.add(wv(w_qkv, 2048, 256), 2, 16, 256)
        P_v = W.add(wv(w_qkv, 2304, 256), 2, 16, 256)
        P_att = []
        for g in range(4):
            qp = [W.add(wv(w_qkv, g * 512 + qh * 256, 256), 2, 16, 256) for qh in range(2)]
            ao = [W.add(wr(w_ao, g * 512, 4, hf * 1024, 1024), 2, 4, 1024) for hf in range(2)]
            P_att.append((qp, ao))
        P_mlp1 = plan_mlp(1)

        for t in range(10):
            rows = 128 if t < 9 else 34
            S.dma("sp", lambda e, t=t, rows=rows: e.dma_start(out=X[0:rows, t, :], in_=xtok[t * 128:t * 128 + rows, :]),
                  ("x", t), writes=XK(t))
        S.dma("sp", lambda e: e.dma_start(out=X[64:75, 9, :], in_=swc), "swc", writes=["X9hi"])
        S.dma("sp", lambda e: e.dma_start(out=COS[:], in_=cos_t), "cos", writes=["COS"])
        S.dma("sp", lambda e: e.dma_start(out=SIN[:], in_=sin_t), "sin", writes=["SIN"])
        S.dma("sp", lambda e: e.dma_start(out=GQ[:], in_=qn.partition_broadcast(128)), "gq", writes=["GQ"])
        S.dma("sp", lambda e: e.dma_start(out=GK[:], in_=kn.partition_broadcast(128)), "gk", writes=["GK"])
        S.dma("sp", lambda e: e.dma_start(out=SK[:], in_=sinks.partition_broadcast(128)), "sk", writes=["SK"])
        S.dma("pool", lambda e: e.dma_start(out=maskb[:], in_=masks), "maskb", writes=["maskb"])
        S.dma("pool", lambda e: e.dma_start(out=msb[:], in_=mask_s), "msb", writes=["msb"])
        S.op("pool", lambda e: e.memset(identf[:], 1.0), writes=["identf"])
        S.op("pool", lambda e: e.affine_select(out=identf[:], in_=identf[:], pattern=[[-1, 128]],
                                               compare_op=ALU.is_equal, fill=0.0, base=0, channel_multiplier=1),
             reads=["identf"], writes=["identf"])
        S.op("dve", lambda e: e.tensor_copy(out=ident[:], in_=identf[:]), reads=["identf"], writes=["ident"])
        S.op("dve", lambda e: e.tensor_scalar(out=mask01[:], in0=maskb[:], scalar1=0.0, scalar2=None, op0=ALU.is_equal),
             reads=["maskb"], writes=["mask01"])
        S.op("dve", lambda e: e.memset(ones64[:], 1.0), writes=["ones64"])
        S.op("dve", lambda e: e.memset(EPSB[:], EPS), writes=["EPSB"])

        def swt_mm(e):
            for c in range(16):
                ins = e.matmul(ps[:, 0, c * 11:(c + 1) * 11], lhsT=X[64:75, 9, c * 128:(c + 1) * 128],
                               rhs=identf[64:75, 64:75], start=True, stop=True)
            return ins
        S.op("pe", swt_mm, reads=["X9hi", "identf"], writes=[("ps", 0)])
        S.op("act", lambda e: e.activation(out=SWT[:].rearrange("p c n -> p (c n)"), in_=ps[:, 0, 0:176], func=AF.Copy),
             reads=[("ps", 0)], writes=["SWT"])

        norm_cnt = [0]

        class NormPipe:
            def __init__(self, li, tiles, rows9, hbsel):
                Hb, HbK = hbsel
                S.dma("sp", lambda e: e.dma_start(out=GB, in_=ln[li:li + 1, :].partition_broadcast(128)),
                      "gb", writes=["T0", "T1", "T1h"])
                self.info = []
                for t in tiles:
                    i = norm_cnt[0]
                    norm_cnt[0] += 1
                    self.info.append((t, 128 if t < 9 else rows9, Hb[i % len(Hb)], HbK[i % len(Hb)], i % 16, (i % 2) * 2))
                self.nbuf = len(Hb)
                self.idx = {t: k for k, t in enumerate(tiles)}
                self.na = 0
                self.nb = 0

            def _a(self, t, rows, hb, hk, col, b0):
                S.op("act", lambda e: e.activation(
                    out=hb[0:rows, :], in_=X[0:rows, t, :], func=AF.Square, accum_out=SS[0:rows, col:col + 1]),
                    reads=XK(t), writes=hk + [("SS", col)])
                S.op("act", lambda e: e.activation(
                    out=SS[0:rows, col:col + 1], in_=SS[0:rows, col:col + 1], func=AF.Sqrt,
                    bias=EPSB[0:rows, :], scale=1.0 / D), reads=[("SS", col), "EPSB"], writes=[("SS", col)])
                S.op("dve", lambda e: e.reciprocal(out=SS[0:rows, col:col + 1], in_=SS[0:rows, col:col + 1]),
                     reads=[("SS", col)], writes=[("SS", col)])
                S.op("dve", lambda e: e.scalar_tensor_tensor(
                    out=hb[0:rows, :], in0=X[0:rows, t, :], scalar=SS[0:rows, col:col + 1], in1=GB[0:rows, :],
                    op0=ALU.mult, op1=ALU.mult), reads=XK(t) + [("SS", col), "T0", "T1", "T1h"], writes=hk)

            def _b(self, t, rows, hb, hk, col, b0):
                def tr(e):
                    for c in range(16):
                        ins = e.transpose(out=psb[:, b0 + c // 8, (c % 8) * 128:(c % 8) * 128 + rows],
                                          in_=hb[0:rows, c * 128:(c + 1) * 128], identity=ident[0:rows, 0:rows])
                    return ins
                S.op("pe", tr, reads=hk + ["ident"], writes=[("ps", b0), ("ps", b0 + 1)])
                for half in range(2):
                    src = psb[:, b0 + half, :].rearrange("p (c n) -> p c n", n=128)[:, :, 0:rows]
                    dst = HT3[:, half * 8:(half + 1) * 8, t * 128:t * 128 + rows]
                    if half == 0:
                        S.op("act", lambda e, src=src, dst=dst: e.activation(out=dst, in_=src, func=AF.Copy),
                             reads=[("ps", b0 + half)], writes=[("H", t, half)])
                    else:
                        S.op("dve", lambda e, src=src, dst=dst: e.tensor_copy(out=dst, in_=src),
                             reads=[("ps", b0 + half)], writes=[("H", t, half)])

            def tile_ready(self, t):
                if t not in self.idx:
                    return
                assert self.idx[t] == self.na
                if self.nbuf >= 3:
                    self._a(*self.info[self.na])
                    self.na += 1
                    if self.na >= 3:
                        self._b(*self.info[self.nb])
                        self.nb += 1
                else:
                    if self.na >= 2:
                        self._b(*self.info[self.nb])
                        self.nb += 1
                    self._a(*self.info[self.na])
                    self.na += 1

            def finish(self):
                n = len(self.info)
                while self.nb < n:
                    if self.na < n and (self.na - self.nb) < 2:
                        self._a(*self.info[self.na])
                        self.na += 1
                        continue
                    self._b(*self.info[self.nb])
                    self.nb += 1

        def HKall(tiles):
            return [("H", t, h) for t in tiles for h in range(2)]

        def fm_mm(Wv, c_lo, bank0, tgs):
            def f(e):
                for k in range(16):
                    for gi, (c0, c1) in enumerate(tgs):
                        ins = e.matmul(ps[:, bank0 + gi, 0:c1 - c0], lhsT=Wv[:, k, c_lo:c_lo + 128],
                                       rhs=HT3[:, k, c0:c1], start=(k == 0), stop=(k == 15))
                return ins
            return f

        def flat(bank0, n):
            return ps[:, bank0:bank0 + 3, :].rearrange("p b n -> p (b n)")[:, 0:n]

        def BK(bank0):
            return [("ps", bank0), ("ps", bank0 + 1), ("ps", bank0 + 2)]

        dn_cnt = [0]

        def down_phase(pieces, nk, tiles, colbase, rows9, akeys, hook=None):
            def one(Wv, wk, ngi, t):
                rows = 128 if t < 9 else rows9
                lc = t * 128 - colbase
                bank = 6 + (dn_cnt[0] % 2)
                dn_cnt[0] += 1
                xg = ngi

                def mm(e):
                    for c in range(nk):
                        ins = e.matmul(ps[0:rows, bank, :], lhsT=AT3[:, c, lc:lc + rows],
                                       rhs=Wv[:, c, :], start=(c == 0), stop=(c == nk - 1))
                    return ins
                S.op("pe", mm, reads=wk + akeys, writes=[("ps", bank)])
                S.op("dve", lambda e: e.tensor_tensor(
                    out=X[0:rows, t, xg * 512:(xg + 1) * 512], in0=X[0:rows, t, xg * 512:(xg + 1) * 512],
                    in1=ps[0:rows, bank, :], op=ALU.add), reads=[("ps", bank), ("X", t, xg)], writes=[("X", t, xg)])

            first = pieces if hook is None else pieces[:-2]
            for ngi, pid in enumerate(first):
                Wv, wk = W.get(pid)
                for t in tiles:
                    one(Wv, wk, ngi, t)
                W.done(pid)
            if hook is not None:
                n0 = len(pieces) - 2
                Wa_, wka_ = W.get(pieces[n0])
                Wb_, wkb_ = W.get(pieces[n0 + 1])
                for t in tiles:
                    one(Wa_, wka_, n0, t)
                    one(Wb_, wkb_, n0 + 1, t)
                    hook(t)
                W.done(pieces[n0])
                W.done(pieces[n0 + 1])

        def ckpt(k):
            if k > stage:
                raise _Stop()

        def record():
            TG0 = [(0, 512), (512, 1024), (1024, NT)]
            ALL10 = list(range(10))
            NormPipe(0, ALL10, 34, HB_ATT).finish()
            ckpt(2)
            HA = HKall(ALL10)
            ubuf = T1
            ubs = T1[:, 1154:1194].rearrange("p (b n) -> p b n", n=10)
            t1 = T2
            csb = T0
            setc = [0]

            def nextset():
                s = (setc[0] % 2) * 3
                setc[0] += 1
                return s

            for gi in range(2):
                up, dn = P_conv[gi]
                for j in range(8):
                    J = gi * 8 + j
                    pc, pv, pb = up[j]
                    w0 = SWT[:, J, 8:9]
                    w1 = SWT[:, J, 9:10]
                    w2 = SWT[:, J, 10:11]
                    Wc, wkc = W.get(pc)
                    sA = nextset()
                    S.op("pe", fm_mm(Wc, 0, sA, TG0), reads=wkc + HA, writes=BK(sA))
                    W.done(pc)
                    S.op("act", lambda e, sA=sA: e.activation(out=csb[:, 0:NT], in_=flat(sA, NT), func=AF.Copy),
                         reads=BK(sA), writes=["T0"])
                    Wvv, wkv = W.get(pv)
                    sB = nextset()
                    S.op("pe", fm_mm(Wvv, 0, sB, TG0), reads=wkv + HA, writes=BK(sB))
                    W.done(pv)
                    S.op("dve", lambda e, sB=sB: e.tensor_tensor(out=ubuf[:, 2:1154], in0=csb[:, 0:1152], in1=flat(sB, NT)[:, 0:1152], op=ALU.mult),
                         reads=BK(sB) + ["T0"], writes=["T1"])
                    S.op("dve", lambda e, sB=sB: e.tensor_tensor(
                        out=ubs[:, :, 2:10], in0=csb[:, 1152:1184].rearrange("p (b n) -> p b n", n=8),
                        in1=flat(sB, NT)[:, 1152:1184].rearrange("p (b n) -> p b n", n=8), op=ALU.mult),
                        reads=BK(sB) + ["T0"], writes=["T1s"])
                    S.op("dve", lambda e, sB=sB: e.tensor_tensor(out=ubuf[:, 0:2], in0=csb[:, 1184:1186], in1=flat(sB, NT)[:, 1184:1186], op=ALU.mult),
                         reads=BK(sB) + ["T0"], writes=["T1h"])
                    S.op("pool", lambda e, J=J: e.tensor_copy(out=ubs[:, :, 0:2], in_=SWT[:, J, 0:8].rearrange("p (b n) -> p b n", n=2)),
                         reads=["SWT"], writes=["T1p"])
                    S.op("pool", lambda e, J=J: e.tensor_copy(out=UK[:, J, 0:2], in_=ubuf[:, 1152:1154]), reads=["T1"], writes=[("UK", J, 0)])
                    S.op("pool", lambda e, J=J: e.tensor_copy(out=UK[:, J, 2:10].rearrange("p (b n) -> p b n", n=2), in_=ubs[:, :, 8:10]),
                         reads=["T1s"], writes=[("UK", J, 1)])
                    UR = ["T1", "T1h"]
                    S.op("dve", lambda e, w0=w0: e.tensor_scalar(out=t1[:, 0:1152], in0=ubuf[:, 0:1152], scalar1=w0, scalar2=None, op0=ALU.mult),
                         reads=UR + ["SWT"], writes=["T2"])
                    S.op("dve", lambda e, w1=w1: e.scalar_tensor_tensor(out=t1[:, 0:1152], in0=ubuf[:, 1:1153], scalar=w1, in1=t1[:, 0:1152], op0=ALU.mult, op1=ALU.add),
                         reads=UR + ["T2"], writes=["T2"])
                    S.op("dve", lambda e, w2=w2: e.scalar_tensor_tensor(out=t1[:, 0:1152], in0=ubuf[:, 2:1154], scalar=w2, in1=t1[:, 0:1152], op0=ALU.mult, op1=ALU.add),
                         reads=UR + ["T2"], writes=["T2"])
                    t1s = t1[:, 1152:1184].rearrange("p (b n) -> p b n", n=8)
                    USR = ["T1s", "T1p"]
                    S.op("dve", lambda e, w0=w0, t1s=t1s: e.tensor_scalar(out=t1s, in0=ubs[:, :, 0:8], scalar1=w0, scalar2=None, op0=ALU.mult),
                         reads=USR + ["SWT"], writes=["T2s"])
                    S.op("dve", lambda e, w1=w1, t1s=t1s: e.scalar_tensor_tensor(out=t1s, in0=ubs[:, :, 1:9], scalar=w1, in1=t1s, op0=ALU.mult, op1=ALU.add),
                         reads=USR + ["T2s"], writes=["T2s"])
                    S.op("dve", lambda e, w2=w2, t1s=t1s: e.scalar_tensor_tensor(out=t1s, in0=ubs[:, :, 2:10], scalar=w2, in1=t1s, op0=ALU.mult, op1=ALU.add),
                         reads=USR + ["T2s"], writes=["T2s"])
                    Wb, wkb = W.get(pb)
                    sC = nextset()
                    S.op("pe", fm_mm(Wb, 0, sC, TG0), reads=wkb + HA, writes=BK(sC))
                    W.done(pb)
                    S.op("dve", lambda e, sC=sC, j=j: e.tensor_tensor(out=AT3[:, j, 0:1184], in0=flat(sC, NT)[:, 0:1184], in1=t1[:, 0:1184], op=ALU.mult),
                         reads=BK(sC) + ["T2", "T2s"], writes=[("A", j)])
                if gi == 0:
                    down_phase(dn, 8, ALL10, 0, 32, [("A", c) for c in range(8)])
                else:
                    def uk_mm(e):
                        for c in range(16):
                            ins = e.matmul(ps[0:10, c // 4, (c % 4) * 128:(c % 4 + 1) * 128], lhsT=UK[:, c, :], rhs=identf[:, :],
                                           start=True, stop=True)
                        return ins
                    S.op("pe", uk_mm, reads=[("UK", J, h) for J in range(16) for h in range(2)] + ["identf"],
                         writes=[("ps", b) for b in range(4)])
                    S.op("act", lambda e: e.activation(out=TMP[0:10, 0:2048], in_=ps[0:10, 0:4, :].rearrange("p b n -> p (b n)"), func=AF.Copy),
                         reads=[("ps", b) for b in range(4)], writes=["T0", "T1", "T1h"])
                    S.dma("sp", lambda e: e.dma_start(out=ncv_o, in_=TMP[0:10, 0:2048]), "ncv", reads=["T0", "T1", "T1h"], is_output=True)


                    NP1 = NormPipe(1, ALL10, 32, HB_ATT)
                    down_phase(dn, 8, ALL10, 0, 32, [("A", c) for c in range(8)], hook=NP1.tile_ready)
            ckpt(3)

            def mlp(NP, plan, tiles, colbase, rows9, tgs, next_norm=None):
                NP.finish()
                NPn = None
                HA_ = HKall(tiles)
                ncols = tgs[-1][1] - tgs[0][0]
                rcnt = 0
                for gi in range(8):
                    up, dn = plan[gi]
                    for pi in range(4):
                        Wv, wk = W.get(up[pi])
                        for cl in range(2):
                            c = pi * 2 + cl
                            sA = nextset()
                            S.op("pe", fm_mm(Wv, cl * 128, sA, tgs), reads=wk + HA_, writes=BK(sA))
                            rt = (T0, T1)[rcnt % 2]
                            rk = (["T0"], ["T1", "T1h", "T1s", "T1p"])[rcnt % 2]
                            rcnt += 1
                            S.op("act", lambda e, sA=sA, rt=rt: e.activation(out=rt[:, 0:ncols], in_=flat(sA, ncols), func=AF.Relu),
                                 reads=BK(sA), writes=rk)
                            S.op("pool", lambda e, rt=rt, c=c: e.tensor_tensor(out=AT3[:, c, 0:ncols], in0=rt[:, 0:ncols], in1=rt[:, 0:ncols], op=ALU.mult),
                                 reads=rk, writes=[("A", c)])
                        W.done(up[pi])
                    if gi == 7 and next_norm is not None:
                        NPn = NormPipe(*next_norm)
                        down_phase(dn, 8, tiles, colbase, rows9, [("A", c) for c in range(8)], hook=NPn.tile_ready)
                    else:
                        down_phase(dn, 8, tiles, colbase, rows9, [("A", c) for c in range(8)])
                return NPn

            ckpt(4)
            NP2 = mlp(NP1, P_mlp0, ALL10, 0, 32, TG0, next_norm=(2, ALL10, 32, HB_ATT))
            ckpt(5)

            NP2.finish()
            HA1 = HKall(ALL10)

            S.op("dve", lambda e: e.tensor_reduce(out=SM[:, 0:1], in_=GQ[:], axis=AX.X, op=ALU.max, apply_absolute_value=True), reads=["GQ"], writes=["SM0"])
            S.op("dve", lambda e: e.tensor_reduce(out=SM[:, 1:2], in_=GK[:], axis=AX.X, op=ALU.max, apply_absolute_value=True), reads=["GK"], writes=["SM1"])
            S.op("dve", lambda e: e.tensor_reduce(out=SM[:, 2:3], in_=SK[:], axis=AX.X, op=ALU.max), reads=["SK"], writes=["SM2"])
            S.op("dve", lambda e: e.tensor_tensor(out=SM[:, 3:4], in0=SM[:, 0:1], in1=SM[:, 1:2], op=ALU.mult), reads=["SM0", "SM1"], writes=["SM3"])
            S.op("dve", lambda e: e.scalar_tensor_tensor(out=SM[:, 4:5], in0=SM[:, 3:4], scalar=8.0, in1=SM[:, 2:3], op0=ALU.mult, op1=ALU.max),
                 reads=["SM3", "SM2"], writes=["SM4"])
            S.op("dve", lambda e: e.tensor_scalar(out=SM[:, 5:6], in0=SM[:, 4:5], scalar1=-1.0, scalar2=None, op0=ALU.mult), reads=["SM4"], writes=["NEGM"])
            NEGM = SM[:, 5:6]
            S.op("act", lambda e: e.activation(out=SE0[:], in_=SK[:], func=AF.Exp, bias=NEGM, scale=1.0), reads=["SK", "NEGM"], writes=["SE0"])
            SE0v = SE0[:].rearrange("p (a b) -> p a b", b=2)
            S.op("dve", lambda e: e.tensor_copy(out=SE[0:64, :], in_=SE0v[0:64, :, 0]), reads=["SE0"], writes=["SEa"])
            S.op("dve", lambda e: e.tensor_copy(out=SE[64:128, :], in_=SE0v[64:128, :, 1]), reads=["SE0"], writes=["SEb"])

            ckpt(5.1)
            S.dma("pool", lambda e: e.dma_start(out=CKB, in_=ck.rearrange("b k n -> k b n")), "ckb", writes=CKBK)
            S.dma("pool", lambda e: e.dma_start(out=Vc, in_=cv.rearrange("b k n -> k b n")), "vc", writes=["VC"])
            S.dma("sp", lambda e: e.dma_start(out=ks_o[:, 0:120, :], in_=ck[:, 8:128, :]), "ksw", is_output=True)
            S.dma("sp", lambda e: e.dma_start(out=vs_o[:, 0:120, :], in_=cv[:, 8:128, :]), "vsw", is_output=True)
            ckpt(5.2)
            kd4 = KDUP.rearrange("p (k d n) -> p k d n", d=2, n=64)
            trc = [0]

            def ktrans(src_rows, rows, dstcols):
                bank = 4 + (trc[0] % 2)
                trc[0] += 1

                def tr(e):
                    for kv in range(4):
                        ins = e.transpose(out=psb[:, bank, kv * 128:kv * 128 + rows], in_=KDUP[0:rows, kv * 128:(kv + 1) * 128],
                                          identity=ident[0:rows, 0:rows])
                    return ins
                S.op("pe", tr, reads=["KDUP", "ident"], writes=[("ps", bank)])
                S.op("act", lambda e: e.activation(
                    out=KT2[:, :, dstcols:dstcols + rows],
                    in_=psb[:, bank, 0:512].rearrange("p (k n) -> p k n", n=128)[:, :, 0:rows], func=AF.Copy),
                    reads=[("ps", bank)] + ALLT, writes=[("KT2", dstcols)])

            for b in range(4):
                S.op("dve", lambda e, b=b: e.tensor_copy(
                    out=kd4, in_=CKB[:, b, :].rearrange("p (k n) -> p k n", n=64).unsqueeze(2).broadcast_to([128, 4, 2, 64])),
                    reads=CKBK, writes=["KDUP"])
                ktrans(None, 128, 1184 + b * 128)

            ckpt(5.3)
            def build_tables(G, gkey):
                S.op("dve", lambda e: e.tensor_tensor(out=COS[:], in0=COS[:], in1=G[:].unsqueeze(1).broadcast_to([128, 10, 64]), op=ALU.mult),
                     reads=["COS", gkey], writes=["COS"])
                S.op("dve", lambda e: e.tensor_tensor(out=SIN[:, :, 0:32], in0=SIN[:, :, 0:32],
                                                      in1=G[:, 32:64].unsqueeze(1).broadcast_to([128, 10, 32]), op=ALU.mult),
                     reads=["SIN", gkey], writes=["SIN"])
                S.op("dve", lambda e: e.tensor_tensor(out=SIN[:, :, 32:64], in0=SIN[:, :, 32:64],
                                                      in1=G[:, 0:32].unsqueeze(1).broadcast_to([128, 10, 32]), op=ALU.mult),
                     reads=["SIN", gkey], writes=["SIN"])

            def qk_chain(bank, rows, t, cb, out_ap, out_keys):
                xs = XS[cb][0:rows, :]
                xs3 = xs.rearrange("p (h d) -> p h d", d=64)
                a3 = AB[cb][0:rows, :].rearrange("p (h d) -> p h d", d=64)
                b3 = BB[cb][0:rows, :].rearrange("p (h d) -> p h d", d=64)
                s4 = S4T[0:rows, cb * 4:cb * 4 + 4]
                xk, ak, bk, sk_ = "XS%d" % cb, "AB%d" % cb, "BB%d" % cb, "S4%d" % cb
                S.op("act", lambda e: e.activation(out=xs, in_=ps[0:rows, bank, 0:256], func=AF.Copy), reads=[("ps", bank)], writes=[xk])
                for h in range(4):
                    S.op("act", lambda e, h=h: e.activation(out=JUNK[0:rows, h * 64:(h + 1) * 64], in_=xs[:, h * 64:(h + 1) * 64],
                                                            func=AF.Square, accum_out=s4[:, h:h + 1]),
                         reads=[xk], writes=[("J", h), (sk_, h)])
                S.op("act", lambda e: e.activation(out=s4, in_=s4, func=AF.Sqrt, bias=EPSB[0:rows, :], scale=1.0 / 64),
                     reads=[(sk_, h) for h in range(4)] + ["EPSB"], writes=[sk_])
                S.op("dve", lambda e: e.tensor_tensor(out=a3, in0=xs3, in1=COS[0:rows, t, :].unsqueeze(1).broadcast_to([rows, 4, 64]), op=ALU.mult),
                     reads=[xk, "COS"], writes=[ak])
                S.op("dve", lambda e: e.tensor_tensor(out=b3[:, :, 0:32], in0=xs3[:, :, 32:64],
                                                      in1=SIN[0:rows, t, 0:32].unsqueeze(1).broadcast_to([rows, 4, 32]), op=ALU.mult),
                     reads=[xk, "SIN"], writes=[bk + "a"])
                S.op("dve", lambda e: e.tensor_tensor(out=b3[:, :, 32:64], in0=xs3[:, :, 0:32],
                                                      in1=SIN[0:rows, t, 32:64].unsqueeze(1).broadcast_to([rows, 4, 32]), op=ALU.mult),
                     reads=[xk, "SIN"], writes=[bk + "b"])
                S.op("dve", lambda e: e.reciprocal(out=s4, in_=s4), reads=[sk_], writes=[sk_])
                S.op("pool", lambda e: e.tensor_tensor(out=AB[cb][0:rows, :], in0=AB[cb][0:rows, :], in1=BB[cb][0:rows, :], op=ALU.add),
                     reads=[ak, bk + "a", bk + "b"], writes=[ak])
                S.op("pool", lambda e: e.tensor_tensor(out=out_ap, in0=a3, in1=s4.unsqueeze(2).broadcast_to([rows, 4, 64]), op=ALU.mult),
                     reads=[ak, sk_], writes=out_keys)

            pj = [0]

            def tm_proj(Wv, wk, t, rows):
                bank = 6 + (pj[0] % 2)
                pj[0] += 1

                def mm(e):
                    for k in range(16):
                        ins = e.matmul(ps[0:rows, bank, 0:256], lhsT=HT3[:, k, t * 128:t * 128 + rows], rhs=Wv[:, k, :],
                                       start=(k == 0), stop=(k == 15))
                    return ins
                S.op("pe", mm, reads=wk + [("H", t, 0), ("H", t, 1)], writes=[("ps", bank)])
                return bank

            def proj_loop(Wv, wk, tiles, post_a, post_b=None):
                n = len(tiles)
                rws = [128 if t < 9 else 32 for t in tiles]
                banks = [None] * n
                banks[0] = tm_proj(Wv, wk, tiles[0], rws[0])
                if n > 1:
                    banks[1] = tm_proj(Wv, wk, tiles[1], rws[1])
                post_a(0, tiles[0], rws[0], banks[0])
                for i in range(n):
                    if i + 2 < n:
                        banks[i + 2] = tm_proj(Wv, wk, tiles[i + 2], rws[i + 2])
                    if i + 1 < n:
                        post_a(i + 1, tiles[i + 1], rws[i + 1], banks[i + 1])
                    if post_b is not None:
                        post_b(i, tiles[i], rws[i])

            build_tables(GK, "GK")
            Wk_, wkk = W.get(P_k)

            def k_post_a(i, t, rows, bank):
                KRB = KRBS[i % 2]
                qk_chain(bank, rows, t, i % 2, KRB[0:rows, :].rearrange("p (h d) -> p h d", d=64), ["KRB%d" % (i % 2)])

            def k_post_b(i, t, rows):
                KRB = KRBS[i % 2]
                kk = "KRB%d" % (i % 2)
                S.op("act", lambda e: e.activation(
                    out=kd4[0:rows], in_=KRB[0:rows, :].rearrange("p (k n) -> p k n", n=64).unsqueeze(2).broadcast_to([rows, 4, 2, 64]),
                    func=AF.Copy), reads=[kk], writes=["KDUP"])
                if t == 8:
                    S.dma("sp", lambda e: e.dma_start(out=kp_o, in_=KRB[:, :]), "kp", reads=[kk], is_output=True)
                if t == 9:
                    for b in range(4):
                        S.dma("sp", lambda e, b=b: e.dma_start(out=ks_o[b, 120:128, :], in_=KRB[b * 8:(b + 1) * 8, :]),
                              ("ksn", b), reads=[kk], is_output=True)
                ktrans(None, rows, t * 128)
            proj_loop(Wk_, wkk, ALL10, k_post_a, k_post_b)
            W.done(P_k)
            ckpt(5.4)
            S.dma("sp", lambda e: e.dma_start(out=COS[:], in_=cos_t), "cos", writes=["COS"])
            S.dma("sp", lambda e: e.dma_start(out=SIN[:], in_=sin_t), "sin", writes=["SIN"])
            build_tables(GQ, "GQ")
            Wv_, wkv_ = W.get(P_v)

            def v_post(i, t, rows, bank):
                S.op("act", lambda e: e.activation(out=V[0:rows, t, :], in_=ps[0:rows, bank, 0:256], func=AF.Copy),
                     reads=[("ps", bank)], writes=[("V", t)])
                if t >= 8:
                    S.op("dve", lambda e: e.tensor_copy(out=KRBS[0][0:rows, :], in_=ps[0:rows, bank, 0:256]),
                         reads=[("ps", bank), ("V", t)], writes=["KRB0"])
                    if t == 8:
                        S.dma("sp", lambda e: e.dma_start(out=vp_o, in_=KRBS[0][:, :]), "vp", reads=["KRB0"], is_output=True)
                    else:
                        for b in range(4):
                            S.dma("sp", lambda e, b=b: e.dma_start(out=vs_o[b, 120:128, :], in_=KRBS[0][b * 8:(b + 1) * 8, :]),
                                  ("vsn", b), reads=["KRB0"], is_output=True)
            proj_loop(Wv_, wkv_, ALL10, v_post)
            W.done(P_v)

            ckpt(6)
            QT3 = AT3[:, 0:4, :]
            OT3 = AT3[:, 4:8, :]
            KT2all = [("KT2", c) for c in [t * 128 for t in range(10)] + [1184 + b * 128 for b in range(4)]] + ALLT
            ptc = [0]
            TILES1 = list(range(1, 10))
            for g in range(4):
                qp, ao = P_att[g]
                for qh in range(2):
                    Wq, wkq = W.get(qp[qh])

                    def q_post_a(i, t, rows, bank):
                        cb = i % 2
                        qk_chain(bank, rows, t, cb, QRB[cb][0:rows, :].rearrange("p (h d) -> p h d", d=64), ["QRB%d" % cb])

                    def q_post_b(i, t, rows, qh=qh):
                        cb = i % 2
                        qb = QRB[cb]
                        qbk = "QRB%d" % cb
                        tb = 4 + (trc[0] % 2)
                        trc[0] += 1

                        def tr(e):
                            for pr in range(2):
                                ins = e.transpose(out=psb[:, tb, pr * 128:pr * 128 + rows], in_=qb[0:rows, pr * 128:(pr + 1) * 128],
                                                  identity=ident[0:rows, 0:rows])
                            return ins
                        S.op("pe", tr, reads=[qbk, "ident"], writes=[("ps", tb)])
                        lc = t * 128 - 128
                        S.op("act", lambda e: e.activation(
                            out=QT3[:, qh * 2:qh * 2 + 2, lc:lc + rows],
                            in_=psb[:, tb, 0:256].rearrange("p (k n) -> p k n", n=128)[:, :, 0:rows], func=AF.Copy),
                            reads=[("ps", tb)], writes=[("A", qh * 2), ("A", qh * 2 + 1)])
                    proj_loop(Wq, wkq, TILES1, q_post_a, q_post_b)
                    W.done(qp[qh])
                QK_ = [("A", c) for c in range(4)]
                OK_ = [("A", c) for c in range(4, 8)]
                bufs = {}
                for n in range(1, 9):
                    bufs[n] = ptc[0] % 2
                    ptc[0] += 1

                def score_exp(n, kbi, g=g):
                    buf = bufs[n]
                    P4 = PT[buf].rearrange("p (k r n) -> p k r n", k=2, r=2)
                    lc = (n - 1) * 128
                    kt = (n - 1, n)[kbi]
                    mi = kbi if (kbi == 1 or n > 1) else 2

                    def st(e):
                        for par in range(2):
                            bank = kbi * 2 + par
                            ph = slice(par * 64, par * 64 + 64)
                            ins = e.matmul(ps[:, bank, :], lhsT=KT2[ph, g, kt * 128:(kt + 1) * 128], rhs=QT3[ph, :, lc:lc + 128],
                                           start=True, stop=True)
                        return ins
                    S.op("pe", st, reads=KT2all + QK_, writes=[("ps", kbi * 2), ("ps", kbi * 2 + 1)])
                    S.op("act", lambda e: e.activation(
                        out=P4[:, kbi], in_=ps[:, kbi * 2:kbi * 2 + 2, :], func=AF.Exp, bias=NEGM, scale=0.125),
                        reads=[("ps", kbi * 2), ("ps", kbi * 2 + 1), "NEGM"], writes=[PTK[buf][kbi]])
                    pm = P4[:, kbi].rearrange("p r (a q) -> p (r a) q", q=128)
                    S.op("pool" if kbi == 0 else "dve", lambda e: e.tensor_tensor(
                        out=pm, in0=pm, in1=mask01[:, mi, :].unsqueeze(1).broadcast_to([128, 8, 128]), op=ALU.mult),
                        reads=[PTK[buf][kbi], "mask01"], writes=[PTK[buf][kbi]])

                def pv_norm(n, g=g):
                    buf = bufs[n]
                    P4 = PT[buf].rearrange("p (k r n) -> p k r n", k=2, r=2)
                    lc = (n - 1) * 128
                    bo = 4 + 2 * (n % 2)
                    bd = bo + 1

                    def pv(e):
                        for par in range(2):
                            ph = slice(par * 64, par * 64 + 64)
                            for kbi, kt in enumerate((n - 1, n)):
                                e.matmul(ps[ph, bo, :], lhsT=V[:, kt, g * 64:(g + 1) * 64], rhs=P4[:, kbi, par, :],
                                         start=(kbi == 0), stop=(kbi == 1), tile_position=(0, par * 64))
                        for par in range(2):
                            ph = slice(par * 64, par * 64 + 64)
                            for kbi in range(2):
                                ins = e.matmul(ps[ph, bd, :], lhsT=ones64[:, :], rhs=P4[:, kbi, par, :],
                                               start=(kbi == 0), stop=(kbi == 1), tile_position=(0, par * 64))
                        return ins
                    S.op("pe", pv, reads=PTK[buf] + [("V", n - 1), ("V", n), "ones64"], writes=[("ps", bo), ("ps", bd)])

                def norm_o(n, g=g):
                    lc = (n - 1) * 128
                    bo = 4 + 2 * (n % 2)
                    bd = bo + 1
                    rd3 = RD.rearrange("p (a q) -> p a q", q=128)
                    S.op("dve", lambda e: e.tensor_tensor(
                        out=rd3, in0=ps[:, bd, :].rearrange("p (a q) -> p a q", q=128),
                        in1=SE[:, 4 * g:4 * g + 4].unsqueeze(2).broadcast_to([128, 4, 128]), op=ALU.add),
                        reads=[("ps", bd), "SEa", "SEb"], writes=RDK)
                    S.op("act", lambda e: e.activation(out=RD, in_=RD, func=AF.Ln), reads=RDK, writes=RDK)
                    S.op("act", lambda e: e.activation(out=RD, in_=RD, func=AF.Exp, scale=-1.0), reads=RDK, writes=RDK)
                    S.op("dve", lambda e: e.tensor_tensor(
                        out=OT3[:, :, lc:lc + 128], in0=ps[:, bo, :].rearrange("p (a q) -> p a q", q=128), in1=rd3, op=ALU.mult),
                        reads=[("ps", bo)] + RDK, writes=OK_)

                score_exp(1, 0)
                score_exp(1, 1)
                for n in range(1, 9):
                    if n + 1 <= 8:
                        score_exp(n + 1, 0)
                    pv_norm(n)
                    if n + 1 <= 8:
                        score_exp(n + 1, 1)
                    norm_o(n)
                buf = ptc[0] % 2
                ptc[0] += 1
                PTc = PT[buf][:, 0:256].rearrange("p (r n) -> p r n", r=2)
                PTn = PT[buf][0:32, 256:512].rearrange("p (r n) -> p r n", r=2)

                def s_mm(e, g=g):
                    for par in range(2):
                        ph = slice(par * 64, par * 64 + 64)
                        e.matmul(ps[:, par, 0:128], lhsT=ident[:, :],
                                 rhs=maskb[:, 0, 0:8].unsqueeze(1).broadcast_to([128, 16, 8]), start=True, stop=False)
                        for b in range(4):
                            e.matmul(ps[:, par, b * 32:(b + 1) * 32], lhsT=KT2[ph, g, 1184 + b * 128:1184 + (b + 1) * 128],
                                     rhs=QT3[ph, :, 1024 + b * 8:1024 + (b + 1) * 8], start=False, stop=(b == 3))
                    for par in range(2):
                        ph = slice(par * 64, par * 64 + 64)
                        e.matmul(ps[0:32, 2 + par, 0:128], lhsT=KT2[ph, g, 1152:1184],
                                 rhs=QT3[ph, :, 1024:1056].rearrange("p a (b t) -> p b a t", t=8), start=True, stop=False)
                        ins = e.matmul(ps[0:32, 2 + par, 0:128], lhsT=ident[0:32, 0:32],
                                       rhs=msb[:, :].rearrange("p (b t) -> p b t", t=8).unsqueeze(2).broadcast_to([32, 4, 4, 8]),
                                       start=False, stop=True)
                    return ins
                S.op("pe", s_mm, reads=KT2all + QK_ + ["ident", "maskb", "msb"], writes=[("ps", b) for b in range(4)])
                S.op("act", lambda e, PTc=PTc: e.activation(out=PTc, in_=ps[:, 0:2, 0:128], func=AF.Exp, bias=NEGM, scale=0.125),
                     reads=[("ps", 0), ("ps", 1), "NEGM"], writes=PTK[buf])
                S.op("act", lambda e, PTn=PTn: e.activation(out=PTn, in_=ps[0:32, 2:4, 0:128], func=AF.Exp, bias=NEGM[0:32, :], scale=0.125),
                     reads=[("ps", 2), ("ps", 3), "NEGM"], writes=PTK[buf])

                def s_pv(e, PTc=PTc, PTn=PTn, g=g):
                    for bank, use_v in ((4, True), (5, False)):
                        for par in range(2):
                            ph = slice(par * 64, par * 64 + 64)
                            lhs_n = V[0:32, 9, g * 64:(g + 1) * 64] if use_v else ones64[0:32, :]
                            e.matmul(ps[ph, bank, 0:128], lhsT=lhs_n, rhs=PTn[:, par, :], start=True, stop=False,
                                     tile_position=(0, par * 64))
                            for b in range(4):
                                lhs_c = Vc[:, b, g * 64:(g + 1) * 64] if use_v else ones64[:, :]
                                ins = e.matmul(ps[ph, bank, b * 32:(b + 1) * 32],
                                               lhsT=lhs_c, rhs=PTc[:, par, b * 32:(b + 1) * 32],
                                               start=False, stop=(b == 3), tile_position=(0, par * 64))
                    return ins
                S.op("pe", s_pv, reads=PTK[buf] + [("V", 9), "VC", "ones64"], writes=[("ps", 4), ("ps", 5)])
                rds = RD[:, 0:128].rearrange("p (b a t) -> p b a t", b=4, t=8)
                S.op("dve", lambda e, rds=rds, g=g: e.tensor_tensor(
                    out=rds, in0=ps[:, 5, 0:128].rearrange("p (b a t) -> p b a t", b=4, t=8),
                    in1=SE[:, 4 * g:4 * g + 4].unsqueeze(1).unsqueeze(3).broadcast_to([128, 4, 4, 8]), op=ALU.add),
                    reads=[("ps", 5), "SEa", "SEb"], writes=RDK)
                S.op("dve", lambda e: e.reciprocal(out=RD[:, 0:128], in_=RD[:, 0:128]), reads=RDK, writes=RDK)
                S.op("dve", lambda e, rds=rds: e.tensor_tensor(
                    out=OT3[:, :, 1024:1056].rearrange("p a (b t) -> p b a t", t=8),
                    in0=ps[:, 4, 0:128].rearrange("p (b a t) -> p b a t", b=4, t=8), in1=rds, op=ALU.mult),
                    reads=[("ps", 4)] + RDK, writes=OK_)
                if g == 3:
                    NP3 = NormPipe(3, TILES1, 32, HB_X0)
                for hf in range(2):
                    Wa, wka = W.get(ao[hf])
                    for t in TILES1:
                        rows = 128 if t < 9 else 32
                        lc = t * 128 - 128
                        for sub in range(2):
                            bank = 6 + (dn_cnt[0] % 2)
                            dn_cnt[0] += 1
                            xg = hf * 2 + sub

                            def mm(e, Wa=Wa, rows=rows, lc=lc, bank=bank, sub=sub):
                                for c in range(4):
                                    ins = e.matmul(ps[0:rows, bank, :], lhsT=OT3[:, c, lc:lc + rows],
                                                   rhs=Wa[:, c, sub * 512:(sub + 1) * 512], start=(c == 0), stop=(c == 3))
                                return ins
                            S.op("pe", mm, reads=wka + OK_, writes=[("ps", bank)])
                            S.op("dve", lambda e, t=t, rows=rows, bank=bank, xg=xg: e.tensor_tensor(
                                out=X[0:rows, t, xg * 512:(xg + 1) * 512], in0=X[0:rows, t, xg * 512:(xg + 1) * 512],
                                in1=ps[0:rows, bank, :], op=ALU.add), reads=[("ps", bank), ("X", t, xg)], writes=[("X", t, xg)])
                        if g == 3 and hf == 1:
                            NP3.tile_ready(t)
                    W.done(ao[hf])

            ckpt(7)
            TG1 = [(128, 640), (640, 1152), (1152, 1184)]
            if not skip_mlp1:
                mlp(NP3, P_mlp1, TILES1, 128, 32, TG1)

        try:
            record()
        except _Stop:
            pass
        TILES1 = list(range(1, 10))
        for g4 in range(4):
            for t in TILES1:
                rows = 128 if t < 9 else 32
                S.dma("sp", lambda e, t=t, rows=rows, g4=g4: e.dma_start(
                    out=y_o[(t - 1) * 128:(t - 1) * 128 + rows, g4 * 512:(g4 + 1) * 512], in_=X[0:rows, t, g4 * 512:(g4 + 1) * 512]),
                    ("x", t), reads=[("X", t, g4)], is_output=True)

        with nc.allow_low_precision("bf16 matmul operands with fp32 PSUM accumulation"):
            S.emit(st)
    return nc


_PROGRAM = None


def _rope_cos_sin(pos):
    half = 32
    try:
        import jax
        import jax.numpy as jnp
        cpu = jax.devices("cpu")[0]
        with jax.default_device(cpu):
            inv = 10000.0 ** (-jnp.arange(half, dtype=jnp.float32) / half)
            ang = jnp.asarray(pos, dtype=jnp.float32)[..., None] * inv
            return np.asarray(jnp.cos(ang), dtype=np.float32), np.asarray(jnp.sin(ang), dtype=np.float32)
    except Exception:
        inv = (np.float32(10000.0) ** (-(np.arange(half, dtype=np.float32)) / np.float32(half))).astype(np.float32)
        ang = (np.asarray(pos, np.float32)[..., None] * inv).astype(np.float32)
        return np.cos(ang).astype(np.float32), np.sin(ang).astype(np.float32)


def _tables(s, first):
    pos = np.zeros((128, 10), np.float32)
    r = np.arange(128)
    for t in range(9):
        pos[:, t] = s - 128 + 128 * t + r
    pos[:32, 9] = PAST + (r[:32] % 8)
    c, sn = _rope_cos_sin(pos)
    cos2 = np.concatenate([c, c], axis=-1)
    sinm = np.concatenate([-sn, sn], axis=-1)
    j = np.arange(128)[:, None]
    i = np.arange(128)[None, :]
    masks = np.zeros((128, 3, 128), np.float32)
    NEG = -30000.0
    masks[:, 0, :] = np.where(j > i, 0.0, NEG)
    masks[:, 1, :] = np.where(j <= i, 0.0, NEG)
    masks[:, 2, :] = NEG if first else np.where(j > i, 0.0, NEG)
    jj = np.arange(32)[:, None]
    qq = np.arange(32)[None, :]
    mask_s = np.where((jj // 8 == qq // 8) & ((jj % 8) <= (qq % 8)), 0.0, NEG).astype(np.float32)
    return np.ascontiguousarray(cos2), np.ascontiguousarray(sinm), masks, mask_s


def make_in_maps(x_prompt, x_sample, state_conv, cache_k_win, cache_v_win, ln_mix, ln_mlp,
                 w_conv_in, w_conv, w_conv_out, w_qkv, w_attn_out, q_norm, k_norm, sinks, w_up, w_down):
    f = lambda a: np.ascontiguousarray(np.asarray(a, dtype=np.float32))
    x_prompt, x_sample, state_conv = f(x_prompt), f(x_sample), f(state_conv)
    cache_k_win, cache_v_win = f(cache_k_win), f(cache_v_win)
    shared = {
        "ln": f(np.stack([np.asarray(ln_mix)[0], np.asarray(ln_mlp)[0], np.asarray(ln_mix)[1], np.asarray(ln_mlp)[1]])),
        "w_in": f(np.asarray(w_conv_in)[0]),
        "w_out": f(np.asarray(w_conv_out)[0]),
        "w_qkv": f(np.asarray(w_qkv)[0]),
        "w_ao": f(np.asarray(w_attn_out)[0]),
        "qn": f(np.asarray(q_norm)[0:1]),
        "kn": f(np.asarray(k_norm)[0:1]),
        "sinks": f(np.asarray(sinks)[0:1]),
        "w_up": f(w_up),
        "w_dn": f(w_down),
    }
    wc = f(np.asarray(w_conv)[0])
    in_maps = []
    for c in range(NCORES):
        bi, qi = c // 4, c % 4
        s = qi * 1024
        xt = np.zeros((NT, D), np.float32)
        if qi > 0:
            xt[0:128] = x_prompt[bi, s - 128:s]
            xt[1184:1186] = x_prompt[bi, s - 130:s - 128]
        xt[128:1152] = x_prompt[bi, s:s + 1024]
        xt[1152:1184] = x_sample[4 * c:4 * c + 4].reshape(32, D)
        swc = np.concatenate([state_conv[0, 4 * c:4 * c + 4].reshape(8, D), wc], axis=0)
        cos2, sinm, masks, mask_s = _tables(s, qi == 0)
        m = dict(shared)
        m.update({
            "xtok": xt, "swc": np.ascontiguousarray(swc),
            "ck": np.ascontiguousarray(cache_k_win[0, 4 * c:4 * c + 4].reshape(4, 128, 256)),
            "cv": np.ascontiguousarray(cache_v_win[0, 4 * c:4 * c + 4].reshape(4, 128, 256)),
            "cos_t": cos2, "sin_t": sinm, "masks": masks, "mask_s": mask_s,
        })
        in_maps.append(m)
    return in_maps


def assemble(R):
    y_prompt = np.zeros((2, 4096, D), np.float32)
    y_sample = np.zeros((32, 8, D), np.float32)
    ncp = np.zeros((1, 2, 2, D), np.float32)
    ncs = np.zeros((1, 32, 2, D), np.float32)
    kp = np.zeros((1, 2, 128, 4, 64), np.float32)
    vp = np.zeros((1, 2, 128, 4, 64), np.float32)
    ks = np.zeros((1, 32, 128, 4, 64), np.float32)
    vs = np.zeros((1, 32, 128, 4, 64), np.float32)
    for c in range(NCORES):
        if R[c] is None:
            continue
        bi, qi = c // 4, c % 4
        s = qi * 1024
        r = R[c]
        y_prompt[bi, s:s + 1024] = r["y"][0:1024]
        y_sample[4 * c:4 * c + 4] = r["y"][1024:1056].reshape(4, 8, D)
        ncs[0, 4 * c:4 * c + 4] = r["ncv"][2:10].reshape(4, 2, D)
        ks[0, 4 * c:4 * c + 4] = r["ks"].reshape(4, 128, 4, 64)
        vs[0, 4 * c:4 * c + 4] = r["vs"].reshape(4, 128, 4, 64)
        if qi == 3:
            ncp[0, bi] = r["ncv"][0:2]
            kp[0, bi] = r["kp"].reshape(128, 4, 64)
            vp[0, bi] = r["vp"].reshape(128, 4, 64)
    return (y_prompt, y_sample, ncp, ncs, kp, vp, ks, vs)


def kernel(**inputs):
    global _PROGRAM
    if _PROGRAM is None:
        _PROGRAM = build_program()
    in_maps = make_in_maps(**inputs)
    res = run_bass_kernel_spmd(_PROGRAM, in_maps, core_ids=list(range(NCORES)))
    return assemble(res.results)
```

```python
import numpy as np
from contextlib import ExitStack
import concourse.bass as bass
import concourse.mybir as mybir
from concourse.bass_utils import run_bass_kernel_spmd

F32 = mybir.dt.float32
BF16 = mybir.dt.bfloat16
AF = mybir.ActivationFunctionType
ALU = mybir.AluOpType
AX = mybir.AxisListType

D = 2048
DFF = 8192
NT = 1186
EPS = 1e-6
NCORES = 8
PAST = 16384


class _Op:
    __slots__ = ("eng", "fn", "deps", "signal", "key", "tick", "is_dma")

    def __init__(self, eng, fn, key, is_dma):
        self.eng = eng
        self.fn = fn
        self.deps = []
        self.signal = is_dma
        self.key = key
        self.tick = 0
        self.is_dma = is_dma


class Sched:
    ENGS = ("pe", "act", "dve", "pool", "sp")

    def __init__(self, nc, same_engine_sync=("act", "dve", "pool")):
        self.nc = nc
        self.streams = {e: [] for e in self.ENGS}
        self.last_writer = {}
        self.readers = {}
        self.same_sync = set(same_engine_sync)
        self.dma_counts = {}
        self.out_dmas = []

    def _add(self, op, reads, writes):
        deps = {}
        for r in reads:
            w = self.last_writer.get(r)
            if w is not None:
                deps[id(w)] = w
        for r in writes:
            w = self.last_writer.get(r)
            if w is not None:
                deps[id(w)] = w
            for rd in self.readers.get(r, ()):
                deps[id(rd)] = rd
        op.deps = list(deps.values())
        for r in reads:
            self.readers.setdefault(r, []).append(op)
        for r in writes:
            self.last_writer[r] = op
            self.readers[r] = []
        self.streams[op.eng].append(op)
        return op

    def op(self, eng, fn, reads=(), writes=()):
        return self._add(_Op(eng, fn, eng, False), reads, writes)

    def dma(self, eng, fn, slot, reads=(), writes=(), is_output=False):
        op = _Op(eng, fn, ("dma", slot), True)
        c = self.dma_counts.get(slot, 0) + 16
        self.dma_counts[slot] = c
        op.tick = c
        self._add(op, reads, writes)
        self.out_dmas.append(op)
        return op

    def finalize(self):
        fin = _Op("sp", None, "sp", False)
        fin.deps = list(self.out_dmas)
        self.streams["sp"].append(fin)
        for e in self.ENGS:
            for op in self.streams[e]:
                for d in op.deps:
                    if d.is_dma:
                        continue
                    if d.eng == op.eng and not op.is_dma and d.eng not in self.same_sync:
                        continue
                    d.signal = True
        for e in self.ENGS:
            t = 0
            for op in self.streams[e]:
                if op.is_dma:
                    continue
                if op.signal:
                    t += 1
                    op.tick = t

    def emit(self, stack):
        nc = self.nc
        self.finalize()
        sems = {}
        import os
        for i in range(int(os.environ.get("SEM_PAD", "0"))):
            stack.enter_context(nc.semaphore("pad%d" % i))
        for e in self.ENGS:
            sems[e] = stack.enter_context(nc.semaphore("s_" + e))
        for i, slot in enumerate(self.dma_counts):
            sems[("dma", slot)] = stack.enter_context(nc.semaphore("d%d" % i))
        block = stack.enter_context(nc.Block())
        sched = self

        def run(ename, eng):
            waited = {}
            for op in sched.streams[ename]:
                need = {}
                for d in op.deps:
                    if (not d.is_dma) and d.eng == ename and (not op.is_dma) and ename not in sched.same_sync:
                        continue
                    k = d.key
                    if need.get(k, 0) < d.tick:
                        need[k] = d.tick
                for k, t in need.items():
                    if waited.get(k, 0) >= t:
                        continue
                    eng.wait_ge(sems[k], t)
                    waited[k] = t
                if op.fn is None:
                    continue
                ins = op.fn(eng)
                if op.is_dma:
                    ins.then_inc(sems[op.key], 16)
                elif op.signal:
                    ins.then_inc(sems[ename], 1)

        @block.tensor
        def _(e):
            run("pe", e)

        @block.scalar
        def _(e):
            run("act", e)

        @block.vector
        def _(e):
            run("dve", e)

        @block.gpsimd
        def _(e):
            run("pool", e)

        @block.sync
        def _(e):
            run("sp", e)


UNIT = 2048
NUNITS = 6


class WStream:
    def __init__(self, S, ring):
        self.S = S
        self.ring = ring
        self.pieces = []
        self.views = {}
        self.next_issue = 0
        self.ptr = 0
        self.free = [True] * NUNITS
        self.units_of = {}
        self.hold_keys = []

    def add(self, src, nunits, a, b):
        self.pieces.append((src, nunits, a, b))
        return len(self.pieces) - 1

    def _try_issue(self):
        if self.next_issue >= len(self.pieces):
            return False
        src, nu, a, b = self.pieces[self.next_issue]
        p = self.ptr
        if p + nu > NUNITS:
            p = 0
        if not all(self.free[p:p + nu]):
            return False
        for u in range(p, p + nu):
            self.free[u] = False
        pid = self.next_issue
        self.units_of[pid] = (p, nu)
        view = self.ring[:, p * UNIT:p * UNIT + a * b].rearrange("p (a b) -> p a b", b=b)
        keys = [("ring", u) for u in range(p, p + nu)]
        self.views[pid] = (view, keys)
        extra = self.hold_keys if (2 <= pid < 6) else []
        self.S.dma("pool", lambda e, view=view, src=src: e.dma_start(out=view, in_=src),
                   ("ring", p), reads=extra, writes=keys)
        self.ptr = p + nu
        self.next_issue += 1
        return True

    def get(self, pid):
        while self._try_issue():
            pass
        assert pid in self.views, "ring deadlock: piece %d not issued" % pid
        return self.views[pid]

    def done(self, pid):
        p, nu = self.units_of[pid]
        for u in range(p, p + nu):
            self.free[u] = True
        while self._try_issue():
            pass


class _Stop(Exception):
    pass


def build_program(skip_mlp1=False, stage=99):
    nc = bass.Bass("TRN2", target_bir_lowering=False)

    def din(name, shape):
        return nc.dram_tensor(name, shape, F32, kind="ExternalInput").ap()

    def dout(name, shape):
        return nc.dram_tensor(name, shape, F32, kind="ExternalOutput").ap()

    xtok = din("xtok", [NT, D])
    swc = din("swc", [11, D])
    ck = din("ck", [4, 128, 256])
    cv = din("cv", [4, 128, 256])
    ln = din("ln", [4, D])
    w_in = din("w_in", [D, 3 * D])
    w_out = din("w_out", [D, D])
    w_qkv = din("w_qkv", [D, 2560])
    w_ao = din("w_ao", [D, D])
    qn = din("qn", [1, 64])
    kn = din("kn", [1, 64])
    sinks = din("sinks", [1, 32])
    w_up = din("w_up", [2, D, DFF])
    w_dn = din("w_dn", [2, DFF, D])
    cos_t = din("cos_t", [128, 10, 64])
    sin_t = din("sin_t", [128, 10, 64])
    masks = din("masks", [128, 3, 128])
    mask_s = din("mask_s", [32, 32])

    y_o = dout("y", [1056, D])
    ncv_o = dout("ncv", [10, D])
    kp_o = dout("kp", [128, 256])
    vp_o = dout("vp", [128, 256])
    ks_o = dout("ks", [4, 128, 256])
    vs_o = dout("vs", [4, 128, 256])

    with ExitStack() as st:
        def sb(name, shape, dt):
            return st.enter_context(nc.sbuf_tensor(name, shape, dt))

        X = sb("X", [128, 10, D], F32)
        HT = sb("HT", [128, 16 * NT], BF16)
        ACT_T = sb("ACT_T", [128, 8 * NT], BF16)
        RING = sb("RING", [128, NUNITS * UNIT], BF16)
        TMP = sb("TMP", [128, 3600], F32)
        ATT = sb("ATT", [128, 4352], F32)
        identf = sb("identf", [128, 128], F32)
        ident = sb("ident", [128, 128], BF16)
        ones64 = sb("ones64", [128, 64], BF16)
        maskb = sb("maskb", [128, 3, 128], BF16)
        mask01 = sb("mask01", [128, 3, 128], BF16)
        msb = sb("msb", [32, 32], BF16)
        COS = sb("COS", [128, 10, 64], F32)
        SIN = sb("SIN", [128, 10, 64], F32)
        SWT = sb("SWT", [128, 16, 11], F32)
        UK = sb("UK", [128, 16, 10], F32)
        SS = sb("SS", [128, 16], F32)
        GQ = sb("GQ", [128, 64], F32)
        GK = sb("GK", [128, 64], F32)
        SK = sb("SK", [128, 32], F32)
        SE0 = sb("SE0", [128, 32], F32)
        SE = sb("SE", [128, 16], F32)
        SM = sb("SM", [128, 8], F32)
        EPSB = sb("EPSB", [128, 1], F32)
        S4T = sb("S4T", [128, 8], F32)
        JUNK = sb("JUNK", [128, 256], BF16)
        ps = st.enter_context(nc.psum_tensor("ps", [128, 8, 512], F32))
        psb = ps.bitcast(BF16)

        HT3 = HT[:].rearrange("p (c n) -> p c n", n=NT)
        AT3 = ACT_T[:].rearrange("p (c n) -> p c n", n=NT)
        T0 = TMP[:, 0:1200]
        T1 = TMP[:, 1200:2400]
        T2 = TMP[:, 2400:3600]
        GB = TMP[:, 0:2048]
        KT2 = TMP[:].bitcast(BF16)[:, 0:4 * 1696].rearrange("p (k n) -> p k n", n=1696)
        ALLT = ["T0", "T1", "T1h", "T1s", "T1p", "T2", "T2s"]
        ATTb = ATT[:].bitcast(BF16)
        V = ATTb[:, 0:2560].rearrange("p (t n) -> p t n", n=256)
        Vc = ATTb[:, 2560:3584].rearrange("p (b n) -> p b n", n=256)
        XS = [ATT[:, 1792:2048], ATT[:, 2048:2304]]
        AB = [ATT[:, 2304:2560], ATT[:, 2560:2816]]
        BB = [ATT[:, 2816:3072], ATT[:, 3072:3328]]
        RD = ATT[:, 1792:2304]
        RDK = ["XS0", "XS1"]
        QRB = [ATTb[:, 6656:6912], ATTb[:, 6912:7168]]
        KDUP = ATTb[:, 7168:7680]
        KRBS = [ATT[:, 3840:4096], ATT[:, 4096:4352]]
        CKB = ATTb[:, 4608:5632].rearrange("p (b n) -> p b n", n=256)
        CKBK = ["AB0", "AB1"]
        X0b = X[:, 0, :].bitcast(BF16)
        PT = [X0b[:, 0:2048], X0b[:, 2048:4096]]
        PTK = [[("X", 0, 0), ("X", 0, 1)], [("X", 0, 2), ("X", 0, 3)]]
        HB_ATT = ([ATTb[:, 0:2048], ATTb[:, 2048:4096], ATTb[:, 4096:6144]],
                  [[("V", t) for t in range(8)], [("V", 8), ("V", 9), "VC", "XS0"], ["XS1", "AB0", "AB1", "BB0"]])
        HB_X0 = (PT, PTK)

        S = Sched(nc)
        W = WStream(S, RING[:])
        W.hold_keys = [("X", 9, 0)]

        def XK(t):
            return [("X", t, g) for g in range(4)]

        HK = [("H", t) for t in range(10)]

        def wv(src, c0, ncol):
            return src[:, c0:c0 + ncol].rearrange("(k p) n -> p k n", p=128)

        def wr(src, r0, nk, c0, ncol):
            return src[r0:r0 + nk * 128, c0:c0 + ncol].rearrange("(k p) n -> p k n", p=128)

        P_conv = []
        for gi in range(2):
            up = []
            for j in range(8):
                J = gi * 8 + j
                up.append((W.add(wv(w_in, 2048 + J * 128, 128), 1, 16, 128),
                           W.add(wv(w_in, 4096 + J * 128, 128), 1, 16, 128),
                           W.add(wv(w_in, J * 128, 128), 1, 16, 128)))
            dn = [W.add(wr(w_out, gi * 1024, 8, ng * 512, 512), 2, 8, 512) for ng in range(4)]
            P_conv.append((up, dn))

        def plan_mlp(l):
            out = []
            for gi in range(8):
                up = [W.add(wv(w_up[l], (gi * 8 + 2 * pi) * 128, 256), 2, 16, 256) for pi in range(4)]
                dn = [W.add(wr(w_dn[l], gi * 1024, 8, ng * 512, 512), 2, 8, 512) for ng in range(4)]
                out.append((up, dn))
            return out

        P_mlp0 = plan_mlp(0)
        P_k = W.add(wv(w_qkv, 2048, 256), 2, 16, 256)
        P_v = W.add(wv(w_qkv, 2304, 256), 2, 16, 256)
        P_att = []
        for g in range(4):
            qp = [W.add(wv(w_qkv, g * 512 + qh * 256, 256), 2, 16, 256) for qh in range(2)]
            ao = [W.add(wr(w_ao, g * 512, 4, hf * 1024, 1024), 2, 4, 1024) for hf in range(2)]
            P_att.append((qp, ao))
        P_mlp1 = plan_mlp(1)

        S.dma("sp", lambda e: e.dma_start(out=GB, in_=ln[0:1, :].partition_broadcast(128)), "gb", writes=["T0", "T1", "T1h"])
        for t in range(10):
            rows = 128 if t < 9 else 34
            S.dma("sp", lambda e, t=t, rows=rows: e.dma_start(out=X[0:rows, t, :], in_=xtok[t * 128:t * 128 + rows, :]),
                  ("x", t), writes=XK(t))
        S.dma("sp", lambda e: e.dma_start(out=X[64:75, 9, :], in_=swc), "swc", writes=["X9hi"])
        S.dma("sp", lambda e: e.dma_start(out=COS[:], in_=cos_t), "cos", writes=["COS"])
        S.dma("sp", lambda e: e.dma_start(out=SIN[:], in_=sin_t), "sin", writes=["SIN"])
        S.dma("sp", lambda e: e.dma_start(out=GQ[:], in_=qn.partition_broadcast(128)), "gq", writes=["GQ"])
        S.dma("sp", lambda e: e.dma_start(out=GK[:], in_=kn.partition_broadcast(128)), "gk", writes=["GK"])
        S.dma("sp", lambda e: e.dma_start(out=SK[:], in_=sinks.partition_broadcast(128)), "sk", writes=["SK"])
        S.dma("pool", lambda e: e.dma_start(out=maskb[:], in_=masks), "maskb", writes=["maskb"])
        S.dma("pool", lambda e: e.dma_start(out=msb[:], in_=mask_s), "msb", writes=["msb"])
        S.op("pool", lambda e: e.memset(identf[:], 1.0), writes=["identf"])
        S.op("pool", lambda e: e.affine_select(out=identf[:], in_=identf[:], pattern=[[-1, 128]],
                                               compare_op=ALU.is_equal, fill=0.0, base=0, channel_multiplier=1),
             reads=["identf"], writes=["identf"])
        S.op("dve", lambda e: e.tensor_copy(out=ident[:], in_=identf[:]), reads=["identf"], writes=["ident"])
        S.op("dve", lambda e: e.tensor_scalar(out=mask01[:], in0=maskb[:], scalar1=0.0, scalar2=None, op0=ALU.is_equal),
             reads=["maskb"], writes=["mask01"])
        S.op("dve", lambda e: e.memset(ones64[:], 1.0), writes=["ones64"])
        S.op("dve", lambda e: e.memset(EPSB[:], EPS), writes=["EPSB"])

        norm_cnt = [0]

        class NormPipe:
            def __init__(self, li, tiles, rows9, hbsel, gb_loaded=False):
                Hb, HbK = hbsel
                if not gb_loaded:
                    S.dma("sp", lambda e: e.dma_start(out=GB, in_=ln[li:li + 1, :].partition_broadcast(128)),
                          "gb", writes=["T0", "T1", "T1h"])
                self.info = []
                for t in tiles:
                    i = norm_cnt[0]
                    norm_cnt[0] += 1
                    self.info.append((t, 128 if t < 9 else rows9, Hb[i % len(Hb)], HbK[i % len(Hb)], i % 16, (i % 2) * 2))
                self.nbuf = len(Hb)
                self.idx = {t: k for k, t in enumerate(tiles)}
                self.na = 0
                self.nb = 0

            def _a(self, t, rows, hb, hk, col, b0):
                S.op("act", lambda e: e.activation(
                    out=hb[0:rows, :], in_=X[0:rows, t, :], func=AF.Square, accum_out=SS[0:rows, col:col + 1]),
                    reads=XK(t), writes=hk + [("SS", col)])
                S.op("act", lambda e: e.activation(
                    out=SS[0:rows, col:col + 1], in_=SS[0:rows, col:col + 1], func=AF.Sqrt,
                    bias=EPSB[0:rows, :], scale=1.0 / D), reads=[("SS", col), "EPSB"], writes=[("SS", col)])
                S.op("dve", lambda e: e.reciprocal(out=SS[0:rows, col:col + 1], in_=SS[0:rows, col:col + 1]),
                     reads=[("SS", col)], writes=[("SS", col)])
                S.op("dve", lambda e: e.scalar_tensor_tensor(
                    out=hb[0:rows, :], in0=X[0:rows, t, :], scalar=SS[0:rows, col:col + 1], in1=GB[0:rows, :],
                    op0=ALU.mult, op1=ALU.mult), reads=XK(t) + [("SS", col), "T0", "T1", "T1h"], writes=hk)

            def _b(self, t, rows, hb, hk, col, b0):
                def tr(e):
                    for c in range(16):
                        ins = e.transpose(out=psb[:, b0 + c // 8, (c % 8) * 128:(c % 8) * 128 + rows],
                                          in_=hb[0:rows, c * 128:(c + 1) * 128], identity=ident[0:rows, 0:rows])
                    return ins
                S.op("pe", tr, reads=hk + ["ident"], writes=[("ps", b0), ("ps", b0 + 1)])
                for half in range(2):
                    src = psb[:, b0 + half, :].rearrange("p (c n) -> p c n", n=128)[:, :, 0:rows]
                    dst = HT3[:, half * 8:(half + 1) * 8, t * 128:t * 128 + rows]
                    if half == 0:
                        S.op("act", lambda e, src=src, dst=dst: e.activation(out=dst, in_=src, func=AF.Copy),
                             reads=[("ps", b0 + half)], writes=[("H", t, half)])
                    else:
                        S.op("dve", lambda e, src=src, dst=dst: e.tensor_copy(out=dst, in_=src),
                             reads=[("ps", b0 + half)], writes=[("H", t, half)])

            def tile_ready(self, t):
                if t not in self.idx:
                    return
                assert self.idx[t] == self.na
                if self.nbuf >= 3:
                    self._a(*self.info[self.na])
                    self.na += 1
                    if self.na >= 3:
                        self._b(*self.info[self.nb])
                        self.nb += 1
                else:
                    if self.na >= 2:
                        self._b(*self.info[self.nb])
                        self.nb += 1
                    self._a(*self.info[self.na])
                    self.na += 1

            def finish(self):
                n = len(self.info)
                while self.nb < n:
                    if self.na < n and (self.na - self.nb) < 2:
                        self._a(*self.info[self.na])
                        self.na += 1
                        continue
                    self._b(*self.info[self.nb])
                    self.nb += 1

        def HKall(tiles):
            return [("H", t, h) for t in tiles for h in range(2)]

        def fm_mm(Wv, c_lo, bank0, tgs):
            def f(e):
                for k in range(16):
                    for gi, (c0, c1) in enumerate(tgs):
                        ins = e.matmul(ps[:, bank0 + gi, 0:c1 - c0], lhsT=Wv[:, k, c_lo:c_lo + 128],
                                       rhs=HT3[:, k, c0:c1], start=(k == 0), stop=(k == 15))
                return ins
            return f

        def flat(bank0, n):
            return ps[:, bank0:bank0 + 3, :].rearrange("p b n -> p (b n)")[:, 0:n]

        def BK(bank0):
            return [("ps", bank0), ("ps", bank0 + 1), ("ps", bank0 + 2)]

        dn_cnt = [0]

        def down_phase(pieces, nk, tiles, colbase, rows9, akeys, hook=None):
            def one(Wv, wk, ngi, t):
                rows = 128 if t < 9 else rows9
                lc = t * 128 - colbase
                bank = 6 + (dn_cnt[0] % 2)
                dn_cnt[0] += 1
                xg = ngi

                def mm(e):
                    for c in range(nk):
                        ins = e.matmul(ps[0:rows, bank, :], lhsT=AT3[:, c, lc:lc + rows],
                                       rhs=Wv[:, c, :], start=(c == 0), stop=(c == nk - 1))
                    return ins
                S.op("pe", mm, reads=wk + akeys, writes=[("ps", bank)])
                S.op("dve", lambda e: e.tensor_tensor(
                    out=X[0:rows, t, xg * 512:(xg + 1) * 512], in0=X[0:rows, t, xg * 512:(xg + 1) * 512],
                    in1=ps[0:rows, bank, :], op=ALU.add), reads=[("ps", bank), ("X", t, xg)], writes=[("X", t, xg)])

            first = pieces if hook is None else pieces[:-2]
            for ngi, pid in enumerate(first):
                Wv, wk = W.get(pid)
                for t in tiles:
                    one(Wv, wk, ngi, t)
                W.done(pid)
            if hook is not None:
                n0 = len(pieces) - 2
                Wa_, wka_ = W.get(pieces[n0])
                Wb_, wkb_ = W.get(pieces[n0 + 1])
                for t in tiles:
                    one(Wa_, wka_, n0, t)
                    one(Wb_, wkb_, n0 + 1, t)
                    hook(t)
                W.done(pieces[n0])
                W.done(pieces[n0 + 1])

        def ckpt(k):
            if k > stage:
                raise _Stop()

        def record():
            TG0 = [(0, 512), (512, 1024), (1024, NT)]
            ALL10 = list(range(10))
            NormPipe(0, ALL10, 34, HB_ATT, gb_loaded=True).finish()
            def swt_mm(e):
                for c in range(16):
                    ins = e.matmul(ps[:, 0, c * 11:(c + 1) * 11], lhsT=X[64:75, 9, c * 128:(c + 1) * 128],
                                   rhs=identf[64:75, 64:75], start=True, stop=True)
                return ins
            S.op("pe", swt_mm, reads=["X9hi", "identf"], writes=[("ps", 0)])
            S.op("act", lambda e: e.activation(out=SWT[:].rearrange("p c n -> p (c n)"), in_=ps[:, 0, 0:176], func=AF.Copy),
                 reads=[("ps", 0)], writes=["SWT"])
            ckpt(2)
            HA = HKall(ALL10)
            ubuf = T1
            ubs = T1[:, 1154:1194].rearrange("p (b n) -> p b n", n=10)
            t1 = T2
            csb = T0
            setc = [0]

            def nextset():
                s = (setc[0] % 2) * 3
                setc[0] += 1
                return s

            for gi in range(2):
                up, dn = P_conv[gi]
                for j in range(8):
                    J = gi * 8 + j
                    pc, pv, pb = up[j]
                    w0 = SWT[:, J, 8:9]
                    w1 = SWT[:, J, 9:10]
                    w2 = SWT[:, J, 10:11]
                    Wc, wkc = W.get(pc)
                    sA = nextset()
                    S.op("pe", fm_mm(Wc, 0, sA, TG0), reads=wkc + HA, writes=BK(sA))
                    W.done(pc)
                    S.op("act", lambda e, sA=sA: e.activation(out=csb[:, 0:NT], in_=flat(sA, NT), func=AF.Copy),
                         reads=BK(sA), writes=["T0"])
                    Wvv, wkv = W.get(pv)
                    sB = nextset()
                    S.op("pe", fm_mm(Wvv, 0, sB, TG0), reads=wkv + HA, writes=BK(sB))
                    W.done(pv)
                    S.op("dve", lambda e, sB=sB: e.tensor_tensor(out=ubuf[:, 2:1154], in0=csb[:, 0:1152], in1=flat(sB, NT)[:, 0:1152], op=ALU.mult),
                         reads=BK(sB) + ["T0"], writes=["T1"])
                    S.op("dve", lambda e, sB=sB: e.tensor_tensor(
                        out=ubs[:, :, 2:10], in0=csb[:, 1152:1184].rearrange("p (b n) -> p b n", n=8),
                        in1=flat(sB, NT)[:, 1152:1184].rearrange("p (b n) -> p b n", n=8), op=ALU.mult),
                        reads=BK(sB) + ["T0"], writes=["T1s"])
                    S.op("dve", lambda e, sB=sB: e.tensor_tensor(out=ubuf[:, 0:2], in0=csb[:, 1184:1186], in1=flat(sB, NT)[:, 1184:1186], op=ALU.mult),
                         reads=BK(sB) + ["T0"], writes=["T1h"])
                    S.op("pool", lambda e, J=J: e.tensor_copy(out=ubs[:, :, 0:2], in_=SWT[:, J, 0:8].rearrange("p (b n) -> p b n", n=2)),
                         reads=["SWT"], writes=["T1p"])
                    S.op("pool", lambda e, J=J: e.tensor_copy(out=UK[:, J, 0:2], in_=ubuf[:, 1152:1154]), reads=["T1"], writes=[("UK", J, 0)])
                    S.op("pool", lambda e, J=J: e.tensor_copy(out=UK[:, J, 2:10].rearrange("p (b n) -> p b n", n=2), in_=ubs[:, :, 8:10]),
                         reads=["T1s"], writes=[("UK", J, 1)])
                    UR = ["T1", "T1h"]
                    S.op("dve", lambda e, w0=w0: e.tensor_scalar(out=t1[:, 0:1152], in0=ubuf[:, 0:1152], scalar1=w0, scalar2=None, op0=ALU.mult),
                         reads=UR + ["SWT"], writes=["T2"])
                    S.op("dve", lambda e, w1=w1: e.scalar_tensor_tensor(out=t1[:, 0:1152], in0=ubuf[:, 1:1153], scalar=w1, in1=t1[:, 0:1152], op0=ALU.mult, op1=ALU.add),
                         reads=UR + ["T2"], writes=["T2"])
                    S.op("dve", lambda e, w2=w2: e.scalar_tensor_tensor(out=t1[:, 0:1152], in0=ubuf[:, 2:1154], scalar=w2, in1=t1[:, 0:1152], op0=ALU.mult, op1=ALU.add),
                         reads=UR + ["T2"], writes=["T2"])
                    t1s = t1[:, 1152:1184].rearrange("p (b n) -> p b n", n=8)
                    USR = ["T1s", "T1p"]
                    S.op("dve", lambda e, w0=w0, t1s=t1s: e.tensor_scalar(out=t1s, in0=ubs[:, :, 0:8], scalar1=w0, scalar2=None, op0=ALU.mult),
                         reads=USR + ["SWT"], writes=["T2s"])
                    S.op("dve", lambda e, w1=w1, t1s=t1s: e.scalar_tensor_tensor(out=t1s, in0=ubs[:, :, 1:9], scalar=w1, in1=t1s, op0=ALU.mult, op1=ALU.add),
                         reads=USR + ["T2s"], writes=["T2s"])
                    S.op("dve", lambda e, w2=w2, t1s=t1s: e.scalar_tensor_tensor(out=t1s, in0=ubs[:, :, 2:10], scalar=w2, in1=t1s, op0=ALU.mult, op1=ALU.add),
                         reads=USR + ["T2s"], writes=["T2s"])
                    Wb, wkb = W.get(pb)
                    sC = nextset()
                    S.op("pe", fm_mm(Wb, 0, sC, TG0), reads=wkb + HA, writes=BK(sC))
                    W.done(pb)
                    S.op("dve", lambda e, sC=sC, j=j: e.tensor_tensor(out=AT3[:, j, 0:1184], in0=flat(sC, NT)[:, 0:1184], in1=t1[:, 0:1184], op=ALU.mult),
                         reads=BK(sC) + ["T2", "T2s"], writes=[("A", j)])
                if gi == 0:
                    down_phase(dn, 8, ALL10, 0, 32, [("A", c) for c in range(8)])
                else:
                    def uk_mm(e):
                        for c in range(16):
                            ins = e.matmul(ps[0:10, c // 4, (c % 4) * 128:(c % 4 + 1) * 128], lhsT=UK[:, c, :], rhs=identf[:, :],
                                           start=True, stop=True)
                        return ins
                    S.op("pe", uk_mm, reads=[("UK", J, h) for J in range(16) for h in range(2)] + ["identf"],
                         writes=[("ps", b) for b in range(4)])
                    S.op("act", lambda e: e.activation(out=TMP[0:10, 0:2048], in_=ps[0:10, 0:4, :].rearrange("p b n -> p (b n)"), func=AF.Copy),
                         reads=[("ps", b) for b in range(4)], writes=["T0", "T1", "T1h"])
                    S.dma("sp", lambda e: e.dma_start(out=ncv_o, in_=TMP[0:10, 0:2048]), "ncv", reads=["T0", "T1", "T1h"], is_output=True)


                    NP1 = NormPipe(1, ALL10, 32, HB_ATT)
                    down_phase(dn, 8, ALL10, 0, 32, [("A", c) for c in range(8)], hook=NP1.tile_ready)
            ckpt(3)

            def mlp(NP, plan, tiles, colbase, rows9, tgs, next_norm=None):
                NP.finish()
                NPn = None
                HA_ = HKall(tiles)
                ncols = tgs[-1][1] - tgs[0][0]
                rcnt = 0
                for gi in range(8):
                    up, dn = plan[gi]
                    for pi in range(4):
                        Wv, wk = W.get(up[pi])
                        for cl in range(2):
                            c = pi * 2 + cl
                            sA = nextset()
                            S.op("pe", fm_mm(Wv, cl * 128, sA, tgs), reads=wk + HA_, writes=BK(sA))
                            rt = (T0, T1)[rcnt % 2]
                            rk = (["T0"], ["T1", "T1h", "T1s", "T1p"])[rcnt % 2]
                            rcnt += 1
                            S.op("act", lambda e, sA=sA, rt=rt: e.activation(out=rt[:, 0:ncols], in_=flat(sA, ncols), func=AF.Relu),
                                 reads=BK(sA), writes=rk)
                            S.op("pool", lambda e, rt=rt, c=c: e.tensor_tensor(out=AT3[:, c, 0:ncols], in0=rt[:, 0:ncols], in1=rt[:, 0:ncols], op=ALU.mult),
                                 reads=rk, writes=[("A", c)])
                        W.done(up[pi])
                    if gi == 7 and next_norm is not None:
                        NPn = NormPipe(*next_norm)
                        down_phase(dn, 8, tiles, colbase, rows9, [("A", c) for c in range(8)], hook=NPn.tile_ready)
                    else:
                        down_phase(dn, 8, tiles, colbase, rows9, [("A", c) for c in range(8)])
                return NPn

            ckpt(4)
            NP2 = mlp(NP1, P_mlp0, ALL10, 0, 32, TG0, next_norm=(2, ALL10, 32, HB_ATT))
            ckpt(5)

            NP2.finish()
            HA1 = HKall(ALL10)

            S.op("dve", lambda e: e.tensor_reduce(out=SM[:, 0:1], in_=GQ[:], axis=AX.X, op=ALU.max, apply_absolute_value=True), reads=["GQ"], writes=["SM0"])
            S.op("dve", lambda e: e.tensor_reduce(out=SM[:, 1:2], in_=GK[:], axis=AX.X, op=ALU.max, apply_absolute_value=True), reads=["GK"], writes=["SM1"])
            S.op("dve", lambda e: e.tensor_reduce(out=SM[:, 2:3], in_=SK[:], axis=AX.X, op=ALU.max), reads=["SK"], writes=["SM2"])
            S.op("dve", lambda e: e.tensor_tensor(out=SM[:, 3:4], in0=SM[:, 0:1], in1=SM[:, 1:2], op=ALU.mult), reads=["SM0", "SM1"], writes=["SM3"])
            S.op("dve", lambda e: e.scalar_tensor_tensor(out=SM[:, 4:5], in0=SM[:, 3:4], scalar=8.0, in1=SM[:, 2:3], op0=ALU.mult, op1=ALU.max),
                 reads=["SM3", "SM2"], writes=["SM4"])
            S.op("dve", lambda e: e.tensor_scalar(out=SM[:, 5:6], in0=SM[:, 4:5], scalar1=-1.0, scalar2=None, op0=ALU.mult), reads=["SM4"], writes=["NEGM"])
            NEGM = SM[:, 5:6]
            S.op("act", lambda e: e.activation(out=SE0[:], in_=SK[:], func=AF.Exp, bias=NEGM, scale=1.0), reads=["SK", "NEGM"], writes=["SE0"])
            SE0v = SE0[:].rearrange("p (a b) -> p a b", b=2)
            S.op("dve", lambda e: e.tensor_copy(out=SE[0:64, :], in_=SE0v[0:64, :, 0]), reads=["SE0"], writes=["SEa"])
            S.op("dve", lambda e: e.tensor_copy(out=SE[64:128, :], in_=SE0v[64:128, :, 1]), reads=["SE0"], writes=["SEb"])

            ckpt(5.1)
            S.dma("pool", lambda e: e.dma_start(out=CKB, in_=ck.rearrange("b k n -> k b n")), "ckb", writes=CKBK)
            S.dma("pool", lambda e: e.dma_start(out=Vc, in_=cv.rearrange("b k n -> k b n")), "vc", writes=["VC"])
            S.dma("sp", lambda e: e.dma_start(out=ks_o[:, 0:120, :], in_=ck[:, 8:128, :]), "ksw", is_output=True)
            S.dma("sp", lambda e: e.dma_start(out=vs_o[:, 0:120, :], in_=cv[:, 8:128, :]), "vsw", is_output=True)
            ckpt(5.2)
            kd4 = KDUP.rearrange("p (k d n) -> p k d n", d=2, n=64)
            trc = [0]

            def ktrans(src_rows, rows, dstcols):
                bank = 4 + (trc[0] % 2)
                trc[0] += 1

                def tr(e):
                    for kv in range(4):
                        ins = e.transpose(out=psb[:, bank, kv * 128:kv * 128 + rows], in_=KDUP[0:rows, kv * 128:(kv + 1) * 128],
                                          identity=ident[0:rows, 0:rows])
                    return ins
                S.op("pe", tr, reads=["KDUP", "ident"], writes=[("ps", bank)])
                S.op("act", lambda e: e.activation(
                    out=KT2[:, :, dstcols:dstcols + rows],
                    in_=psb[:, bank, 0:512].rearrange("p (k n) -> p k n", n=128)[:, :, 0:rows], func=AF.Copy),
                    reads=[("ps", bank)] + ALLT, writes=[("KT2", dstcols)])

            for b in range(4):
                S.op("dve", lambda e, b=b: e.tensor_copy(
                    out=kd4, in_=CKB[:, b, :].rearrange("p (k n) -> p k n", n=64).unsqueeze(2).broadcast_to([128, 4, 2, 64])),
                    reads=CKBK, writes=["KDUP"])
                ktrans(None, 128, 1184 + b * 128)

            ckpt(5.3)
            def build_tables(G, gkey):
                S.op("dve", lambda e: e.tensor_tensor(out=COS[:], in0=COS[:], in1=G[:].unsqueeze(1).broadcast_to([128, 10, 64]), op=ALU.mult),
                     reads=["COS", gkey], writes=["COS"])
                S.op("dve", lambda e: e.tensor_tensor(out=SIN[:, :, 0:32], in0=SIN[:, :, 0:32],
                                                      in1=G[:, 32:64].unsqueeze(1).broadcast_to([128, 10, 32]), op=ALU.mult),
                     reads=["SIN", gkey], writes=["SIN"])
                S.op("dve", lambda e: e.tensor_tensor(out=SIN[:, :, 32:64], in0=SIN[:, :, 32:64],
                                                      in1=G[:, 0:32].unsqueeze(1).broadcast_to([128, 10, 32]), op=ALU.mult),
                     reads=["SIN", gkey], writes=["SIN"])

            def qk_chain(bank, rows, t, cb, out_ap, out_keys):
                xs = XS[cb][0:rows, :]
                xs3 = xs.rearrange("p (h d) -> p h d", d=64)
                a3 = AB[cb][0:rows, :].rearrange("p (h d) -> p h d", d=64)
                b3 = BB[cb][0:rows, :].rearrange("p (h d) -> p h d", d=64)
                s4 = S4T[0:rows, cb * 4:cb * 4 + 4]
                xk, ak, bk, sk_ = "XS%d" % cb, "AB%d" % cb, "BB%d" % cb, "S4%d" % cb
                S.op("act", lambda e: e.activation(out=xs, in_=ps[0:rows, bank, 0:256], func=AF.Copy), reads=[("ps", bank)], writes=[xk])
                for h in range(4):
                    S.op("act", lambda e, h=h: e.activation(out=JUNK[0:rows, h * 64:(h + 1) * 64], in_=xs[:, h * 64:(h + 1) * 64],
                                                            func=AF.Square, accum_out=s4[:, h:h + 1]),
                         reads=[xk], writes=[("J", h), (sk_, h)])
                S.op("act", lambda e: e.activation(out=s4, in_=s4, func=AF.Sqrt, bias=EPSB[0:rows, :], scale=1.0 / 64),
                     reads=[(sk_, h) for h in range(4)] + ["EPSB"], writes=[sk_])
                S.op("dve", lambda e: e.tensor_tensor(out=a3, in0=xs3, in1=COS[0:rows, t, :].unsqueeze(1).broadcast_to([rows, 4, 64]), op=ALU.mult),
                     reads=[xk, "COS"], writes=[ak])
                S.op("dve", lambda e: e.tensor_tensor(out=b3[:, :, 0:32], in0=xs3[:, :, 32:64],
                                                      in1=SIN[0:rows, t, 0:32].unsqueeze(1).broadcast_to([rows, 4, 32]), op=ALU.mult),
                     reads=[xk, "SIN"], writes=[bk + "a"])
                S.op("dve", lambda e: e.tensor_tensor(out=b3[:, :, 32:64], in0=xs3[:, :, 0:32],
                                                      in1=SIN[0:rows, t, 32:64].unsqueeze(1).broadcast_to([rows, 4, 32]), op=ALU.mult),
                     reads=[xk, "SIN"], writes=[bk + "b"])
                S.op("dve", lambda e: e.reciprocal(out=s4, in_=s4), reads=[sk_], writes=[sk_])
                S.op("pool", lambda e: e.tensor_tensor(out=AB[cb][0:rows, :], in0=AB[cb][0:rows, :], in1=BB[cb][0:rows, :], op=ALU.add),
                     reads=[ak, bk + "a", bk + "b"], writes=[ak])
                S.op("pool", lambda e: e.tensor_tensor(out=out_ap, in0=a3, in1=s4.unsqueeze(2).broadcast_to([rows, 4, 64]), op=ALU.mult),
                     reads=[ak, sk_], writes=out_keys)

            pj = [0]

            def tm_proj(Wv, wk, t, rows):
                bank = 6 + (pj[0] % 2)
                pj[0] += 1

                def mm(e):
                    for k in range(16):
                        ins = e.matmul(ps[0:rows, bank, 0:256], lhsT=HT3[:, k, t * 128:t * 128 + rows], rhs=Wv[:, k, :],
                                       start=(k == 0), stop=(k == 15))
                    return ins
                S.op("pe", mm, reads=wk + [("H", t, 0), ("H", t, 1)], writes=[("ps", bank)])
                return bank

            def proj_loop(Wv, wk, tiles, post_a, post_b=None):
                n = len(tiles)
                rws = [128 if t < 9 else 32 for t in tiles]
                banks = [None] * n
                banks[0] = tm_proj(Wv, wk, tiles[0], rws[0])
                if n > 1:
                    banks[1] = tm_proj(Wv, wk, tiles[1], rws[1])
                post_a(0, tiles[0], rws[0], banks[0])
                for i in range(n):
                    if i + 2 < n:
                        banks[i + 2] = tm_proj(Wv, wk, tiles[i + 2], rws[i + 2])
                    if i + 1 < n:
                        post_a(i + 1, tiles[i + 1], rws[i + 1], banks[i + 1])
                    if post_b is not None:
                        post_b(i, tiles[i], rws[i])

            build_tables(GK, "GK")
            Wk_, wkk = W.get(P_k)

            def k_post_a(i, t, rows, bank):
                KRB = KRBS[i % 2]
                qk_chain(bank, rows, t, i % 2, KRB[0:rows, :].rearrange("p (h d) -> p h d", d=64), ["KRB%d" % (i % 2)])

            def k_post_b(i, t, rows):
                KRB = KRBS[i % 2]
                kk = "KRB%d" % (i % 2)
                S.op("act", lambda e: e.activation(
                    out=kd4[0:rows], in_=KRB[0:rows, :].rearrange("p (k n) -> p k n", n=64).unsqueeze(2).broadcast_to([rows, 4, 2, 64]),
                    func=AF.Copy), reads=[kk], writes=["KDUP"])
                if t == 8:
                    S.dma("sp", lambda e: e.dma_start(out=kp_o, in_=KRB[:, :]), "kp", reads=[kk], is_output=True)
                if t == 9:
                    for b in range(4):
                        S.dma("sp", lambda e, b=b: e.dma_start(out=ks_o[b, 120:128, :], in_=KRB[b * 8:(b + 1) * 8, :]),
                              ("ksn", b), reads=[kk], is_output=True)
                ktrans(None, rows, t * 128)
            proj_loop(Wk_, wkk, ALL10, k_post_a, k_post_b)
            W.done(P_k)
            ckpt(5.4)
            S.dma("sp", lambda e: e.dma_start(out=COS[:], in_=cos_t), "cos", writes=["COS"])
            S.dma("sp", lambda e: e.dma_start(out=SIN[:], in_=sin_t), "sin", writes=["SIN"])
            build_tables(GQ, "GQ")
            Wv_, wkv_ = W.get(P_v)

            def v_post(i, t, rows, bank):
                S.op("act", lambda e: e.activation(out=V[0:rows, t, :], in_=ps[0:rows, bank, 0:256], func=AF.Copy),
                     reads=[("ps", bank)], writes=[("V", t)])
                if t >= 8:
                    S.op("dve", lambda e: e.tensor_copy(out=KRBS[0][0:rows, :], in_=ps[0:rows, bank, 0:256]),
                         reads=[("ps", bank), ("V", t)], writes=["KRB0"])
                    if t == 8:
                        S.dma("sp", lambda e: e.dma_start(out=vp_o, in_=KRBS[0][:, :]), "vp", reads=["KRB0"], is_output=True)
                    else:
                        for b in range(4):
                            S.dma("sp", lambda e, b=b: e.dma_start(out=vs_o[b, 120:128, :], in_=KRBS[0][b * 8:(b + 1) * 8, :]),
                                  ("vsn", b), reads=["KRB0"], is_output=True)
            proj_loop(Wv_, wkv_, ALL10, v_post)
            W.done(P_v)

            ckpt(6)
            QT3 = AT3[:, 0:4, :]
            OT3 = AT3[:, 4:8, :]
            KT2all = [("KT2", c) for c in [t * 128 for t in range(10)] + [1184 + b * 128 for b in range(4)]] + ALLT
            ptc = [0]
            TILES1 = list(range(1, 10))
            for g in range(4):
                qp, ao = P_att[g]
                for qh in range(2):
                    Wq, wkq = W.get(qp[qh])

                    def q_post_a(i, t, rows, bank):
                        cb = i % 2
                        qk_chain(bank, rows, t, cb, QRB[cb][0:rows, :].rearrange("p (h d) -> p h d", d=64), ["QRB%d" % cb])

                    def q_post_b(i, t, rows, qh=qh):
                        cb = i % 2
                        qb = QRB[cb]
                        qbk = "QRB%d" % cb
                        tb = 4 + (trc[0] % 2)
                        trc[0] += 1

                        def tr(e):
                            for pr in range(2):
                                ins = e.transpose(out=psb[:, tb, pr * 128:pr * 128 + rows], in_=qb[0:rows, pr * 128:(pr + 1) * 128],
                                                  identity=ident[0:rows, 0:rows])
                            return ins
                        S.op("pe", tr, reads=[qbk, "ident"], writes=[("ps", tb)])
                        lc = t * 128 - 128
                        S.op("act", lambda e: e.activation(
                            out=QT3[:, qh * 2:qh * 2 + 2, lc:lc + rows],
                            in_=psb[:, tb, 0:256].rearrange("p (k n) -> p k n", n=128)[:, :, 0:rows], func=AF.Copy),
                            reads=[("ps", tb)], writes=[("A", qh * 2), ("A", qh * 2 + 1)])
                    proj_loop(Wq, wkq, TILES1, q_post_a, q_post_b)
                    W.done(qp[qh])
                QK_ = [("A", c) for c in range(4)]
                OK_ = [("A", c) for c in range(4, 8)]
                bufs = {}
                for n in range(1, 9):
                    bufs[n] = ptc[0] % 2
                    ptc[0] += 1

                def score_exp(n, kbi, g=g):
                    buf = bufs[n]
                    P4 = PT[buf].rearrange("p (k r n) -> p k r n", k=2, r=2)
                    lc = (n - 1) * 128
                    kt = (n - 1, n)[kbi]
                    mi = kbi if (kbi == 1 or n > 1) else 2

                    def st(e):
                        for par in range(2):
                            bank = kbi * 2 + par
                            ph = slice(par * 64, par * 64 + 64)
                            ins = e.matmul(ps[:, bank, :], lhsT=KT2[ph, g, kt * 128:(kt + 1) * 128], rhs=QT3[ph, :, lc:lc + 128],
                                           start=True, stop=True)
                        return ins
                    S.op("pe", st, reads=KT2all + QK_, writes=[("ps", kbi * 2), ("ps", kbi * 2 + 1)])
                    S.op("act", lambda e: e.activation(
                        out=P4[:, kbi], in_=ps[:, kbi * 2:kbi * 2 + 2, :], func=AF.Exp, bias=NEGM, scale=0.125),
                        reads=[("ps", kbi * 2), ("ps", kbi * 2 + 1), "NEGM"], writes=[PTK[buf][kbi]])
                    pm = P4[:, kbi].rearrange("p r (a q) -> p (r a) q", q=128)
                    S.op("pool" if kbi == 0 else "dve", lambda e: e.tensor_tensor(
                        out=pm, in0=pm, in1=mask01[:, mi, :].unsqueeze(1).broadcast_to([128, 8, 128]), op=ALU.mult),
                        reads=[PTK[buf][kbi], "mask01"], writes=[PTK[buf][kbi]])

                def pv_norm(n, g=g):
                    buf = bufs[n]
                    P4 = PT[buf].rearrange("p (k r n) -> p k r n", k=2, r=2)
                    lc = (n - 1) * 128
                    bo = 4 + 2 * (n % 2)
                    bd = bo + 1

                    def pv(e):
                        for par in range(2):
                            ph = slice(par * 64, par * 64 + 64)
                            for kbi, kt in enumerate((n - 1, n)):
                                e.matmul(ps[ph, bo, :], lhsT=V[:, kt, g * 64:(g + 1) * 64], rhs=P4[:, kbi, par, :],
                                         start=(kbi == 0), stop=(kbi == 1), tile_position=(0, par * 64))
                        for par in range(2):
                            ph = slice(par * 64, par * 64 + 64)
                            for kbi in range(2):
                                ins = e.matmul(ps[ph, bd, :], lhsT=ones64[:, :], rhs=P4[:, kbi, par, :],
                                               start=(kbi == 0), stop=(kbi == 1), tile_position=(0, par * 64))
                        return ins
                    S.op("pe", pv, reads=PTK[buf] + [("V", n - 1), ("V", n), "ones64"], writes=[("ps", bo), ("ps", bd)])

                RDS = [(RD, RDK), (ATT[:, 2304:2816], ["AB0", "AB1"])]

                def norm_add(n, g=g):
                    rd, rdk = RDS[n % 2]
                    bd = 4 + 2 * (n % 2) + 1
                    S.op("dve", lambda e: e.tensor_tensor(
                        out=rd.rearrange("p (a q) -> p a q", q=128), in0=ps[:, bd, :].rearrange("p (a q) -> p a q", q=128),
                        in1=SE[:, 4 * g:4 * g + 4].unsqueeze(2).broadcast_to([128, 4, 128]), op=ALU.add),
                        reads=[("ps", bd), "SEa", "SEb"], writes=rdk)

                def norm_fin(n, g=g):
                    rd, rdk = RDS[n % 2]
                    lc = (n - 1) * 128
                    bo = 4 + 2 * (n % 2)
                    S.op("act", lambda e: e.activation(out=rd, in_=rd, func=AF.Ln), reads=rdk, writes=rdk)
                    S.op("act", lambda e: e.activation(out=rd, in_=rd, func=AF.Exp, scale=-1.0), reads=rdk, writes=rdk)
                    S.op("dve", lambda e: e.tensor_tensor(
                        out=OT3[:, :, lc:lc + 128], in0=ps[:, bo, :].rearrange("p (a q) -> p a q", q=128),
                        in1=rd.rearrange("p (a q) -> p a q", q=128), op=ALU.mult),
                        reads=[("ps", bo)] + rdk, writes=OK_)

                for i in range(1, 11):
                    if i <= 8:
                        score_exp(i, 0)
                        score_exp(i, 1)
                    if 1 <= i - 1 <= 8:
                        pv_norm(i - 1)
                        norm_add(i - 1)
                    if 1 <= i - 2 <= 8:
                        norm_fin(i - 2)
                buf = ptc[0] % 2
                ptc[0] += 1
                PTc = PT[buf][:, 0:256].rearrange("p (r n) -> p r n", r=2)
                PTn = PT[buf][0:32, 256:512].rearrange("p (r n) -> p r n", r=2)

                def s_mm(e, g=g):
                    for par in range(2):
                        ph = slice(par * 64, par * 64 + 64)
                        e.matmul(ps[:, par, 0:128], lhsT=ident[:, :],
                                 rhs=maskb[:, 0, 0:8].unsqueeze(1).broadcast_to([128, 16, 8]), start=True, stop=False)
                        for b in range(4):
                            e.matmul(ps[:, par, b * 32:(b + 1) * 32], lhsT=KT2[ph, g, 1184 + b * 128:1184 + (b + 1) * 128],
                                     rhs=QT3[ph, :, 1024 + b * 8:1024 + (b + 1) * 8], start=False, stop=(b == 3))
                    for par in range(2):
                        ph = slice(par * 64, par * 64 + 64)
                        e.matmul(ps[0:32, 2 + par, 0:128], lhsT=KT2[ph, g, 1152:1184],
                                 rhs=QT3[ph, :, 1024:1056].rearrange("p a (b t) -> p b a t", t=8), start=True, stop=False)
                        ins = e.matmul(ps[0:32, 2 + par, 0:128], lhsT=ident[0:32, 0:32],
                                       rhs=msb[:, :].rearrange("p (b t) -> p b t", t=8).unsqueeze(2).broadcast_to([32, 4, 4, 8]),
                                       start=False, stop=True)
                    return ins
                S.op("pe", s_mm, reads=KT2all + QK_ + ["ident", "maskb", "msb"], writes=[("ps", b) for b in range(4)])
                S.op("act", lambda e, PTc=PTc: e.activation(out=PTc, in_=ps[:, 0:2, 0:128], func=AF.Exp, bias=NEGM, scale=0.125),
                     reads=[("ps", 0), ("ps", 1), "NEGM"], writes=PTK[buf])
                S.op("act", lambda e, PTn=PTn: e.activation(out=PTn, in_=ps[0:32, 2:4, 0:128], func=AF.Exp, bias=NEGM[0:32, :], scale=0.125),
                     reads=[("ps", 2), ("ps", 3), "NEGM"], writes=PTK[buf])

                def s_pv(e, PTc=PTc, PTn=PTn, g=g):
                    for bank, use_v in ((4, True), (5, False)):
                        for par in range(2):
                            ph = slice(par * 64, par * 64 + 64)
                            lhs_n = V[0:32, 9, g * 64:(g + 1) * 64] if use_v else ones64[0:32, :]
                            e.matmul(ps[ph, bank, 0:128], lhsT=lhs_n, rhs=PTn[:, par, :], start=True, stop=False,
                                     tile_position=(0, par * 64))
                            for b in range(4):
                                lhs_c = Vc[:, b, g * 64:(g + 1) * 64] if use_v else ones64[:, :]
                                ins = e.matmul(ps[ph, bank, b * 32:(b + 1) * 32],
                                               lhsT=lhs_c, rhs=PTc[:, par, b * 32:(b + 1) * 32],
                                               start=False, stop=(b == 3), tile_position=(0, par * 64))
                    return ins
                S.op("pe", s_pv, reads=PTK[buf] + [("V", 9), "VC", "ones64"], writes=[("ps", 4), ("ps", 5)])
                rds = RD[:, 0:128].rearrange("p (b a t) -> p b a t", b=4, t=8)
                S.op("dve", lambda e, rds=rds, g=g: e.tensor_tensor(
                    out=rds, in0=ps[:, 5, 0:128].rearrange("p (b a t) -> p b a t", b=4, t=8),
                    in1=SE[:, 4 * g:4 * g + 4].unsqueeze(1).unsqueeze(3).broadcast_to([128, 4, 4, 8]), op=ALU.add),
                    reads=[("ps", 5), "SEa", "SEb"], writes=RDK)
                S.op("dve", lambda e: e.reciprocal(out=RD[:, 0:128], in_=RD[:, 0:128]), reads=RDK, writes=RDK)
                S.op("dve", lambda e, rds=rds: e.tensor_tensor(
                    out=OT3[:, :, 1024:1056].rearrange("p a (b t) -> p b a t", t=8),
                    in0=ps[:, 4, 0:128].rearrange("p (b a t) -> p b a t", b=4, t=8), in1=rds, op=ALU.mult),
                    reads=[("ps", 4)] + RDK, writes=OK_)
                if g == 3:
                    NP3 = NormPipe(3, TILES1, 32, HB_X0)
                for hf in range(2):
                    Wa, wka = W.get(ao[hf])
                    for t in TILES1:
                        rows = 128 if t < 9 else 32
                        lc = t * 128 - 128
                        for sub in range(2):
                            bank = 6 + (dn_cnt[0] % 2)
                            dn_cnt[0] += 1
                            xg = hf * 2 + sub

                            def mm(e, Wa=Wa, rows=rows, lc=lc, bank=bank, sub=sub):
                                for c in range(4):
                                    ins = e.matmul(ps[0:rows, bank, :], lhsT=OT3[:, c, lc:lc + rows],
                                                   rhs=Wa[:, c, sub * 512:(sub + 1) * 512], start=(c == 0), stop=(c == 3))
                                return ins
                            S.op("pe", mm, reads=wka + OK_, writes=[("ps", bank)])
                            S.op("dve", lambda e, t=t, rows=rows, bank=bank, xg=xg: e.tensor_tensor(
                                out=X[0:rows, t, xg * 512:(xg + 1) * 512], in0=X[0:rows, t, xg * 512:(xg + 1) * 512],
                                in1=ps[0:rows, bank, :], op=ALU.add), reads=[("ps", bank), ("X", t, xg)], writes=[("X", t, xg)])
                        if g == 3 and hf == 1:
                            NP3.tile_ready(t)
                    W.done(ao[hf])

            ckpt(7)
            TG1 = [(128, 640), (640, 1152), (1152, 1184)]
            if not skip_mlp1:
                mlp(NP3, P_mlp1, TILES1, 128, 32, TG1)

        try:
            record()
        except _Stop:
            pass
        TILES1 = list(range(1, 10))
        for g4 in range(4):
            for t in TILES1:
                rows = 128 if t < 9 else 32
                S.dma("sp", lambda e, t=t, rows=rows, g4=g4: e.dma_start(
                    out=y_o[(t - 1) * 128:(t - 1) * 128 + rows, g4 * 512:(g4 + 1) * 512], in_=X[0:rows, t, g4 * 512:(g4 + 1) * 512]),
                    ("x", t), reads=[("X", t, g4)], is_output=True)

        with nc.allow_low_precision("bf16 matmul operands with fp32 PSUM accumulation"):
            S.emit(st)
    return nc


_PROGRAM = None


def _rope_cos_sin(pos):
    half = 32
    try:
        import jax
        import jax.numpy as jnp
        cpu = jax.devices("cpu")[0]
        with jax.default_device(cpu):
            inv = 10000.0 ** (-jnp.arange(half, dtype=jnp.float32) / half)
            ang = jnp.asarray(pos, dtype=jnp.float32)[..., None] * inv
            return np.asarray(jnp.cos(ang), dtype=np.float32), np.asarray(jnp.sin(ang), dtype=np.float32)
    except Exception:
        inv = (np.float32(10000.0) ** (-(np.arange(half, dtype=np.float32)) / np.float32(half))).astype(np.float32)
        ang = (np.asarray(pos, np.float32)[..., None] * inv).astype(np.float32)
        return np.cos(ang).astype(np.float32), np.sin(ang).astype(np.float32)


def _tables(s, first):
    pos = np.zeros((128, 10), np.float32)
    r = np.arange(128)
    for t in range(9):
        pos[:, t] = s - 128 + 128 * t + r
    pos[:32, 9] = PAST + (r[:32] % 8)
    c, sn = _rope_cos_sin(pos)
    cos2 = np.concatenate([c, c], axis=-1)
    sinm = np.concatenate([-sn, sn], axis=-1)
    j = np.arange(128)[:, None]
    i = np.arange(128)[None, :]
    masks = np.zeros((128, 3, 128), np.float32)
    NEG = -30000.0
    masks[:, 0, :] = np.where(j > i, 0.0, NEG)
    masks[:, 1, :] = np.where(j <= i, 0.0, NEG)
    masks[:, 2, :] = NEG if first else np.where(j > i, 0.0, NEG)
    jj = np.arange(32)[:, None]
    qq = np.arange(32)[None, :]
    mask_s = np.where((jj // 8 == qq // 8) & ((jj % 8) <= (qq % 8)), 0.0, NEG).astype(np.float32)
    return np.ascontiguousarray(cos2), np.ascontiguousarray(sinm), masks, mask_s


def make_in_maps(x_prompt, x_sample, state_conv, cache_k_win, cache_v_win, ln_mix, ln_mlp,
                 w_conv_in, w_conv, w_conv_out, w_qkv, w_attn_out, q_norm, k_norm, sinks, w_up, w_down):
    f = lambda a: np.ascontiguousarray(np.asarray(a, dtype=np.float32))
    x_prompt, x_sample, state_conv = f(x_prompt), f(x_sample), f(state_conv)
    cache_k_win, cache_v_win = f(cache_k_win), f(cache_v_win)
    shared = {
        "ln": f(np.stack([np.asarray(ln_mix)[0], np.asarray(ln_mlp)[0], np.asarray(ln_mix)[1], np.asarray(ln_mlp)[1]])),
        "w_in": f(np.asarray(w_conv_in)[0]),
        "w_out": f(np.asarray(w_conv_out)[0]),
        "w_qkv": f(np.asarray(w_qkv)[0]),
        "w_ao": f(np.asarray(w_attn_out)[0]),
        "qn": f(np.asarray(q_norm)[0:1]),
        "kn": f(np.asarray(k_norm)[0:1]),
        "sinks": f(np.asarray(sinks)[0:1]),
        "w_up": f(w_up),
        "w_dn": f(w_down),
    }
    wc = f(np.asarray(w_conv)[0])
    in_maps = []
    for c in range(NCORES):
        bi, qi = c // 4, c % 4
        s = qi * 1024
        xt = np.zeros((NT, D), np.float32)
        if qi > 0:
            xt[0:128] = x_prompt[bi, s - 128:s]
            xt[1184:1186] = x_prompt[bi, s - 130:s - 128]
        xt[128:1152] = x_prompt[bi, s:s + 1024]
        xt[1152:1184] = x_sample[4 * c:4 * c + 4].reshape(32, D)
        swc = np.concatenate([state_conv[0, 4 * c:4 * c + 4].reshape(8, D), wc], axis=0)
        cos2, sinm, masks, mask_s = _tables(s, qi == 0)
        m = dict(shared)
        m.update({
            "xtok": xt, "swc": np.ascontiguousarray(swc),
            "ck": np.ascontiguousarray(cache_k_win[0, 4 * c:4 * c + 4].reshape(4, 128, 256)),
            "cv": np.ascontiguousarray(cache_v_win[0, 4 * c:4 * c + 4].reshape(4, 128, 256)),
            "cos_t": cos2, "sin_t": sinm, "masks": masks, "mask_s": mask_s,
        })
        in_maps.append(m)
    return in_maps


def assemble(R):
    y_prompt = np.zeros((2, 4096, D), np.float32)
    y_sample = np.zeros((32, 8, D), np.float32)
    ncp = np.zeros((1, 2, 2, D), np.float32)
    ncs = np.zeros((1, 32, 2, D), np.float32)
    kp = np.zeros((1, 2, 128, 4, 64), np.float32)
    vp = np.zeros((1, 2, 128, 4, 64), np.float32)
    ks = np.zeros((1, 32, 128, 4, 64), np.float32)
    vs = np.zeros((1, 32, 128, 4, 64), np.float32)
    for c in range(NCORES):
        if R[c] is None:
            continue
        bi, qi = c // 4, c % 4
        s = qi * 1024
        r = R[c]
        y_prompt[bi, s:s + 1024] = r["y"][0:1024]
        y_sample[4 * c:4 * c + 4] = r["y"][1024:1056].reshape(4, 8, D)
        ncs[0, 4 * c:4 * c + 4] = r["ncv"][2:10].reshape(4, 2, D)
        ks[0, 4 * c:4 * c + 4] = r["ks"].reshape(4, 128, 4, 64)
        vs[0, 4 * c:4 * c + 4] = r["vs"].reshape(4, 128, 4, 64)
        if qi == 3:
            ncp[0, bi] = r["ncv"][0:2]
            kp[0, bi] = r["kp"].reshape(128, 4, 64)
            vp[0, bi] = r["vp"].reshape(128, 4, 64)
    return (y_prompt, y_sample, ncp, ncs, kp, vp, ks, vs)


def kernel(**inputs):
    global _PROGRAM
    if _PROGRAM is None:
        _PROGRAM = build_program()
    in_maps = make_in_maps(**inputs)
    res = run_bass_kernel_spmd(_PROGRAM, in_maps, core_ids=list(range(NCORES)))
    return assemble(res.results)
```

```python
import numpy as np
from contextlib import ExitStack
import concourse.bass as bass
import concourse.mybir as mybir
from concourse.bass_utils import run_bass_kernel_spmd

F32 = mybir.dt.float32
BF16 = mybir.dt.bfloat16
AF = mybir.ActivationFunctionType
ALU = mybir.AluOpType
AX = mybir.AxisListType

D = 2048
DFF = 8192
NT = 1186
EPS = 1e-6
NCORES = 8
PAST = 16384


class _Op:
    __slots__ = ("eng", "fn", "deps", "signal", "key", "tick", "is_dma")

    def __init__(self, eng, fn, key, is_dma):
        self.eng = eng
        self.fn = fn
        self.deps = []
        self.signal = is_dma
        self.key = key
        self.tick = 0
        self.is_dma = is_dma


class Sched:
    ENGS = ("pe", "act", "dve", "pool", "sp")

    def __init__(self, nc, same_engine_sync=("act", "dve", "pool")):
        self.nc = nc
        self.streams = {e: [] for e in self.ENGS}
        self.last_writer = {}
        self.readers = {}
        self.same_sync = set(same_engine_sync)
        self.dma_counts = {}
        self.out_dmas = []

    def _add(self, op, reads, writes):
        deps = {}
        for r in reads:
            w = self.last_writer.get(r)
            if w is not None:
                deps[id(w)] = w
        for r in writes:
            w = self.last_writer.get(r)
            if w is not None:
                deps[id(w)] = w
            for rd in self.readers.get(r, ()):
                deps[id(rd)] = rd
        op.deps = list(deps.values())
        for r in reads:
            self.readers.setdefault(r, []).append(op)
        for r in writes:
            self.last_writer[r] = op
            self.readers[r] = []
        self.streams[op.eng].append(op)
        return op

    def op(self, eng, fn, reads=(), writes=()):
        return self._add(_Op(eng, fn, eng, False), reads, writes)

    def dma(self, eng, fn, slot, reads=(), writes=(), is_output=False):
        op = _Op(eng, fn, ("dma", slot), True)
        c = self.dma_counts.get(slot, 0) + 16
        self.dma_counts[slot] = c
        op.tick = c
        self._add(op, reads, writes)
        self.out_dmas.append(op)
        return op

    def finalize(self):
        fin = _Op("sp", None, "sp", False)
        fin.deps = list(self.out_dmas)
        self.streams["sp"].append(fin)
        for e in self.ENGS:
            for op in self.streams[e]:
                for d in op.deps:
                    if d.is_dma:
                        continue
                    if d.eng == op.eng and not op.is_dma and d.eng not in self.same_sync:
                        continue
                    d.signal = True
        for e in self.ENGS:
            t = 0
            for op in self.streams[e]:
                if op.is_dma:
                    continue
                if op.signal:
                    t += 1
                    op.tick = t

    def emit(self, stack):
        nc = self.nc
        self.finalize()
        sems = {}
        import os
        for i in range(int(os.environ.get("SEM_PAD", "0"))):
            stack.enter_context(nc.semaphore("pad%d" % i))
        for e in self.ENGS:
            sems[e] = stack.enter_context(nc.semaphore("s_" + e))
        for i, slot in enumerate(self.dma_counts):
            sems[("dma", slot)] = stack.enter_context(nc.semaphore("d%d" % i))
        block = stack.enter_context(nc.Block())
        sched = self

        def run(ename, eng):
            waited = {}
            for op in sched.streams[ename]:
                need = {}
                for d in op.deps:
                    if (not d.is_dma) and d.eng == ename and (not op.is_dma) and ename not in sched.same_sync:
                        continue
                    k = d.key
                    if need.get(k, 0) < d.tick:
                        need[k] = d.tick
                for k, t in need.items():
                    if waited.get(k, 0) >= t:
                        continue
                    eng.wait_ge(sems[k], t)
                    waited[k] = t
                if op.fn is None:
                    continue
                ins = op.fn(eng)
                if op.is_dma:
                    ins.then_inc(sems[op.key], 16)
                elif op.signal:
                    ins.then_inc(sems[ename], 1)

        @block.tensor
        def _(e):
            run("pe", e)

        @block.scalar
        def _(e):
            run("act", e)

        @block.vector
        def _(e):
            run("dve", e)

        @block.gpsimd
        def _(e):
            run("pool", e)

        @block.sync
        def _(e):
            run("sp", e)


UNIT = 2048
NUNITS = 6


class WStream:
    def __init__(self, S, ring):
        self.S = S
        self.ring = ring
        self.pieces = []
        self.views = {}
        self.next_issue = 0
        self.ptr = 0
        self.free = [True] * NUNITS
        self.units_of = {}
        self.hold_keys = []

    def add(self, src, nunits, a, b):
        self.pieces.append((src, nunits, a, b))
        return len(self.pieces) - 1

    def _try_issue(self):
        if self.next_issue >= len(self.pieces):
            return False
        src, nu, a, b = self.pieces[self.next_issue]
        p = self.ptr
        if p + nu > NUNITS:
            p = 0
        if not all(self.free[p:p + nu]):
            return False
        for u in range(p, p + nu):
            self.free[u] = False
        pid = self.next_issue
        self.units_of[pid] = (p, nu)
        view = self.ring[:, p * UNIT:p * UNIT + a * b].rearrange("p (a b) -> p a b", b=b)
        keys = [("ring", u) for u in range(p, p + nu)]
        self.views[pid] = (view, keys)
        extra = self.hold_keys if (2 <= pid < 6) else []
        self.S.dma("pool", lambda e, view=view, src=src: e.dma_start(out=view, in_=src),
                   ("ring", p), reads=extra, writes=keys)
        self.ptr = p + nu
        self.next_issue += 1
        return True

    def get(self, pid):
        while self._try_issue():
            pass
        assert pid in self.views, "ring deadlock: piece %d not issued" % pid
        return self.views[pid]

    def done(self, pid):
        p, nu = self.units_of[pid]
        for u in range(p, p + nu):
            self.free[u] = True
        while self._try_issue():
            pass


class _Stop(Exception):
    pass


def build_program(skip_mlp1=False, stage=99):
    nc = bass.Bass("TRN2", target_bir_lowering=False)

    def din(name, shape):
        return nc.dram_tensor(name, shape, F32, kind="ExternalInput").ap()

    def dout(name, shape):
        return nc.dram_tensor(name, shape, F32, kind="ExternalOutput").ap()

    xtok = din("xtok", [NT, D])
    swc = din("swc", [11, D])
    ck = din("ck", [4, 128, 256])
    cv = din("cv", [4, 128, 256])
    ln = din("ln", [4, D])
    w_in = din("w_in", [D, 3 * D])
    w_out = din("w_out", [D, D])
    w_qkv = din("w_qkv", [D, 2560])
    w_ao = din("w_ao", [D, D])
    qn = din("qn", [1, 64])
    kn = din("kn", [1, 64])
    sinks = din("sinks", [1, 32])
    w_up = din("w_up", [2, D, DFF])
    w_dn = din("w_dn", [2, DFF, D])
    cos_t = din("cos_t", [128, 10, 64])
    sin_t = din("sin_t", [128, 10, 64])
    masks = din("masks", [128, 3, 128])
    mask_s = din("mask_s", [32, 32])

    y_o = dout("y", [1056, D])
    ncv_o = dout("ncv", [10, D])
    kp_o = dout("kp", [128, 256])
    vp_o = dout("vp", [128, 256])
    ks_o = dout("ks", [4, 128, 256])
    vs_o = dout("vs", [4, 128, 256])

    with ExitStack() as st:
        def sb(name, shape, dt):
            return st.enter_context(nc.sbuf_tensor(name, shape, dt))

        X = sb("X", [128, 10, D], F32)
        HT = sb("HT", [128, 16 * NT], BF16)
        ACT_T = sb("ACT_T", [128, 8 * NT], BF16)
        RING = sb("RING", [128, NUNITS * UNIT], BF16)
        TMP = sb("TMP", [128, 3600], F32)
        ATT = sb("ATT", [128, 4352], F32)
        identf = sb("identf", [128, 128], F32)
        ident = sb("ident", [128, 128], BF16)
        ones64 = sb("ones64", [128, 64], BF16)
        maskb = sb("maskb", [128, 3, 128], BF16)
        mask01 = sb("mask01", [128, 3, 128], BF16)
        msb = sb("msb", [32, 32], BF16)
        COS = sb("COS", [128, 10, 64], F32)
        SIN = sb("SIN", [128, 10, 64], F32)
        SWT = sb("SWT", [128, 16, 11], F32)
        UK = sb("UK", [128, 16, 10], F32)
        SS = sb("SS", [128, 16], F32)
        GQ = sb("GQ", [128, 64], F32)
        GK = sb("GK", [128, 64], F32)
        SK = sb("SK", [128, 32], F32)
        SE0 = sb("SE0", [128, 32], F32)
        SE = sb("SE", [128, 16], F32)
        SM = sb("SM", [128, 8], F32)
        EPSB = sb("EPSB", [128, 1], F32)
        S4T = sb("S4T", [128, 8], F32)
        JUNK = sb("JUNK", [128, 256], BF16)
        ps = st.enter_context(nc.psum_tensor("ps", [128, 8, 512], F32))
        psb = ps.bitcast(BF16)

        HT3 = HT[:].rearrange("p (c n) -> p c n", n=NT)
        AT3 = ACT_T[:].rearrange("p (c n) -> p c n", n=NT)
        T0 = TMP[:, 0:1200]
        T1 = TMP[:, 1200:2400]
        T2 = TMP[:, 2400:3600]
        GB = TMP[:, 0:2048]
        KT2 = TMP[:].bitcast(BF16)[:, 0:4 * 1696].rearrange("p (k n) -> p k n", n=1696)
        ALLT = ["T0", "T1", "T1h", "T1s", "T1p", "T2", "T2s"]
        ATTb = ATT[:].bitcast(BF16)
        V = ATTb[:, 0:2560].rearrange("p (t n) -> p t n", n=256)
        Vc = ATTb[:, 2560:3584].rearrange("p (b n) -> p b n", n=256)
        XS = [ATT[:, 1792:2048], ATT[:, 2048:2304]]
        AB = [ATT[:, 2304:2560], ATT[:, 2560:2816]]
        BB = [ATT[:, 2816:3072], ATT[:, 3072:3328]]
        RD = ATT[:, 1792:2304]
        RDK = ["XS0", "XS1"]
        QRB = [ATTb[:, 6656:6912], ATTb[:, 6912:7168]]
        KDUP = ATTb[:, 7168:7680]
        KRBS = [ATT[:, 3840:4096], ATT[:, 4096:4352]]
        CKB = ATTb[:, 4608:5632].rearrange("p (b n) -> p b n", n=256)
        CKBK = ["AB0", "AB1"]
        X0b = X[:, 0, :].bitcast(BF16)
        PT = [X0b[:, 0:2048], X0b[:, 2048:4096]]
        PTK = [[("X", 0, 0), ("X", 0, 1)], [("X", 0, 2), ("X", 0, 3)]]
        HB_ATT = ([ATTb[:, 0:2048], ATTb[:, 2048:4096], ATTb[:, 4096:6144]],
                  [[("V", t) for t in range(8)], [("V", 8), ("V", 9), "VC", "XS0"], ["XS1", "AB0", "AB1", "BB0"]])
        HB_X0 = (PT, PTK)

        S = Sched(nc)
        W = WStream(S, RING[:])
        W.hold_keys = [("X", 9, 0)]

        def XK(t):
            return [("X", t, g) for g in range(4)]

        HK = [("H", t) for t in range(10)]

        def wv(src, c0, ncol):
            return src[:, c0:c0 + ncol].rearrange("(k p) n -> p k n", p=128)

        def wr(src, r0, nk, c0, ncol):
            return src[r0:r0 + nk * 128, c0:c0 + ncol].rearrange("(k p) n -> p k n", p=128)

        P_conv = []
        for gi in range(2):
            up = []
            for j in range(8):
                J = gi * 8 + j
                up.append((W.add(wv(w_in, 2048 + J * 128, 128), 1, 16, 128),
                           W.add(wv(w_in, 4096 + J * 128, 128), 1, 16, 128),
                           W.add(wv(w_in, J * 128, 128), 1, 16, 128)))
            dn = [W.add(wr(w_out, gi * 1024, 8, ng * 512, 512), 2, 8, 512) for ng in range(4)]
            P_conv.append((up, dn))

        def plan_mlp(l):
            out = []
            for gi in range(8):
                up = [W.add(wv(w_up[l], (gi * 8 + 2 * pi) * 128, 256), 2, 16, 256) for pi in range(4)]
                dn = [W.add(wr(w_dn[l], gi * 1024, 8, ng * 512, 512), 2, 8, 512) for ng in range(4)]
                out.append((up, dn))
            return out

        P_mlp0 = plan_mlp(0)
        P_k = W.add(wv(w_qkv, 2048, 256), 2, 16, 256)
        P_v = W.add(wv(w_qkv, 2304, 256), 2, 16, 256)
        P_att = []
        for g in range(4):
            qp = [W.add(wv(w_qkv, g * 512 + qh * 256, 256), 2, 16, 256) for qh in range(2)]
            ao = [W.add(wr(w_ao, g * 512, 4, hf * 1024, 1024), 2, 4, 1024) for hf in range(2)]
            P_att.append((qp, ao))
        P_mlp1 = plan_mlp(1)

        S.dma("sp", lambda e: e.dma_start(out=GB, in_=ln[0:1, :].partition_broadcast(128)), "gb", writes=["T0", "T1", "T1h"])
        for t in range(10):
            rows = 128 if t < 9 else 34
            S.dma("sp", lambda e, t=t, rows=rows: e.dma_start(out=X[0:rows, t, :], in_=xtok[t * 128:t * 128 + rows, :]),
                  ("x", t), writes=XK(t))
        S.dma("sp", lambda e: e.dma_start(out=X[64:75, 9, :], in_=swc), "swc", writes=["X9hi"])
        S.dma("sp", lambda e: e.dma_start(out=COS[:], in_=cos_t), "cos", writes=["COS"])
        S.dma("sp", lambda e: e.dma_start(out=SIN[:], in_=sin_t), "sin", writes=["SIN"])
        S.dma("sp", lambda e: e.dma_start(out=GQ[:], in_=qn.partition_broadcast(128)), "gq", writes=["GQ"])
        S.dma("sp", lambda e: e.dma_start(out=GK[:], in_=kn.partition_broadcast(128)), "gk", writes=["GK"])
        S.dma("sp", lambda e: e.dma_start(out=SK[:], in_=sinks.partition_broadcast(128)), "sk", writes=["SK"])
        S.dma("pool", lambda e: e.dma_start(out=maskb[:], in_=masks), "maskb", writes=["maskb"])
        S.dma("pool", lambda e: e.dma_start(out=msb[:], in_=mask_s), "msb", writes=["msb"])
        S.op("pool", lambda e: e.memset(identf[:], 1.0), writes=["identf"])
        S.op("pool", lambda e: e.affine_select(out=identf[:], in_=identf[:], pattern=[[-1, 128]],
                                               compare_op=ALU.is_equal, fill=0.0, base=0, channel_multiplier=1),
             reads=["identf"], writes=["identf"])
        S.op("dve", lambda e: e.tensor_copy(out=ident[:], in_=identf[:]), reads=["identf"], writes=["ident"])
        S.op("dve", lambda e: e.tensor_scalar(out=mask01[:], in0=maskb[:], scalar1=0.0, scalar2=None, op0=ALU.is_equal),
             reads=["maskb"], writes=["mask01"])
        S.op("dve", lambda e: e.memset(ones64[:], 1.0), writes=["ones64"])
        S.op("dve", lambda e: e.memset(EPSB[:], EPS), writes=["EPSB"])

        norm_cnt = [0]

        class NormPipe:
            def __init__(self, li, tiles, rows9, hbsel, gb_loaded=False):
                Hb, HbK = hbsel
                self.overlapped = False
                if not gb_loaded:
                    S.dma("sp", lambda e: e.dma_start(out=GB, in_=ln[li:li + 1, :].partition_broadcast(128)),
                          "gb", writes=["T0", "T1", "T1h"])
                self.info = []
                for t in tiles:
                    i = norm_cnt[0]
                    norm_cnt[0] += 1
                    self.info.append((t, 128 if t < 9 else rows9, Hb[i % len(Hb)], HbK[i % len(Hb)], i % 16, (i % 2) * 2))
                self.nbuf = len(Hb)
                self.idx = {t: k for k, t in enumerate(tiles)}
                self.na = 0
                self.nb = 0

            def _a(self, t, rows, hb, hk, col, b0):
                S.op("act", lambda e: e.activation(
                    out=hb[0:rows, :], in_=X[0:rows, t, :], func=AF.Square, accum_out=SS[0:rows, col:col + 1]),
                    reads=XK(t), writes=hk + [("SS", col)])
                S.op("act", lambda e: e.activation(
                    out=SS[0:rows, col:col + 1], in_=SS[0:rows, col:col + 1], func=AF.Sqrt,
                    bias=EPSB[0:rows, :], scale=1.0 / D), reads=[("SS", col), "EPSB"], writes=[("SS", col)])
                S.op("dve", lambda e: e.reciprocal(out=SS[0:rows, col:col + 1], in_=SS[0:rows, col:col + 1]),
                     reads=[("SS", col)], writes=[("SS", col)])
                S.op("dve", lambda e: e.scalar_tensor_tensor(
                    out=hb[0:rows, :], in0=X[0:rows, t, :], scalar=SS[0:rows, col:col + 1], in1=GB[0:rows, :],
                    op0=ALU.mult, op1=ALU.mult), reads=XK(t) + [("SS", col), "T0", "T1", "T1h"], writes=hk)

            def _b(self, t, rows, hb, hk, col, b0):
                def tr(e):
                    for c in range(16):
                        ins = e.transpose(out=psb[:, b0 + c // 8, (c % 8) * 128:(c % 8) * 128 + rows],
                                          in_=hb[0:rows, c * 128:(c + 1) * 128], identity=ident[0:rows, 0:rows])
                    return ins
                S.op("pe", tr, reads=hk + ["ident"], writes=[("ps", b0), ("ps", b0 + 1)])
                for half in range(2):
                    src = psb[:, b0 + half, :].rearrange("p (c n) -> p c n", n=128)[:, :, 0:rows]
                    dst = HT3[:, half * 8:(half + 1) * 8, t * 128:t * 128 + rows]
                    if half == 0 or self.overlapped:
                        S.op("act", lambda e, src=src, dst=dst: e.activation(out=dst, in_=src, func=AF.Copy),
                             reads=[("ps", b0 + half)], writes=[("H", t, half)])
                    else:
                        S.op("dve", lambda e, src=src, dst=dst: e.tensor_copy(out=dst, in_=src),
                             reads=[("ps", b0 + half)], writes=[("H", t, half)])

            def tile_ready(self, t):
                if t not in self.idx:
                    return
                assert self.idx[t] == self.na
                self.overlapped = True
                if self.nbuf >= 3:
                    self._a(*self.info[self.na])
                    self.na += 1
                    if self.na >= 3:
                        self._b(*self.info[self.nb])
                        self.nb += 1
                else:
                    if self.na >= 2:
                        self._b(*self.info[self.nb])
                        self.nb += 1
                    self._a(*self.info[self.na])
                    self.na += 1

            def finish(self):
                self.overlapped = False
                n = len(self.info)
                while self.nb < n:
                    if self.na < n and (self.na - self.nb) < 2:
                        self._a(*self.info[self.na])
                        self.na += 1
                        continue
                    self._b(*self.info[self.nb])
                    self.nb += 1

        def HKall(tiles):
            return [("H", t, h) for t in tiles for h in range(2)]

        def fm_mm(Wv, c_lo, bank0, tgs):
            def f(e):
                for k in range(16):
                    for gi, (c0, c1) in enumerate(tgs):
                        ins = e.matmul(ps[:, bank0 + gi, 0:c1 - c0], lhsT=Wv[:, k, c_lo:c_lo + 128],
                                       rhs=HT3[:, k, c0:c1], start=(k == 0), stop=(k == 15))
                return ins
            return f

        def flat(bank0, n):
            return ps[:, bank0:bank0 + 3, :].rearrange("p b n -> p (b n)")[:, 0:n]

        def BK(bank0):
            return [("ps", bank0), ("ps", bank0 + 1), ("ps", bank0 + 2)]

        dn_cnt = [0]

        def down_phase(pieces, nk, tiles, colbase, rows9, akeys, hook=None):
            def one(Wv, wk, ngi, t):
                rows = 128 if t < 9 else rows9
                lc = t * 128 - colbase
                bank = 6 + (dn_cnt[0] % 2)
                dn_cnt[0] += 1
                xg = ngi

                def mm(e):
                    for c in range(nk):
                        ins = e.matmul(ps[0:rows, bank, :], lhsT=AT3[:, c, lc:lc + rows],
                                       rhs=Wv[:, c, :], start=(c == 0), stop=(c == nk - 1))
                    return ins
                S.op("pe", mm, reads=wk + akeys, writes=[("ps", bank)])
                S.op("dve", lambda e: e.tensor_tensor(
                    out=X[0:rows, t, xg * 512:(xg + 1) * 512], in0=X[0:rows, t, xg * 512:(xg + 1) * 512],
                    in1=ps[0:rows, bank, :], op=ALU.add), reads=[("ps", bank), ("X", t, xg)], writes=[("X", t, xg)])

            first = pieces if hook is None else pieces[:-2]
            for ngi, pid in enumerate(first):
                Wv, wk = W.get(pid)
                for t in tiles:
                    one(Wv, wk, ngi, t)
                W.done(pid)
            if hook is not None:
                n0 = len(pieces) - 2
                Wa_, wka_ = W.get(pieces[n0])
                Wb_, wkb_ = W.get(pieces[n0 + 1])
                for t in tiles:
                    one(Wa_, wka_, n0, t)
                    one(Wb_, wkb_, n0 + 1, t)
                    hook(t)
                W.done(pieces[n0])
                W.done(pieces[n0 + 1])

        def ckpt(k):
            if k > stage:
                raise _Stop()

        def record():
            TG0 = [(0, 512), (512, 1024), (1024, NT)]
            ALL10 = list(range(10))
            NormPipe(0, ALL10, 34, HB_ATT, gb_loaded=True).finish()
            def swt_mm(e):
                for c in range(16):
                    ins = e.matmul(ps[:, 0, c * 11:(c + 1) * 11], lhsT=X[64:75, 9, c * 128:(c + 1) * 128],
                                   rhs=identf[64:75, 64:75], start=True, stop=True)
                return ins
            S.op("pe", swt_mm, reads=["X9hi", "identf"], writes=[("ps", 0)])
            S.op("act", lambda e: e.activation(out=SWT[:].rearrange("p c n -> p (c n)"), in_=ps[:, 0, 0:176], func=AF.Copy),
                 reads=[("ps", 0)], writes=["SWT"])
            ckpt(2)
            HA = HKall(ALL10)
            ubuf = T1
            ubs = T1[:, 1154:1194].rearrange("p (b n) -> p b n", n=10)
            t1 = T2
            csb = T0
            setc = [0]

            def nextset():
                s = (setc[0] % 2) * 3
                setc[0] += 1
                return s

            for gi in range(2):
                up, dn = P_conv[gi]
                for j in range(8):
                    J = gi * 8 + j
                    pc, pv, pb = up[j]
                    w0 = SWT[:, J, 8:9]
                    w1 = SWT[:, J, 9:10]
                    w2 = SWT[:, J, 10:11]
                    Wc, wkc = W.get(pc)
                    sA = nextset()
                    S.op("pe", fm_mm(Wc, 0, sA, TG0), reads=wkc + HA, writes=BK(sA))
                    W.done(pc)
                    S.op("act", lambda e, sA=sA: e.activation(out=csb[:, 0:NT], in_=flat(sA, NT), func=AF.Copy),
                         reads=BK(sA), writes=["T0"])
                    Wvv, wkv = W.get(pv)
                    sB = nextset()
                    S.op("pe", fm_mm(Wvv, 0, sB, TG0), reads=wkv + HA, writes=BK(sB))
                    W.done(pv)
                    S.op("dve", lambda e, sB=sB: e.tensor_tensor(out=ubuf[:, 2:1154], in0=csb[:, 0:1152], in1=flat(sB, NT)[:, 0:1152], op=ALU.mult),
                         reads=BK(sB) + ["T0"], writes=["T1"])
                    S.op("dve", lambda e, sB=sB: e.tensor_tensor(
                        out=ubs[:, :, 2:10], in0=csb[:, 1152:1184].rearrange("p (b n) -> p b n", n=8),
                        in1=flat(sB, NT)[:, 1152:1184].rearrange("p (b n) -> p b n", n=8), op=ALU.mult),
                        reads=BK(sB) + ["T0"], writes=["T1s"])
                    S.op("dve", lambda e, sB=sB: e.tensor_tensor(out=ubuf[:, 0:2], in0=csb[:, 1184:1186], in1=flat(sB, NT)[:, 1184:1186], op=ALU.mult),
                         reads=BK(sB) + ["T0"], writes=["T1h"])
                    S.op("pool", lambda e, J=J: e.tensor_copy(out=ubs[:, :, 0:2], in_=SWT[:, J, 0:8].rearrange("p (b n) -> p b n", n=2)),
                         reads=["SWT"], writes=["T1p"])
                    S.op("pool", lambda e, J=J: e.tensor_copy(out=UK[:, J, 0:2], in_=ubuf[:, 1152:1154]), reads=["T1"], writes=[("UK", J, 0)])
                    S.op("pool", lambda e, J=J: e.tensor_copy(out=UK[:, J, 2:10].rearrange("p (b n) -> p b n", n=2), in_=ubs[:, :, 8:10]),
                         reads=["T1s"], writes=[("UK", J, 1)])
                    UR = ["T1", "T1h"]
                    S.op("dve", lambda e, w0=w0: e.tensor_scalar(out=t1[:, 0:1152], in0=ubuf[:, 0:1152], scalar1=w0, scalar2=None, op0=ALU.mult),
                         reads=UR + ["SWT"], writes=["T2"])
                    S.op("dve", lambda e, w1=w1: e.scalar_tensor_tensor(out=t1[:, 0:1152], in0=ubuf[:, 1:1153], scalar=w1, in1=t1[:, 0:1152], op0=ALU.mult, op1=ALU.add),
                         reads=UR + ["T2"], writes=["T2"])
                    S.op("dve", lambda e, w2=w2: e.scalar_tensor_tensor(out=t1[:, 0:1152], in0=ubuf[:, 2:1154], scalar=w2, in1=t1[:, 0:1152], op0=ALU.mult, op1=ALU.add),
                         reads=UR + ["T2"], writes=["T2"])
                    t1s = t1[:, 1152:1184].rearrange("p (b n) -> p b n", n=8)
                    USR = ["T1s", "T1p"]
                    S.op("dve", lambda e, w0=w0, t1s=t1s: e.tensor_scalar(out=t1s, in0=ubs[:, :, 0:8], scalar1=w0, scalar2=None, op0=ALU.mult),
                         reads=USR + ["SWT"], writes=["T2s"])
                    S.op("dve", lambda e, w1=w1, t1s=t1s: e.scalar_tensor_tensor(out=t1s, in0=ubs[:, :, 1:9], scalar=w1, in1=t1s, op0=ALU.mult, op1=ALU.add),
                         reads=USR + ["T2s"], writes=["T2s"])
                    S.op("dve", lambda e, w2=w2, t1s=t1s: e.scalar_tensor_tensor(out=t1s, in0=ubs[:, :, 2:10], scalar=w2, in1=t1s, op0=ALU.mult, op1=ALU.add),
                         reads=USR + ["T2s"], writes=["T2s"])
                    Wb, wkb = W.get(pb)
                    sC = nextset()
                    S.op("pe", fm_mm(Wb, 0, sC, TG0), reads=wkb + HA, writes=BK(sC))
                    W.done(pb)
                    S.op("dve", lambda e, sC=sC, j=j: e.tensor_tensor(out=AT3[:, j, 0:1184], in0=flat(sC, NT)[:, 0:1184], in1=t1[:, 0:1184], op=ALU.mult),
                         reads=BK(sC) + ["T2", "T2s"], writes=[("A", j)])
                if gi == 0:
                    down_phase(dn, 8, ALL10, 0, 32, [("A", c) for c in range(8)])
                else:
                    def uk_mm(e):
                        for c in range(16):
                            ins = e.matmul(ps[0:10, c // 4, (c % 4) * 128:(c % 4 + 1) * 128], lhsT=UK[:, c, :], rhs=identf[:, :],
                                           start=True, stop=True)
                        return ins
                    S.op("pe", uk_mm, reads=[("UK", J, h) for J in range(16) for h in range(2)] + ["identf"],
                         writes=[("ps", b) for b in range(4)])
                    S.op("act", lambda e: e.activation(out=TMP[0:10, 0:2048], in_=ps[0:10, 0:4, :].rearrange("p b n -> p (b n)"), func=AF.Copy),
                         reads=[("ps", b) for b in range(4)], writes=["T0", "T1", "T1h"])
                    S.dma("sp", lambda e: e.dma_start(out=ncv_o, in_=TMP[0:10, 0:2048]), "ncv", reads=["T0", "T1", "T1h"], is_output=True)


                    NP1 = NormPipe(1, ALL10, 32, HB_ATT)
                    down_phase(dn, 8, ALL10, 0, 32, [("A", c) for c in range(8)], hook=NP1.tile_ready)
            ckpt(3)

            def mlp(NP, plan, tiles, colbase, rows9, tgs, next_norm=None):
                NP.finish()
                NPn = None
                HA_ = HKall(tiles)
                ncols = tgs[-1][1] - tgs[0][0]
                rcnt = 0
                for gi in range(8):
                    up, dn = plan[gi]
                    for pi in range(4):
                        Wv, wk = W.get(up[pi])
                        for cl in range(2):
                            c = pi * 2 + cl
                            sA = nextset()
                            S.op("pe", fm_mm(Wv, cl * 128, sA, tgs), reads=wk + HA_, writes=BK(sA))
                            rt = (T0, T1)[rcnt % 2]
                            rk = (["T0"], ["T1", "T1h", "T1s", "T1p"])[rcnt % 2]
                            rcnt += 1
                            S.op("act", lambda e, sA=sA, rt=rt: e.activation(out=rt[:, 0:ncols], in_=flat(sA, ncols), func=AF.Relu),
                                 reads=BK(sA), writes=rk)
                            S.op("pool", lambda e, rt=rt, c=c: e.tensor_tensor(out=AT3[:, c, 0:ncols], in0=rt[:, 0:ncols], in1=rt[:, 0:ncols], op=ALU.mult),
                                 reads=rk, writes=[("A", c)])
                        W.done(up[pi])
                    if gi == 7 and next_norm is not None:
                        NPn = NormPipe(*next_norm)
                        down_phase(dn, 8, tiles, colbase, rows9, [("A", c) for c in range(8)], hook=NPn.tile_ready)
                    else:
                        down_phase(dn, 8, tiles, colbase, rows9, [("A", c) for c in range(8)])
                return NPn

            ckpt(4)
            NP2 = mlp(NP1, P_mlp0, ALL10, 0, 32, TG0, next_norm=(2, ALL10, 32, HB_ATT))
            ckpt(5)

            NP2.finish()
            HA1 = HKall(ALL10)

            S.op("dve", lambda e: e.tensor_reduce(out=SM[:, 0:1], in_=GQ[:], axis=AX.X, op=ALU.max, apply_absolute_value=True), reads=["GQ"], writes=["SM0"])
            S.op("dve", lambda e: e.tensor_reduce(out=SM[:, 1:2], in_=GK[:], axis=AX.X, op=ALU.max, apply_absolute_value=True), reads=["GK"], writes=["SM1"])
            S.op("dve", lambda e: e.tensor_reduce(out=SM[:, 2:3], in_=SK[:], axis=AX.X, op=ALU.max), reads=["SK"], writes=["SM2"])
            S.op("dve", lambda e: e.tensor_tensor(out=SM[:, 3:4], in0=SM[:, 0:1], in1=SM[:, 1:2], op=ALU.mult), reads=["SM0", "SM1"], writes=["SM3"])
            S.op("dve", lambda e: e.scalar_tensor_tensor(out=SM[:, 4:5], in0=SM[:, 3:4], scalar=8.0, in1=SM[:, 2:3], op0=ALU.mult, op1=ALU.max),
                 reads=["SM3", "SM2"], writes=["SM4"])
            S.op("dve", lambda e: e.tensor_scalar(out=SM[:, 5:6], in0=SM[:, 4:5], scalar1=-1.0, scalar2=None, op0=ALU.mult), reads=["SM4"], writes=["NEGM"])
            NEGM = SM[:, 5:6]
            S.op("act", lambda e: e.activation(out=SE0[:], in_=SK[:], func=AF.Exp, bias=NEGM, scale=1.0), reads=["SK", "NEGM"], writes=["SE0"])
            SE0v = SE0[:].rearrange("p (a b) -> p a b", b=2)
            S.op("dve", lambda e: e.tensor_copy(out=SE[0:64, :], in_=SE0v[0:64, :, 0]), reads=["SE0"], writes=["SEa"])
            S.op("dve", lambda e: e.tensor_copy(out=SE[64:128, :], in_=SE0v[64:128, :, 1]), reads=["SE0"], writes=["SEb"])

            ckpt(5.1)
            S.dma("pool", lambda e: e.dma_start(out=CKB, in_=ck.rearrange("b k n -> k b n")), "ckb", writes=CKBK)
            S.dma("pool", lambda e: e.dma_start(out=Vc, in_=cv.rearrange("b k n -> k b n")), "vc", writes=["VC"])
            S.dma("sp", lambda e: e.dma_start(out=ks_o[:, 0:120, :], in_=ck[:, 8:128, :]), "ksw", is_output=True)
            S.dma("sp", lambda e: e.dma_start(out=vs_o[:, 0:120, :], in_=cv[:, 8:128, :]), "vsw", is_output=True)
            ckpt(5.2)
            kd4 = KDUP.rearrange("p (k d n) -> p k d n", d=2, n=64)
            trc = [0]

            def ktrans(src_rows, rows, dstcols):
                bank = 4 + (trc[0] % 2)
                trc[0] += 1

                def tr(e):
                    for kv in range(4):
                        ins = e.transpose(out=psb[:, bank, kv * 128:kv * 128 + rows], in_=KDUP[0:rows, kv * 128:(kv + 1) * 128],
                                          identity=ident[0:rows, 0:rows])
                    return ins
                S.op("pe", tr, reads=["KDUP", "ident"], writes=[("ps", bank)])
                S.op("act", lambda e: e.activation(
                    out=KT2[:, :, dstcols:dstcols + rows],
                    in_=psb[:, bank, 0:512].rearrange("p (k n) -> p k n", n=128)[:, :, 0:rows], func=AF.Copy),
                    reads=[("ps", bank)] + ALLT, writes=[("KT2", dstcols)])

            for b in range(4):
                S.op("dve", lambda e, b=b: e.tensor_copy(
                    out=kd4, in_=CKB[:, b, :].rearrange("p (k n) -> p k n", n=64).unsqueeze(2).broadcast_to([128, 4, 2, 64])),
                    reads=CKBK, writes=["KDUP"])
                ktrans(None, 128, 1184 + b * 128)

            ckpt(5.3)
            def build_tables(G, gkey):
                S.op("dve", lambda e: e.tensor_tensor(out=COS[:], in0=COS[:], in1=G[:].unsqueeze(1).broadcast_to([128, 10, 64]), op=ALU.mult),
                     reads=["COS", gkey], writes=["COS"])
                S.op("dve", lambda e: e.tensor_tensor(out=SIN[:, :, 0:32], in0=SIN[:, :, 0:32],
                                                      in1=G[:, 32:64].unsqueeze(1).broadcast_to([128, 10, 32]), op=ALU.mult),
                     reads=["SIN", gkey], writes=["SIN"])
                S.op("dve", lambda e: e.tensor_tensor(out=SIN[:, :, 32:64], in0=SIN[:, :, 32:64],
                                                      in1=G[:, 0:32].unsqueeze(1).broadcast_to([128, 10, 32]), op=ALU.mult),
                     reads=["SIN", gkey], writes=["SIN"])

            def qk_chain(bank, rows, t, cb, out_ap, out_keys):
                xs = XS[cb][0:rows, :]
                xs3 = xs.rearrange("p (h d) -> p h d", d=64)
                a3 = AB[cb][0:rows, :].rearrange("p (h d) -> p h d", d=64)
                b3 = BB[cb][0:rows, :].rearrange("p (h d) -> p h d", d=64)
                s4 = S4T[0:rows, cb * 4:cb * 4 + 4]
                xk, ak, bk, sk_ = "XS%d" % cb, "AB%d" % cb, "BB%d" % cb, "S4%d" % cb
                S.op("act", lambda e: e.activation(out=xs, in_=ps[0:rows, bank, 0:256], func=AF.Copy), reads=[("ps", bank)], writes=[xk])
                for h in range(4):
                    S.op("act", lambda e, h=h: e.activation(out=JUNK[0:rows, h * 64:(h + 1) * 64], in_=xs[:, h * 64:(h + 1) * 64],
                                                            func=AF.Square, accum_out=s4[:, h:h + 1]),
                         reads=[xk], writes=[("J", h), (sk_, h)])
                S.op("act", lambda e: e.activation(out=s4, in_=s4, func=AF.Sqrt, bias=EPSB[0:rows, :], scale=1.0 / 64),
                     reads=[(sk_, h) for h in range(4)] + ["EPSB"], writes=[sk_])
                S.op("dve", lambda e: e.tensor_tensor(out=a3, in0=xs3, in1=COS[0:rows, t, :].unsqueeze(1).broadcast_to([rows, 4, 64]), op=ALU.mult),
                     reads=[xk, "COS"], writes=[ak])
                S.op("dve", lambda e: e.tensor_tensor(out=b3[:, :, 0:32], in0=xs3[:, :, 32:64],
                                                      in1=SIN[0:rows, t, 0:32].unsqueeze(1).broadcast_to([rows, 4, 32]), op=ALU.mult),
                     reads=[xk, "SIN"], writes=[bk + "a"])
                S.op("dve", lambda e: e.tensor_tensor(out=b3[:, :, 32:64], in0=xs3[:, :, 0:32],
                                                      in1=SIN[0:rows, t, 32:64].unsqueeze(1).broadcast_to([rows, 4, 32]), op=ALU.mult),
                     reads=[xk, "SIN"], writes=[bk + "b"])
                S.op("dve", lambda e: e.reciprocal(out=s4, in_=s4), reads=[sk_], writes=[sk_])
                S.op("pool", lambda e: e.tensor_tensor(out=AB[cb][0:rows, :], in0=AB[cb][0:rows, :], in1=BB[cb][0:rows, :], op=ALU.add),
                     reads=[ak, bk + "a", bk + "b"], writes=[ak])
                S.op("pool", lambda e: e.tensor_tensor(out=out_ap, in0=a3, in1=s4.unsqueeze(2).broadcast_to([rows, 4, 64]), op=ALU.mult),
                     reads=[ak, sk_], writes=out_keys)

            pj = [0]

            def tm_proj(Wv, wk, t, rows):
                bank = 6 + (pj[0] % 2)
                pj[0] += 1

                def mm(e):
                    for k in range(16):
                        ins = e.matmul(ps[0:rows, bank, 0:256], lhsT=HT3[:, k, t * 128:t * 128 + rows], rhs=Wv[:, k, :],
                                       start=(k == 0), stop=(k == 15))
                    return ins
                S.op("pe", mm, reads=wk + [("H", t, 0), ("H", t, 1)], writes=[("ps", bank)])
                return bank

            def proj_loop(Wv, wk, tiles, post_a, post_b=None):
                n = len(tiles)
                rws = [128 if t < 9 else 32 for t in tiles]
                banks = [None] * n
                banks[0] = tm_proj(Wv, wk, tiles[0], rws[0])
                if n > 1:
                    banks[1] = tm_proj(Wv, wk, tiles[1], rws[1])
                post_a(0, tiles[0], rws[0], banks[0])
                for i in range(n):
                    if i + 2 < n:
                        banks[i + 2] = tm_proj(Wv, wk, tiles[i + 2], rws[i + 2])
                    if i + 1 < n:
                        post_a(i + 1, tiles[i + 1], rws[i + 1], banks[i + 1])
                    if post_b is not None:
                        post_b(i, tiles[i], rws[i])

            build_tables(GK, "GK")
            Wk_, wkk = W.get(P_k)

            def k_post_a(i, t, rows, bank):
                KRB = KRBS[i % 2]
                qk_chain(bank, rows, t, i % 2, KRB[0:rows, :].rearrange("p (h d) -> p h d", d=64), ["KRB%d" % (i % 2)])

            def k_post_b(i, t, rows):
                KRB = KRBS[i % 2]
                kk = "KRB%d" % (i % 2)
                S.op("act", lambda e: e.activation(
                    out=kd4[0:rows], in_=KRB[0:rows, :].rearrange("p (k n) -> p k n", n=64).unsqueeze(2).broadcast_to([rows, 4, 2, 64]),
                    func=AF.Copy), reads=[kk], writes=["KDUP"])
                if t == 8:
                    S.dma("sp", lambda e: e.dma_start(out=kp_o, in_=KRB[:, :]), "kp", reads=[kk], is_output=True)
                if t == 9:
                    for b in range(4):
                        S.dma("sp", lambda e, b=b: e.dma_start(out=ks_o[b, 120:128, :], in_=KRB[b * 8:(b + 1) * 8, :]),
                              ("ksn", b), reads=[kk], is_output=True)
                ktrans(None, rows, t * 128)
            proj_loop(Wk_, wkk, ALL10, k_post_a, k_post_b)
            W.done(P_k)
            ckpt(5.4)
            S.dma("sp", lambda e: e.dma_start(out=COS[:], in_=cos_t), "cos", writes=["COS"])
            S.dma("sp", lambda e: e.dma_start(out=SIN[:], in_=sin_t), "sin", writes=["SIN"])
            build_tables(GQ, "GQ")
            Wv_, wkv_ = W.get(P_v)

            def v_post(i, t, rows, bank):
                S.op("act", lambda e: e.activation(out=V[0:rows, t, :], in_=ps[0:rows, bank, 0:256], func=AF.Copy),
                     reads=[("ps", bank)], writes=[("V", t)])
                if t >= 8:
                    S.op("dve", lambda e: e.tensor_copy(out=KRBS[0][0:rows, :], in_=ps[0:rows, bank, 0:256]),
                         reads=[("ps", bank), ("V", t)], writes=["KRB0"])
                    if t == 8:
                        S.dma("sp", lambda e: e.dma_start(out=vp_o, in_=KRBS[0][:, :]), "vp", reads=["KRB0"], is_output=True)
                    else:
                        for b in range(4):
                            S.dma("sp", lambda e, b=b: e.dma_start(out=vs_o[b, 120:128, :], in_=KRBS[0][b * 8:(b + 1) * 8, :]),
                                  ("vsn", b), reads=["KRB0"], is_output=True)
            proj_loop(Wv_, wkv_, ALL10, v_post)
            W.done(P_v)

            ckpt(6)
            QT3 = AT3[:, 0:4, :]
            OT3 = AT3[:, 4:8, :]
            KT2all = [("KT2", c) for c in [t * 128 for t in range(10)] + [1184 + b * 128 for b in range(4)]] + ALLT
            ptc = [0]
            TILES1 = list(range(1, 10))
            for g in range(4):
                qp, ao = P_att[g]
                for qh in range(2):
                    Wq, wkq = W.get(qp[qh])

                    def q_post_a(i, t, rows, bank):
                        cb = i % 2
                        qk_chain(bank, rows, t, cb, QRB[cb][0:rows, :].rearrange("p (h d) -> p h d", d=64), ["QRB%d" % cb])

                    def q_post_b(i, t, rows, qh=qh):
                        cb = i % 2
                        qb = QRB[cb]
                        qbk = "QRB%d" % cb
                        tb = 4 + (trc[0] % 2)
                        trc[0] += 1

                        def tr(e):
                            for pr in range(2):
                                ins = e.transpose(out=psb[:, tb, pr * 128:pr * 128 + rows], in_=qb[0:rows, pr * 128:(pr + 1) * 128],
                                                  identity=ident[0:rows, 0:rows])
                            return ins
                        S.op("pe", tr, reads=[qbk, "ident"], writes=[("ps", tb)])
                        lc = t * 128 - 128
                        S.op("act", lambda e: e.activation(
                            out=QT3[:, qh * 2:qh * 2 + 2, lc:lc + rows],
                            in_=psb[:, tb, 0:256].rearrange("p (k n) -> p k n", n=128)[:, :, 0:rows], func=AF.Copy),
                            reads=[("ps", tb)], writes=[("A", qh * 2), ("A", qh * 2 + 1)])
                    proj_loop(Wq, wkq, TILES1, q_post_a, q_post_b)
                    W.done(qp[qh])
                QK_ = [("A", c) for c in range(4)]
                OK_ = [("A", c) for c in range(4, 8)]
                bufs = {}
                for n in range(1, 9):
                    bufs[n] = ptc[0] % 2
                    ptc[0] += 1

                def score_exp(n, kbi, g=g):
                    buf = bufs[n]
                    P4 = PT[buf].rearrange("p (k r n) -> p k r n", k=2, r=2)
                    lc = (n - 1) * 128
                    kt = (n - 1, n)[kbi]
                    mi = kbi if (kbi == 1 or n > 1) else 2

                    def st(e):
                        for par in range(2):
                            bank = kbi * 2 + par
                            ph = slice(par * 64, par * 64 + 64)
                            ins = e.matmul(ps[:, bank, :], lhsT=KT2[ph, g, kt * 128:(kt + 1) * 128], rhs=QT3[ph, :, lc:lc + 128],
                                           start=True, stop=True)
                        return ins
                    S.op("pe", st, reads=KT2all + QK_, writes=[("ps", kbi * 2), ("ps", kbi * 2 + 1)])
                    S.op("act", lambda e: e.activation(
                        out=P4[:, kbi], in_=ps[:, kbi * 2:kbi * 2 + 2, :], func=AF.Exp, bias=NEGM, scale=0.125),
                        reads=[("ps", kbi * 2), ("ps", kbi * 2 + 1), "NEGM"], writes=[PTK[buf][kbi]])
                    pm = P4[:, kbi].rearrange("p r (a q) -> p (r a) q", q=128)
                    S.op("pool" if kbi == 0 else "dve", lambda e: e.tensor_tensor(
                        out=pm, in0=pm, in1=mask01[:, mi, :].unsqueeze(1).broadcast_to([128, 8, 128]), op=ALU.mult),
                        reads=[PTK[buf][kbi], "mask01"], writes=[PTK[buf][kbi]])

                def pv_norm(n, g=g):
                    buf = bufs[n]
                    P4 = PT[buf].rearrange("p (k r n) -> p k r n", k=2, r=2)
                    lc = (n - 1) * 128
                    bo = 4 + 2 * (n % 2)
                    bd = bo + 1

                    def pv(e):
                        for par in range(2):
                            ph = slice(par * 64, par * 64 + 64)
                            for kbi, kt in enumerate((n - 1, n)):
                                e.matmul(ps[ph, bo, :], lhsT=V[:, kt, g * 64:(g + 1) * 64], rhs=P4[:, kbi, par, :],
                                         start=(kbi == 0), stop=(kbi == 1), tile_position=(0, par * 64))
                        for par in range(2):
                            ph = slice(par * 64, par * 64 + 64)
                            for kbi in range(2):
                                ins = e.matmul(ps[ph, bd, :], lhsT=ones64[:, :], rhs=P4[:, kbi, par, :],
                                               start=(kbi == 0), stop=(kbi == 1), tile_position=(0, par * 64))
                        return ins
                    S.op("pe", pv, reads=PTK[buf] + [("V", n - 1), ("V", n), "ones64"], writes=[("ps", bo), ("ps", bd)])

                RDS = [(RD, RDK), (ATT[:, 2304:2816], ["AB0", "AB1"])]

                def norm_add(n, g=g):
                    rd, rdk = RDS[n % 2]
                    bd = 4 + 2 * (n % 2) + 1
                    S.op("dve", lambda e: e.tensor_tensor(
                        out=rd.rearrange("p (a q) -> p a q", q=128), in0=ps[:, bd, :].rearrange("p (a q) -> p a q", q=128),
                        in1=SE[:, 4 * g:4 * g + 4].unsqueeze(2).broadcast_to([128, 4, 128]), op=ALU.add),
                        reads=[("ps", bd), "SEa", "SEb"], writes=rdk)

                def norm_fin(n, g=g):
                    rd, rdk = RDS[n % 2]
                    lc = (n - 1) * 128
                    bo = 4 + 2 * (n % 2)
                    S.op("act", lambda e: e.activation(out=rd, in_=rd, func=AF.Ln), reads=rdk, writes=rdk)
                    S.op("act", lambda e: e.activation(out=rd, in_=rd, func=AF.Exp, scale=-1.0), reads=rdk, writes=rdk)
                    S.op("dve", lambda e: e.tensor_tensor(
                        out=OT3[:, :, lc:lc + 128], in0=ps[:, bo, :].rearrange("p (a q) -> p a q", q=128),
                        in1=rd.rearrange("p (a q) -> p a q", q=128), op=ALU.mult),
                        reads=[("ps", bo)] + rdk, writes=OK_)

                for i in range(1, 11):
                    if i <= 8:
                        score_exp(i, 0)
                        score_exp(i, 1)
                    if 1 <= i - 1 <= 8:
                        pv_norm(i - 1)
                        norm_add(i - 1)
                    if 1 <= i - 2 <= 8:
                        norm_fin(i - 2)
                buf = ptc[0] % 2
                ptc[0] += 1
                PTc = PT[buf][:, 0:256].rearrange("p (r n) -> p r n", r=2)
                PTn = PT[buf][0:32, 256:512].rearrange("p (r n) -> p r n", r=2)

                def s_mm(e, g=g):
                    for par in range(2):
                        ph = slice(par * 64, par * 64 + 64)
                        e.matmul(ps[:, par, 0:128], lhsT=ident[:, :],
                                 rhs=maskb[:, 0, 0:8].unsqueeze(1).broadcast_to([128, 16, 8]), start=True, stop=False)
                        for b in range(4):
                            e.matmul(ps[:, par, b * 32:(b + 1) * 32], lhsT=KT2[ph, g, 1184 + b * 128:1184 + (b + 1) * 128],
                                     rhs=QT3[ph, :, 1024 + b * 8:1024 + (b + 1) * 8], start=False, stop=(b == 3))
                    for par in range(2):
                        ph = slice(par * 64, par * 64 + 64)
                        e.matmul(ps[0:32, 2 + par, 0:128], lhsT=KT2[ph, g, 1152:1184],
                                 rhs=QT3[ph, :, 1024:1056].rearrange("p a (b t) -> p b a t", t=8), start=True, stop=False)
                        ins = e.matmul(ps[0:32, 2 + par, 0:128], lhsT=ident[0:32, 0:32],
                                       rhs=msb[:, :].rearrange("p (b t) -> p b t", t=8).unsqueeze(2).broadcast_to([32, 4, 4, 8]),
                                       start=False, stop=True)
                    return ins
                S.op("pe", s_mm, reads=KT2all + QK_ + ["ident", "maskb", "msb"], writes=[("ps", b) for b in range(4)])
                S.op("act", lambda e, PTc=PTc: e.activation(out=PTc, in_=ps[:, 0:2, 0:128], func=AF.Exp, bias=NEGM, scale=0.125),
                     reads=[("ps", 0), ("ps", 1), "NEGM"], writes=PTK[buf])
                S.op("act", lambda e, PTn=PTn: e.activation(out=PTn, in_=ps[0:32, 2:4, 0:128], func=AF.Exp, bias=NEGM[0:32, :], scale=0.125),
                     reads=[("ps", 2), ("ps", 3), "NEGM"], writes=PTK[buf])

                def s_pv(e, PTc=PTc, PTn=PTn, g=g):
                    for bank, use_v in ((4, True), (5, False)):
                        for par in range(2):
                            ph = slice(par * 64, par * 64 + 64)
                            lhs_n = V[0:32, 9, g * 64:(g + 1) * 64] if use_v else ones64[0:32, :]
                            e.matmul(ps[ph, bank, 0:128], lhsT=lhs_n, rhs=PTn[:, par, :], start=True, stop=False,
                                     tile_position=(0, par * 64))
                            for b in range(4):
                                lhs_c = Vc[:, b, g * 64:(g + 1) * 64] if use_v else ones64[:, :]
                                ins = e.matmul(ps[ph, bank, b * 32:(b + 1) * 32],
                                               lhsT=lhs_c, rhs=PTc[:, par, b * 32:(b + 1) * 32],
                                               start=False, stop=(b == 3), tile_position=(0, par * 64))
                    return ins
                S.op("pe", s_pv, reads=PTK[buf] + [("V", 9), "VC", "ones64"], writes=[("ps", 4), ("ps", 5)])
                rds = RD[:, 0:128].rearrange("p (b a t) -> p b a t", b=4, t=8)
                S.op("dve", lambda e, rds=rds, g=g: e.tensor_tensor(
                    out=rds, in0=ps[:, 5, 0:128].rearrange("p (b a t) -> p b a t", b=4, t=8),
                    in1=SE[:, 4 * g:4 * g + 4].unsqueeze(1).unsqueeze(3).broadcast_to([128, 4, 4, 8]), op=ALU.add),
                    reads=[("ps", 5), "SEa", "SEb"], writes=RDK)
                S.op("dve", lambda e: e.reciprocal(out=RD[:, 0:128], in_=RD[:, 0:128]), reads=RDK, writes=RDK)
                S.op("dve", lambda e, rds=rds: e.tensor_tensor(
                    out=OT3[:, :, 1024:1056].rearrange("p a (b t) -> p b a t", t=8),
                    in0=ps[:, 4, 0:128].rearrange("p (b a t) -> p b a t", b=4, t=8), in1=rds, op=ALU.mult),
                    reads=[("ps", 4)] + RDK, writes=OK_)
                if g == 3:
                    NP3 = NormPipe(3, TILES1, 32, HB_X0)
                for hf in range(2):
                    Wa, wka = W.get(ao[hf])
                    for t in TILES1:
                        rows = 128 if t < 9 else 32
                        lc = t * 128 - 128
                        for sub in range(2):
                            bank = 6 + (dn_cnt[0] % 2)
                            dn_cnt[0] += 1
                            xg = hf * 2 + sub

                            def mm(e, Wa=Wa, rows=rows, lc=lc, bank=bank, sub=sub):
                                for c in range(4):
                                    ins = e.matmul(ps[0:rows, bank, :], lhsT=OT3[:, c, lc:lc + rows],
                                                   rhs=Wa[:, c, sub * 512:(sub + 1) * 512], start=(c == 0), stop=(c == 3))
                                return ins
                            S.op("pe", mm, reads=wka + OK_, writes=[("ps", bank)])
                            S.op("dve", lambda e, t=t, rows=rows, bank=bank, xg=xg: e.tensor_tensor(
                                out=X[0:rows, t, xg * 512:(xg + 1) * 512], in0=X[0:rows, t, xg * 512:(xg + 1) * 512],
                                in1=ps[0:rows, bank, :], op=ALU.add), reads=[("ps", bank), ("X", t, xg)], writes=[("X", t, xg)])
                        if g == 3 and hf == 1:
                            NP3.tile_ready(t)
                    W.done(ao[hf])

            ckpt(7)
            TG1 = [(128, 640), (640, 1152), (1152, 1184)]
            if not skip_mlp1:
                mlp(NP3, P_mlp1, TILES1, 128, 32, TG1)

        try:
            record()
        except _Stop:
            pass
        TILES1 = list(range(1, 10))
        for g4 in range(4):
            for t in TILES1:
                rows = 128 if t < 9 else 32
                S.dma("sp", lambda e, t=t, rows=rows, g4=g4: e.dma_start(
                    out=y_o[(t - 1) * 128:(t - 1) * 128 + rows, g4 * 512:(g4 + 1) * 512], in_=X[0:rows, t, g4 * 512:(g4 + 1) * 512]),
                    ("x", t), reads=[("X", t, g4)], is_output=True)

        with nc.allow_low_precision("bf16 matmul operands with fp32 PSUM accumulation"):
            S.emit(st)
    return nc


_PROGRAM = None


def _rope_cos_sin(pos):
    half = 32
    try:
        import jax
        import jax.numpy as jnp
        cpu = jax.devices("cpu")[0]
        with jax.default_device(cpu):
            inv = 10000.0 ** (-jnp.arange(half, dtype=jnp.float32) / half)
            ang = jnp.asarray(pos, dtype=jnp.float32)[..., None] * inv
            return np.asarray(jnp.cos(ang), dtype=np.float32), np.asarray(jnp.sin(ang), dtype=np.float32)
    except Exception:
        inv = (np.float32(10000.0) ** (-(np.arange(half, dtype=np.float32)) / np.float32(half))).astype(np.float32)
        ang = (np.asarray(pos, np.float32)[..., None] * inv).astype(np.float32)
        return np.cos(ang).astype(np.float32), np.sin(ang).astype(np.float32)


def _tables(s, first):
    pos = np.zeros((128, 10), np.float32)
    r = np.arange(128)
    for t in range(9):
        pos[:, t] = s - 128 + 128 * t + r
    pos[:32, 9] = PAST + (r[:32] % 8)
    c, sn = _rope_cos_sin(pos)
    cos2 = np.concatenate([c, c], axis=-1)
    sinm = np.concatenate([-sn, sn], axis=-1)
    j = np.arange(128)[:, None]
    i = np.arange(128)[None, :]
    masks = np.zeros((128, 3, 128), np.float32)
    NEG = -30000.0
    masks[:, 0, :] = np.where(j > i, 0.0, NEG)
    masks[:, 1, :] = np.where(j <= i, 0.0, NEG)
    masks[:, 2, :] = NEG if first else np.where(j > i, 0.0, NEG)
    jj = np.arange(32)[:, None]
    qq = np.arange(32)[None, :]
    mask_s = np.where((jj // 8 == qq // 8) & ((jj % 8) <= (qq % 8)), 0.0, NEG).astype(np.float32)
    return np.ascontiguousarray(cos2), np.ascontiguousarray(sinm), masks, mask_s


def make_in_maps(x_prompt, x_sample, state_conv, cache_k_win, cache_v_win, ln_mix, ln_mlp,
                 w_conv_in, w_conv, w_conv_out, w_qkv, w_attn_out, q_norm, k_norm, sinks, w_up, w_down):
    f = lambda a: np.ascontiguousarray(np.asarray(a, dtype=np.float32))
    x_prompt, x_sample, state_conv = f(x_prompt), f(x_sample), f(state_conv)
    cache_k_win, cache_v_win = f(cache_k_win), f(cache_v_win)
    shared = {
        "ln": f(np.stack([np.asarray(ln_mix)[0], np.asarray(ln_mlp)[0], np.asarray(ln_mix)[1], np.asarray(ln_mlp)[1]])),
        "w_in": f(np.asarray(w_conv_in)[0]),
        "w_out": f(np.asarray(w_conv_out)[0]),
        "w_qkv": f(np.asarray(w_qkv)[0]),
        "w_ao": f(np.asarray(w_attn_out)[0]),
        "qn": f(np.asarray(q_norm)[0:1]),
        "kn": f(np.asarray(k_norm)[0:1]),
        "sinks": f(np.asarray(sinks)[0:1]),
        "w_up": f(w_up),
        "w_dn": f(w_down),
    }
    wc = f(np.asarray(w_conv)[0])
    in_maps = []
    for c in range(NCORES):
        bi, qi = c // 4, c % 4
        s = qi * 1024
        xt = np.zeros((NT, D), np.float32)
        if qi > 0:
            xt[0:128] = x_prompt[bi, s - 128:s]
            xt[1184:1186] = x_prompt[bi, s - 130:s - 128]
        xt[128:1152] = x_prompt[bi, s:s + 1024]
        xt[1152:1184] = x_sample[4 * c:4 * c + 4].reshape(32, D)
        swc = np.concatenate([state_conv[0, 4 * c:4 * c + 4].reshape(8, D), wc], axis=0)
        cos2, sinm, masks, mask_s = _tables(s, qi == 0)
        m = dict(shared)
        m.update({
            "xtok": xt, "swc": np.ascontiguousarray(swc),
            "ck": np.ascontiguousarray(cache_k_win[0, 4 * c:4 * c + 4].reshape(4, 128, 256)),
            "cv": np.ascontiguousarray(cache_v_win[0, 4 * c:4 * c + 4].reshape(4, 128, 256)),
            "cos_t": cos2, "sin_t": sinm, "masks": masks, "mask_s": mask_s,
        })
        in_maps.append(m)
    return in_maps


def assemble(R):
    y_prompt = np.zeros((2, 4096, D), np.float32)
    y_sample = np.zeros((32, 8, D), np.float32)
    ncp = np.zeros((1, 2, 2, D), np.float32)
    ncs = np.zeros((1, 32, 2, D), np.float32)
    kp = np.zeros((1, 2, 128, 4, 64), np.float32)
    vp = np.zeros((1, 2, 128, 4, 64), np.float32)
    ks = np.zeros((1, 32, 128, 4, 64), np.float32)
    vs = np.zeros((1, 32, 128, 4, 64), np.float32)
    for c in range(NCORES):
        if R[c] is None:
            continue
        bi, qi = c // 4, c % 4
        s = qi * 1024
        r = R[c]
        y_prompt[bi, s:s + 1024] = r["y"][0:1024]
        y_sample[4 * c:4 * c + 4] = r["y"][1024:1056].reshape(4, 8, D)
        ncs[0, 4 * c:4 * c + 4] = r["ncv"][2:10].reshape(4, 2, D)
        ks[0, 4 * c:4 * c + 4] = r["ks"].reshape(4, 128, 4, 64)
        vs[0, 4 * c:4 * c + 4] = r["vs"].reshape(4, 128, 4, 64)
        if qi == 3:
            ncp[0, bi] = r["ncv"][0:2]
            kp[0, bi] = r["kp"].reshape(128, 4, 64)
            vp[0, bi] = r["vp"].reshape(128, 4, 64)
    return (y_prompt, y_sample, ncp, ncs, kp, vp, ks, vs)


def kernel(**inputs):
    global _PROGRAM
    if _PROGRAM is None:
        _PROGRAM = build_program()
    in_maps = make_in_maps(**inputs)
    res = run_bass_kernel_spmd(_PROGRAM, in_maps, core_ids=list(range(NCORES)))
    return assemble(res.results)
```

```python
import numpy as np
from contextlib import ExitStack
import concourse.bass as bass
import concourse.mybir as mybir
from concourse.bass_utils import run_bass_kernel_spmd

F32 = mybir.dt.float32
BF16 = mybir.dt.bfloat16
AF = mybir.ActivationFunctionType
ALU = mybir.AluOpType
AX = mybir.AxisListType

D = 2048
DFF = 8192
NT = 1186
EPS = 1e-6
NCORES = 8
PAST = 16384


class _Op:
    __slots__ = ("eng", "fn", "deps", "signal", "key", "tick", "is_dma")

    def __init__(self, eng, fn, key, is_dma):
        self.eng = eng
        self.fn = fn
        self.deps = []
        self.signal = is_dma
        self.key = key
        self.tick = 0
        self.is_dma = is_dma


class Sched:
    ENGS = ("pe", "act", "dve", "pool", "sp")

    def __init__(self, nc, same_engine_sync=("act", "dve", "pool")):
        self.nc = nc
        self.streams = {e: [] for e in self.ENGS}
        self.last_writer = {}
        self.readers = {}
        self.same_sync = set(same_engine_sync)
        self.dma_counts = {}
        self.out_dmas = []

    def _add(self, op, reads, writes):
        deps = {}
        for r in reads:
            w = self.last_writer.get(r)
            if w is not None:
                deps[id(w)] = w
        for r in writes:
            w = self.last_writer.get(r)
            if w is not None:
                deps[id(w)] = w
            for rd in self.readers.get(r, ()):
                deps[id(rd)] = rd
        op.deps = list(deps.values())
        for r in reads:
            self.readers.setdefault(r, []).append(op)
        for r in writes:
            self.last_writer[r] = op
            self.readers[r] = []
        self.streams[op.eng].append(op)
        return op

    def op(self, eng, fn, reads=(), writes=()):
        return self._add(_Op(eng, fn, eng, False), reads, writes)

    def dma(self, eng, fn, slot, reads=(), writes=(), is_output=False):
        op = _Op(eng, fn, ("dma", slot), True)
        c = self.dma_counts.get(slot, 0) + 16
        self.dma_counts[slot] = c
        op.tick = c
        self._add(op, reads, writes)
        self.out_dmas.append(op)
        return op

    def finalize(self):
        fin = _Op("sp", None, "sp", False)
        fin.deps = list(self.out_dmas)
        self.streams["sp"].append(fin)
        for e in self.ENGS:
            for op in self.streams[e]:
                for d in op.deps:
                    if d.is_dma:
                        continue
                    if d.eng == op.eng and not op.is_dma and d.eng not in self.same_sync:
                        continue
                    d.signal = True
        for e in self.ENGS:
            t = 0
            for op in self.streams[e]:
                if op.is_dma:
                    continue
                if op.signal:
                    t += 1
                    op.tick = t

    def emit(self, stack):
        nc = self.nc
        self.finalize()
        sems = {}
        import os
        for i in range(int(os.environ.get("SEM_PAD", "0"))):
            stack.enter_context(nc.semaphore("pad%d" % i))
        for e in self.ENGS:
            sems[e] = stack.enter_context(nc.semaphore("s_" + e))
        for i, slot in enumerate(self.dma_counts):
            sems[("dma", slot)] = stack.enter_context(nc.semaphore("d%d" % i))
        block = stack.enter_context(nc.Block())
        sched = self

        def run(ename, eng):
            waited = {}
            for op in sched.streams[ename]:
                need = {}
                for d in op.deps:
                    if (not d.is_dma) and d.eng == ename and (not op.is_dma) and ename not in sched.same_sync:
                        continue
                    k = d.key
                    if need.get(k, 0) < d.tick:
                        need[k] = d.tick
                for k, t in need.items():
                    if waited.get(k, 0) >= t:
                        continue
                    eng.wait_ge(sems[k], t)
                    waited[k] = t
                if op.fn is None:
                    continue
                ins = op.fn(eng)
                if op.is_dma:
                    ins.then_inc(sems[op.key], 16)
                elif op.signal:
                    ins.then_inc(sems[ename], 1)

        @block.tensor
        def _(e):
            run("pe", e)

        @block.scalar
        def _(e):
            run("act", e)

        @block.vector
        def _(e):
            run("dve", e)

        @block.gpsimd
        def _(e):
            run("pool", e)

        @block.sync
        def _(e):
            run("sp", e)


UNIT = 2048
NUNITS = 6


class WStream:
    def __init__(self, S, ring):
        self.S = S
        self.ring = ring
        self.pieces = []
        self.views = {}
        self.next_issue = 0
        self.ptr = 0
        self.free = [True] * NUNITS
        self.units_of = {}
        self.hold_keys = []

    def add(self, src, nunits, a, b):
        self.pieces.append((src, nunits, a, b))
        return len(self.pieces) - 1

    def _try_issue(self):
        if self.next_issue >= len(self.pieces):
            return False
        src, nu, a, b = self.pieces[self.next_issue]
        p = self.ptr
        if p + nu > NUNITS:
            p = 0
        if not all(self.free[p:p + nu]):
            return False
        for u in range(p, p + nu):
            self.free[u] = False
        pid = self.next_issue
        self.units_of[pid] = (p, nu)
        view = self.ring[:, p * UNIT:p * UNIT + a * b].rearrange("p (a b) -> p a b", b=b)
        keys = [("ring", u) for u in range(p, p + nu)]
        self.views[pid] = (view, keys)
        extra = self.hold_keys if (2 <= pid < 6) else []
        self.S.dma("pool", lambda e, view=view, src=src: e.dma_start(out=view, in_=src),
                   ("ring", p), reads=extra, writes=keys)
        self.ptr = p + nu
        self.next_issue += 1
        return True

    def get(self, pid):
        while self._try_issue():
            pass
        assert pid in self.views, "ring deadlock: piece %d not issued" % pid
        return self.views[pid]

    def done(self, pid):
        p, nu = self.units_of[pid]
        for u in range(p, p + nu):
            self.free[u] = True
        while self._try_issue():
            pass


class _Stop(Exception):
    pass


def build_program(skip_mlp1=False, stage=99):
    nc = bass.Bass("TRN2", target_bir_lowering=False)

    def din(name, shape):
        return nc.dram_tensor(name, shape, F32, kind="ExternalInput").ap()

    def dout(name, shape):
        return nc.dram_tensor(name, shape, F32, kind="ExternalOutput").ap()

    xtok = din("xtok", [NT, D])
    swc = din("swc", [11, D])
    ck = din("ck", [4, 128, 256])
    cv = din("cv", [4, 128, 256])
    ln = din("ln", [4, D])
    w_in = din("w_in", [D, 3 * D])
    w_out = din("w_out", [D, D])
    w_qkv = din("w_qkv", [D, 2560])
    w_ao = din("w_ao", [D, D])
    qn = din("qn", [1, 64])
    kn = din("kn", [1, 64])
    sinks = din("sinks", [1, 32])
    w_up = din("w_up", [2, D, DFF])
    w_dn = din("w_dn", [2, DFF, D])
    cos_t = din("cos_t", [128, 10, 64])
    sin_t = din("sin_t", [128, 10, 64])
    masks = din("masks", [128, 3, 128])
    mask_s = din("mask_s", [32, 32])

    y_o = dout("y", [1056, D])
    ncv_o = dout("ncv", [10, D])
    kp_o = dout("kp", [128, 256])
    vp_o = dout("vp", [128, 256])
    ks_o = dout("ks", [4, 128, 256])
    vs_o = dout("vs", [4, 128, 256])

    with ExitStack() as st:
        def sb(name, shape, dt):
            return st.enter_context(nc.sbuf_tensor(name, shape, dt))

        X = sb("X", [128, 10, D], F32)
        HT = sb("HT", [128, 16 * NT], BF16)
        ACT_T = sb("ACT_T", [128, 8 * NT], BF16)
        RING = sb("RING", [128, NUNITS * UNIT], BF16)
        TMP = sb("TMP", [128, 3600], F32)
        ATT = sb("ATT", [128, 4352], F32)
        identf = sb("identf", [128, 128], F32)
        ident = sb("ident", [128, 128], BF16)
        ones64 = sb("ones64", [128, 64], BF16)
        maskb = sb("maskb", [128, 3, 128], BF16)
        mask01 = sb("mask01", [128, 3, 128], BF16)
        msb = sb("msb", [32, 32], BF16)
        COS = sb("COS", [128, 10, 64], F32)
        SIN = sb("SIN", [128, 10, 64], F32)
        SWT = sb("SWT", [128, 16, 11], F32)
        UK = sb("UK", [128, 16, 10], F32)
        SS = sb("SS", [128, 16], F32)
        GQ = sb("GQ", [128, 64], F32)
        GK = sb("GK", [128, 64], F32)
        SK = sb("SK", [128, 32], F32)
        SE0 = sb("SE0", [128, 32], F32)
        SE = sb("SE", [128, 16], F32)
        SM = sb("SM", [128, 8], F32)
        EPSB = sb("EPSB", [128, 1], F32)
        S4T = sb("S4T", [128, 8], F32)
        JUNK = sb("JUNK", [128, 256], BF16)
        ps = st.enter_context(nc.psum_tensor("ps", [128, 8, 512], F32))
        psb = ps.bitcast(BF16)

        HT3 = HT[:].rearrange("p (c n) -> p c n", n=NT)
        AT3 = ACT_T[:].rearrange("p (c n) -> p c n", n=NT)
        T0 = TMP[:, 0:1200]
        T1 = TMP[:, 1200:2400]
        T2 = TMP[:, 2400:3600]
        GB = TMP[:, 0:2048]
        KT2 = TMP[:].bitcast(BF16)[:, 0:4 * 1696].rearrange("p (k n) -> p k n", n=1696)
        ALLT = ["T0", "T1", "T1h", "T1s", "T1p", "T2", "T2s"]
        ATTb = ATT[:].bitcast(BF16)
        V = ATTb[:, 0:2560].rearrange("p (t n) -> p t n", n=256)
        Vc = ATTb[:, 2560:3584].rearrange("p (b n) -> p b n", n=256)
        XS = [ATT[:, 1792:2048], ATT[:, 2048:2304]]
        AB = [ATT[:, 2304:2560], ATT[:, 2560:2816]]
        BB = [ATT[:, 2816:3072], ATT[:, 3072:3328]]
        RD = ATT[:, 1792:2304]
        RDK = ["XS0", "XS1"]
        QRB = [ATTb[:, 6656:6912], ATTb[:, 6912:7168]]
        KDUP = ATTb[:, 7168:7680]
        KRBS = [ATT[:, 3840:4096], ATT[:, 4096:4352]]
        CKB = ATTb[:, 4608:5632].rearrange("p (b n) -> p b n", n=256)
        CKBK = ["AB0", "AB1"]
        X0b = X[:, 0, :].bitcast(BF16)
        PT = [X0b[:, 0:2048], X0b[:, 2048:4096]]
        PTK = [[("X", 0, 0), ("X", 0, 1)], [("X", 0, 2), ("X", 0, 3)]]
        HB_ATT = ([ATTb[:, 0:2048], ATTb[:, 2048:4096], ATTb[:, 4096:6144]],
                  [[("V", t) for t in range(8)], [("V", 8), ("V", 9), "VC", "XS0"], ["XS1", "AB0", "AB1", "BB0"]])
        HB_X0 = (PT, PTK)

        S = Sched(nc)
        W = WStream(S, RING[:])
        W.hold_keys = [("X", 9, 0)]

        def XK(t):
            return [("X", t, g) for g in range(4)]

        HK = [("H", t) for t in range(10)]

        def wv(src, c0, ncol):
            return src[:, c0:c0 + ncol].rearrange("(k p) n -> p k n", p=128)

        def wr(src, r0, nk, c0, ncol):
            return src[r0:r0 + nk * 128, c0:c0 + ncol].rearrange("(k p) n -> p k n", p=128)

        P_conv = []
        for gi in range(2):
            up = []
            for j in range(8):
                J = gi * 8 + j
                up.append((W.add(wv(w_in, 2048 + J * 128, 128), 1, 16, 128),
                           W.add(wv(w_in, 4096 + J * 128, 128), 1, 16, 128),
                           W.add(wv(w_in, J * 128, 128), 1, 16, 128)))
            dn = [W.add(wr(w_out, gi * 1024, 8, ng * 512, 512), 2, 8, 512) for ng in range(4)]
            P_conv.append((up, dn))

        def plan_mlp(l):
            out = []
            for gi in range(8):
                up = [W.add(wv(w_up[l], (gi * 8 + 2 * pi) * 128, 256), 2, 16, 256) for pi in range(4)]
                dn = [W.add(wr(w_dn[l], gi * 1024, 8, ng * 512, 512), 2, 8, 512) for ng in range(4)]
                out.append((up, dn))
            return out

        P_mlp0 = plan_mlp(0)
        P_k = W.add(wv(w_qkv, 2048, 256), 2, 16, 256)
        P_v = W.add(wv(w_qkv, 2304, 256), 2, 16, 256)
        P_att = []
        for g in range(4):
            qp = [W.add(wv(w_qkv, g * 512 + qh * 256, 256), 2, 16, 256) for qh in range(2)]
            ao = [W.add(wr(w_ao, g * 512, 4, hf * 1024, 1024), 2, 4, 1024) for hf in range(2)]
            P_att.append((qp, ao))
        P_mlp1 = plan_mlp(1)

        S.dma("sp", lambda e: e.dma_start(out=GB, in_=ln[0:1, :].partition_broadcast(128)), "gb", writes=["T0", "T1", "T1h"])
        for t in range(10):
            rows = 128 if t < 9 else 34
            S.dma("sp", lambda e, t=t, rows=rows: e.dma_start(out=X[0:rows, t, :], in_=xtok[t * 128:t * 128 + rows, :]),
                  ("x", t), writes=XK(t))
        S.dma("sp", lambda e: e.dma_start(out=X[64:75, 9, :], in_=swc), "swc", writes=["X9hi"])
        S.dma("sp", lambda e: e.dma_start(out=COS[:], in_=cos_t), "cos", writes=["COS"])
        S.dma("sp", lambda e: e.dma_start(out=SIN[:], in_=sin_t), "sin", writes=["SIN"])
        S.dma("sp", lambda e: e.dma_start(out=GQ[:], in_=qn.partition_broadcast(128)), "gq", writes=["GQ"])
        S.dma("sp", lambda e: e.dma_start(out=GK[:], in_=kn.partition_broadcast(128)), "gk", writes=["GK"])
        S.dma("sp", lambda e: e.dma_start(out=SK[:], in_=sinks.partition_broadcast(128)), "sk", writes=["SK"])
        S.dma("pool", lambda e: e.dma_start(out=maskb[:], in_=masks), "maskb", writes=["maskb"])
        S.dma("pool", lambda e: e.dma_start(out=msb[:], in_=mask_s), "msb", writes=["msb"])
        S.op("pool", lambda e: e.memset(identf[:], 1.0), writes=["identf"])
        S.op("pool", lambda e: e.affine_select(out=identf[:], in_=identf[:], pattern=[[-1, 128]],
                                               compare_op=ALU.is_equal, fill=0.0, base=0, channel_multiplier=1),
             reads=["identf"], writes=["identf"])
        S.op("dve", lambda e: e.tensor_copy(out=ident[:], in_=identf[:]), reads=["identf"], writes=["ident"])
        S.op("dve", lambda e: e.tensor_scalar(out=mask01[:], in0=maskb[:], scalar1=0.0, scalar2=None, op0=ALU.is_equal),
             reads=["maskb"], writes=["mask01"])
        S.op("dve", lambda e: e.memset(ones64[:], 1.0), writes=["ones64"])
        S.op("dve", lambda e: e.memset(EPSB[:], EPS), writes=["EPSB"])

        norm_cnt = [0]

        class NormPipe:
            def __init__(self, li, tiles, rows9, hbsel, gb_loaded=False):
                Hb, HbK = hbsel
                self.overlapped = False
                if not gb_loaded:
                    S.dma("sp", lambda e: e.dma_start(out=GB, in_=ln[li:li + 1, :].partition_broadcast(128)),
                          "gb", writes=["T0", "T1", "T1h"])
                self.info = []
                for t in tiles:
                    i = norm_cnt[0]
                    norm_cnt[0] += 1
                    self.info.append((t, 128 if t < 9 else rows9, Hb[i % len(Hb)], HbK[i % len(Hb)], i % 16, (i % 2) * 2))
                self.nbuf = len(Hb)
                self.idx = {t: k for k, t in enumerate(tiles)}
                self.na = 0
                self.nb = 0

            def _a(self, t, rows, hb, hk, col, b0):
                S.op("act", lambda e: e.activation(
                    out=hb[0:rows, :], in_=X[0:rows, t, :], func=AF.Square, accum_out=SS[0:rows, col:col + 1]),
                    reads=XK(t), writes=hk + [("SS", col)])
                S.op("act", lambda e: e.activation(
                    out=SS[0:rows, col:col + 1], in_=SS[0:rows, col:col + 1], func=AF.Sqrt,
                    bias=EPSB[0:rows, :], scale=1.0 / D), reads=[("SS", col), "EPSB"], writes=[("SS", col)])
                S.op("dve", lambda e: e.reciprocal(out=SS[0:rows, col:col + 1], in_=SS[0:rows, col:col + 1]),
                     reads=[("SS", col)], writes=[("SS", col)])
                S.op("dve", lambda e: e.scalar_tensor_tensor(
                    out=hb[0:rows, :], in0=X[0:rows, t, :], scalar=SS[0:rows, col:col + 1], in1=GB[0:rows, :],
                    op0=ALU.mult, op1=ALU.mult), reads=XK(t) + [("SS", col), "T0", "T1", "T1h"], writes=hk)

            def _b(self, t, rows, hb, hk, col, b0):
                def tr(e):
                    for c in range(16):
                        ins = e.transpose(out=psb[:, b0 + c // 8, (c % 8) * 128:(c % 8) * 128 + rows],
                                          in_=hb[0:rows, c * 128:(c + 1) * 128], identity=ident[0:rows, 0:rows])
                    return ins
                S.op("pe", tr, reads=hk + ["ident"], writes=[("ps", b0), ("ps", b0 + 1)])
                for half in range(2):
                    src = psb[:, b0 + half, :].rearrange("p (c n) -> p c n", n=128)[:, :, 0:rows]
                    dst = HT3[:, half * 8:(half + 1) * 8, t * 128:t * 128 + rows]
                    if half == 0 or self.overlapped:
                        S.op("act", lambda e, src=src, dst=dst: e.activation(out=dst, in_=src, func=AF.Copy),
                             reads=[("ps", b0 + half)], writes=[("H", t, half)])
                    else:
                        S.op("dve", lambda e, src=src, dst=dst: e.tensor_copy(out=dst, in_=src),
                             reads=[("ps", b0 + half)], writes=[("H", t, half)])

            def tile_ready(self, t):
                if t not in self.idx:
                    return
                assert self.idx[t] == self.na
                self.overlapped = True
                if self.nbuf >= 3:
                    self._a(*self.info[self.na])
                    self.na += 1
                    if self.na >= 3:
                        self._b(*self.info[self.nb])
                        self.nb += 1
                else:
                    if self.na >= 2:
                        self._b(*self.info[self.nb])
                        self.nb += 1
                    self._a(*self.info[self.na])
                    self.na += 1

            def finish(self):
                self.overlapped = False
                n = len(self.info)
                while self.nb < n:
                    if self.na < n and (self.na - self.nb) < 2:
                        self._a(*self.info[self.na])
                        self.na += 1
                        continue
                    self._b(*self.info[self.nb])
                    self.nb += 1

        def HKall(tiles):
            return [("H", t, h) for t in tiles for h in range(2)]

        def fm_mm(Wv, c_lo, bank0, tgs):
            def f(e):
                for k in range(16):
                    for gi, (c0, c1) in enumerate(tgs):
                        ins = e.matmul(ps[:, bank0 + gi, 0:c1 - c0], lhsT=Wv[:, k, c_lo:c_lo + 128],
                                       rhs=HT3[:, k, c0:c1], start=(k == 0), stop=(k == 15))
                return ins
            return f

        def flat(bank0, n):
            return ps[:, bank0:bank0 + 3, :].rearrange("p b n -> p (b n)")[:, 0:n]

        def BK(bank0):
            return [("ps", bank0), ("ps", bank0 + 1), ("ps", bank0 + 2)]

        dn_cnt = [0]

        def down_phase(pieces, nk, tiles, colbase, rows9, akeys, hook=None):
            def one(Wv, wk, ngi, t):
                rows = 128 if t < 9 else rows9
                lc = t * 128 - colbase
                bank = 6 + (dn_cnt[0] % 2)
                dn_cnt[0] += 1
                xg = ngi

                def mm(e):
                    for c in range(nk):
                        ins = e.matmul(ps[0:rows, bank, :], lhsT=AT3[:, c, lc:lc + rows],
                                       rhs=Wv[:, c, :], start=(c == 0), stop=(c == nk - 1))
                    return ins
                S.op("pe", mm, reads=wk + akeys, writes=[("ps", bank)])
                S.op("dve", lambda e: e.tensor_tensor(
                    out=X[0:rows, t, xg * 512:(xg + 1) * 512], in0=X[0:rows, t, xg * 512:(xg + 1) * 512],
                    in1=ps[0:rows, bank, :], op=ALU.add), reads=[("ps", bank), ("X", t, xg)], writes=[("X", t, xg)])

            first = pieces if hook is None else pieces[:-2]
            for ngi, pid in enumerate(first):
                Wv, wk = W.get(pid)
                for t in tiles:
                    one(Wv, wk, ngi, t)
                W.done(pid)
            if hook is not None:
                n0 = len(pieces) - 2
                Wa_, wka_ = W.get(pieces[n0])
                Wb_, wkb_ = W.get(pieces[n0 + 1])
                for t in tiles:
                    one(Wa_, wka_, n0, t)
                    one(Wb_, wkb_, n0 + 1, t)
                    hook(t)
                W.done(pieces[n0])
                W.done(pieces[n0 + 1])

        def ckpt(k):
            if k > stage:
                raise _Stop()

        def record():
            TG0 = [(0, 512), (512, 1024), (1024, NT)]
            ALL10 = list(range(10))
            NormPipe(0, ALL10, 34, HB_ATT, gb_loaded=True).finish()
            def swt_mm(e):
                for c in range(16):
                    ins = e.matmul(ps[:, 0, c * 11:(c + 1) * 11], lhsT=X[64:75, 9, c * 128:(c + 1) * 128],
                                   rhs=identf[64:75, 64:75], start=True, stop=True)
                return ins
            S.op("pe", swt_mm, reads=["X9hi", "identf"], writes=[("ps", 0)])
            S.op("act", lambda e: e.activation(out=SWT[:].rearrange("p c n -> p (c n)"), in_=ps[:, 0, 0:176], func=AF.Copy),
                 reads=[("ps", 0)], writes=["SWT"])
            ckpt(2)
            HA = HKall(ALL10)
            ubuf = T1
            ubs = T1[:, 1154:1194].rearrange("p (b n) -> p b n", n=10)
            t1 = T2
            csb = T0
            setc = [0]

            def nextset():
                s = (setc[0] % 2) * 3
                setc[0] += 1
                return s

            for gi in range(2):
                up, dn = P_conv[gi]
                for j in range(8):
                    J = gi * 8 + j
                    pc, pv, pb = up[j]
                    w0 = SWT[:, J, 8:9]
                    w1 = SWT[:, J, 9:10]
                    w2 = SWT[:, J, 10:11]
                    Wc, wkc = W.get(pc)
                    sA = nextset()
                    S.op("pe", fm_mm(Wc, 0, sA, TG0), reads=wkc + HA, writes=BK(sA))
                    W.done(pc)
                    S.op("act", lambda e, sA=sA: e.activation(out=csb[:, 0:NT], in_=flat(sA, NT), func=AF.Copy),
                         reads=BK(sA), writes=["T0"])
                    Wvv, wkv = W.get(pv)
                    sB = nextset()
                    S.op("pe", fm_mm(Wvv, 0, sB, TG0), reads=wkv + HA, writes=BK(sB))
                    W.done(pv)
                    S.op("dve", lambda e, sB=sB: e.tensor_tensor(out=ubuf[:, 2:1154], in0=csb[:, 0:1152], in1=flat(sB, NT)[:, 0:1152], op=ALU.mult),
                         reads=BK(sB) + ["T0"], writes=["T1"])
                    S.op("dve", lambda e, sB=sB: e.tensor_tensor(
                        out=ubs[:, :, 2:10], in0=csb[:, 1152:1184].rearrange("p (b n) -> p b n", n=8),
                        in1=flat(sB, NT)[:, 1152:1184].rearrange("p (b n) -> p b n", n=8), op=ALU.mult),
                        reads=BK(sB) + ["T0"], writes=["T1s"])
                    S.op("dve", lambda e, sB=sB: e.tensor_tensor(out=ubuf[:, 0:2], in0=csb[:, 1184:1186], in1=flat(sB, NT)[:, 1184:1186], op=ALU.mult),
                         reads=BK(sB) + ["T0"], writes=["T1h"])
                    S.op("pool", lambda e, J=J: e.tensor_copy(out=ubs[:, :, 0:2], in_=SWT[:, J, 0:8].rearrange("p (b n) -> p b n", n=2)),
                         reads=["SWT"], writes=["T1p"])
                    S.op("pool", lambda e, J=J: e.tensor_copy(out=UK[:, J, 0:2], in_=ubuf[:, 1152:1154]), reads=["T1"], writes=[("UK", J, 0)])
                    S.op("pool", lambda e, J=J: e.tensor_copy(out=UK[:, J, 2:10].rearrange("p (b n) -> p b n", n=2), in_=ubs[:, :, 8:10]),
                         reads=["T1s"], writes=[("UK", J, 1)])
                    UR = ["T1", "T1h"]
                    S.op("dve", lambda e, w0=w0: e.tensor_scalar(out=t1[:, 0:1152], in0=ubuf[:, 0:1152], scalar1=w0, scalar2=None, op0=ALU.mult),
                         reads=UR + ["SWT"], writes=["T2"])
                    S.op("dve", lambda e, w1=w1: e.scalar_tensor_tensor(out=t1[:, 0:1152], in0=ubuf[:, 1:1153], scalar=w1, in1=t1[:, 0:1152], op0=ALU.mult, op1=ALU.add),
                         reads=UR + ["T2"], writes=["T2"])
                    S.op("dve", lambda e, w2=w2: e.scalar_tensor_tensor(out=t1[:, 0:1152], in0=ubuf[:, 2:1154], scalar=w2, in1=t1[:, 0:1152], op0=ALU.mult, op1=ALU.add),
                         reads=UR + ["T2"], writes=["T2"])
                    t1s = t1[:, 1152:1184].rearrange("p (b n) -> p b n", n=8)
                    USR = ["T1s", "T1p"]
                    S.op("dve", lambda e, w0=w0, t1s=t1s: e.tensor_scalar(out=t1s, in0=ubs[:, :, 0:8], scalar1=w0, scalar2=None, op0=ALU.mult),
                         reads=USR + ["SWT"], writes=["T2s"])
                    S.op("dve", lambda e, w1=w1, t1s=t1s: e.scalar_tensor_tensor(out=t1s, in0=ubs[:, :, 1:9], scalar=w1, in1=t1s, op0=ALU.mult, op1=ALU.add),
                         reads=USR + ["T2s"], writes=["T2s"])
                    S.op("dve", lambda e, w2=w2, t1s=t1s: e.scalar_tensor_tensor(out=t1s, in0=ubs[:, :, 2:10], scalar=w2, in1=t1s, op0=ALU.mult, op1=ALU.add),
                         reads=USR + ["T2s"], writes=["T2s"])
                    Wb, wkb = W.get(pb)
                    sC = nextset()
                    S.op("pe", fm_mm(Wb, 0, sC, TG0), reads=wkb + HA, writes=BK(sC))
                    W.done(pb)
                    S.op("dve", lambda e, sC=sC, j=j: e.tensor_tensor(out=AT3[:, j, 0:1184], in0=flat(sC, NT)[:, 0:1184], in1=t1[:, 0:1184], op=ALU.mult),
                         reads=BK(sC) + ["T2", "T2s"], writes=[("A", j)])
                if gi == 0:
                    down_phase(dn, 8, ALL10, 0, 32, [("A", c) for c in range(8)])
                else:
                    def uk_mm(e):
                        for c in range(16):
                            ins = e.matmul(ps[0:10, c // 4, (c % 4) * 128:(c % 4 + 1) * 128], lhsT=UK[:, c, :], rhs=identf[:, :],
                                           start=True, stop=True)
                        return ins
                    S.op("pe", uk_mm, reads=[("UK", J, h) for J in range(16) for h in range(2)] + ["identf"],
                         writes=[("ps", b) for b in range(4)])
                    S.op("act", lambda e: e.activation(out=TMP[0:10, 0:2048], in_=ps[0:10, 0:4, :].rearrange("p b n -> p (b n)"), func=AF.Copy),
                         reads=[("ps", b) for b in range(4)], writes=["T0", "T1", "T1h"])
                    S.dma("sp", lambda e: e.dma_start(out=ncv_o, in_=TMP[0:10, 0:2048]), "ncv", reads=["T0", "T1", "T1h"], is_output=True)


                    NP1 = NormPipe(1, ALL10, 32, HB_ATT)
                    down_phase(dn, 8, ALL10, 0, 32, [("A", c) for c in range(8)], hook=NP1.tile_ready)
            ckpt(3)

            def mlp(NP, plan, tiles, colbase, rows9, tgs, next_norm=None):
                NP.finish()
                NPn = None
                HA_ = HKall(tiles)
                ncols = tgs[-1][1] - tgs[0][0]
                rcnt = 0
                for gi in range(8):
                    up, dn = plan[gi]
                    for pi in range(4):
                        Wv, wk = W.get(up[pi])
                        for cl in range(2):
                            c = pi * 2 + cl
                            sA = nextset()
                            S.op("pe", fm_mm(Wv, cl * 128, sA, tgs), reads=wk + HA_, writes=BK(sA))
                            rt = (T0, T1)[rcnt % 2]
                            rk = (["T0"], ["T1", "T1h", "T1s", "T1p"])[rcnt % 2]
                            rcnt += 1
                            S.op("act", lambda e, sA=sA, rt=rt: e.activation(out=rt[:, 0:ncols], in_=flat(sA, ncols), func=AF.Relu),
                                 reads=BK(sA), writes=rk)
                            S.op("pool", lambda e, rt=rt, c=c: e.tensor_tensor(out=AT3[:, c, 0:ncols], in0=rt[:, 0:ncols], in1=rt[:, 0:ncols], op=ALU.mult),
                                 reads=rk, writes=[("A", c)])
                        W.done(up[pi])
                    if gi == 7 and next_norm is not None:
                        NPn = NormPipe(*next_norm)
                        down_phase(dn, 8, tiles, colbase, rows9, [("A", c) for c in range(8)], hook=NPn.tile_ready)
                    else:
                        down_phase(dn, 8, tiles, colbase, rows9, [("A", c) for c in range(8)])
                return NPn

            ckpt(4)
            NP2 = mlp(NP1, P_mlp0, ALL10, 0, 32, TG0, next_norm=(2, ALL10, 32, HB_ATT))
            ckpt(5)

            NP2.finish()
            HA1 = HKall(ALL10)

            S.op("dve", lambda e: e.tensor_reduce(out=SM[:, 0:1], in_=GQ[:], axis=AX.X, op=ALU.max, apply_absolute_value=True), reads=["GQ"], writes=["SM0"])
            S.op("dve", lambda e: e.tensor_reduce(out=SM[:, 1:2], in_=GK[:], axis=AX.X, op=ALU.max, apply_absolute_value=True), reads=["GK"], writes=["SM1"])
            S.op("dve", lambda e: e.tensor_reduce(out=SM[:, 2:3], in_=SK[:], axis=AX.X, op=ALU.max), reads=["SK"], writes=["SM2"])
            S.op("dve", lambda e: e.tensor_tensor(out=SM[:, 3:4], in0=SM[:, 0:1], in1=SM[:, 1:2], op=ALU.mult), reads=["SM0", "SM1"], writes=["SM3"])
            S.op("dve", lambda e: e.scalar_tensor_tensor(out=SM[:, 4:5], in0=SM[:, 3:4], scalar=8.0, in1=SM[:, 2:3], op0=ALU.mult, op1=ALU.max),
                 reads=["SM3", "SM2"], writes=["SM4"])
            S.op("dve", lambda e: e.tensor_scalar(out=SM[:, 5:6], in0=SM[:, 4:5], scalar1=-1.0, scalar2=None, op0=ALU.mult), reads=["SM4"], writes=["NEGM"])
            NEGM = SM[:, 5:6]
            S.op("act", lambda e: e.activation(out=SE0[:], in_=SK[:], func=AF.Exp, bias=NEGM, scale=1.0), reads=["SK", "NEGM"], writes=["SE0"])
            SE0v = SE0[:].rearrange("p (a b) -> p a b", b=2)
            S.op("dve", lambda e: e.tensor_copy(out=SE[0:64, :], in_=SE0v[0:64, :, 0]), reads=["SE0"], writes=["SEa"])
            S.op("dve", lambda e: e.tensor_copy(out=SE[64:128, :], in_=SE0v[64:128, :, 1]), reads=["SE0"], writes=["SEb"])

            ckpt(5.1)
            S.dma("pool", lambda e: e.dma_start(out=CKB, in_=ck.rearrange("b k n -> k b n")), "ckb", writes=CKBK)
            S.dma("pool", lambda e: e.dma_start(out=Vc, in_=cv.rearrange("b k n -> k b n")), "vc", writes=["VC"])
            S.dma("sp", lambda e: e.dma_start(out=ks_o[:, 0:120, :], in_=ck[:, 8:128, :]), "ksw", is_output=True)
            S.dma("sp", lambda e: e.dma_start(out=vs_o[:, 0:120, :], in_=cv[:, 8:128, :]), "vsw", is_output=True)
            ckpt(5.2)
            kd4 = KDUP.rearrange("p (k d n) -> p k d n", d=2, n=64)
            trc = [0]

            def ktrans(src_rows, rows, dstcols):
                bank = 4 + (trc[0] % 2)
                trc[0] += 1

                def tr(e):
                    for kv in range(4):
                        ins = e.transpose(out=psb[:, bank, kv * 128:kv * 128 + rows], in_=KDUP[0:rows, kv * 128:(kv + 1) * 128],
                                          identity=ident[0:rows, 0:rows])
                    return ins
                S.op("pe", tr, reads=["KDUP", "ident"], writes=[("ps", bank)])
                S.op("act", lambda e: e.activation(
                    out=KT2[:, :, dstcols:dstcols + rows],
                    in_=psb[:, bank, 0:512].rearrange("p (k n) -> p k n", n=128)[:, :, 0:rows], func=AF.Copy),
                    reads=[("ps", bank)] + ALLT, writes=[("KT2", dstcols)])

            for b in range(4):
                S.op("dve", lambda e, b=b: e.tensor_copy(
                    out=kd4, in_=CKB[:, b, :].rearrange("p (k n) -> p k n", n=64).unsqueeze(2).broadcast_to([128, 4, 2, 64])),
                    reads=CKBK, writes=["KDUP"])
                ktrans(None, 128, 1184 + b * 128)

            ckpt(5.3)
            def build_tables(G, gkey):
                S.op("dve", lambda e: e.tensor_tensor(out=COS[:], in0=COS[:], in1=G[:].unsqueeze(1).broadcast_to([128, 10, 64]), op=ALU.mult),
                     reads=["COS", gkey], writes=["COS"])
                S.op("dve", lambda e: e.tensor_tensor(out=SIN[:, :, 0:32], in0=SIN[:, :, 0:32],
                                                      in1=G[:, 32:64].unsqueeze(1).broadcast_to([128, 10, 32]), op=ALU.mult),
                     reads=["SIN", gkey], writes=["SIN"])
                S.op("dve", lambda e: e.tensor_tensor(out=SIN[:, :, 32:64], in0=SIN[:, :, 32:64],
                                                      in1=G[:, 0:32].unsqueeze(1).broadcast_to([128, 10, 32]), op=ALU.mult),
                     reads=["SIN", gkey], writes=["SIN"])

            def qk_chain(bank, rows, t, cb, out_ap, out_keys):
                xs = XS[cb][0:rows, :]
                xs3 = xs.rearrange("p (h d) -> p h d", d=64)
                a3 = AB[cb][0:rows, :].rearrange("p (h d) -> p h d", d=64)
                b3 = BB[cb][0:rows, :].rearrange("p (h d) -> p h d", d=64)
                s4 = S4T[0:rows, cb * 4:cb * 4 + 4]
                xk, ak, bk, sk_ = "XS%d" % cb, "AB%d" % cb, "BB%d" % cb, "S4%d" % cb
                S.op("act", lambda e: e.activation(out=xs, in_=ps[0:rows, bank, 0:256], func=AF.Copy), reads=[("ps", bank)], writes=[xk])
                for h in range(4):
                    S.op("act", lambda e, h=h: e.activation(out=JUNK[0:rows, h * 64:(h + 1) * 64], in_=xs[:, h * 64:(h + 1) * 64],
                                                            func=AF.Square, accum_out=s4[:, h:h + 1]),
                         reads=[xk], writes=[("J", h), (sk_, h)])
                S.op("act", lambda e: e.activation(out=s4, in_=s4, func=AF.Sqrt, bias=EPSB[0:rows, :], scale=1.0 / 64),
                     reads=[(sk_, h) for h in range(4)] + ["EPSB"], writes=[sk_])
                S.op("dve", lambda e: e.tensor_tensor(out=a3, in0=xs3, in1=COS[0:rows, t, :].unsqueeze(1).broadcast_to([rows, 4, 64]), op=ALU.mult),
                     reads=[xk, "COS"], writes=[ak])
                S.op("dve", lambda e: e.tensor_tensor(out=b3[:, :, 0:32], in0=xs3[:, :, 32:64],
                                                      in1=SIN[0:rows, t, 0:32].unsqueeze(1).broadcast_to([rows, 4, 32]), op=ALU.mult),
                     reads=[xk, "SIN"], writes=[bk + "a"])
                S.op("dve", lambda e: e.tensor_tensor(out=b3[:, :, 32:64], in0=xs3[:, :, 0:32],
                                                      in1=SIN[0:rows, t, 32:64].unsqueeze(1).broadcast_to([rows, 4, 32]), op=ALU.mult),
                     reads=[xk, "SIN"], writes=[bk + "b"])
                S.op("dve", lambda e: e.reciprocal(out=s4, in_=s4), reads=[sk_], writes=[sk_])
                S.op("pool", lambda e: e.tensor_tensor(out=AB[cb][0:rows, :], in0=AB[cb][0:rows, :], in1=BB[cb][0:rows, :], op=ALU.add),
                     reads=[ak, bk + "a", bk + "b"], writes=[ak])
                S.op("pool", lambda e: e.tensor_tensor(out=out_ap, in0=a3, in1=s4.unsqueeze(2).broadcast_to([rows, 4, 64]), op=ALU.mult),
                     reads=[ak, sk_], writes=out_keys)

            pj = [0]

            def tm_proj(Wv, wk, t, rows):
                bank = 6 + (pj[0] % 2)
                pj[0] += 1

                def mm(e):
                    for k in range(16):
                        ins = e.matmul(ps[0:rows, bank, 0:256], lhsT=HT3[:, k, t * 128:t * 128 + rows], rhs=Wv[:, k, :],
                                       start=(k == 0), stop=(k == 15))
                    return ins
                S.op("pe", mm, reads=wk + [("H", t, 0), ("H", t, 1)], writes=[("ps", bank)])
                return bank

            def proj_stream(items, post_a, post_b=None):
                n = len(items)
                rws = [128 if it[2] < 9 else 32 for it in items]
                banks = [None] * n

                def pj_(i):
                    banks[i] = tm_proj(items[i][0], items[i][1], items[i][2], rws[i])
                pj_(0)
                if n > 1:
                    pj_(1)
                post_a(0, items[0][2], rws[0], banks[0], items[0][3])
                for i in range(n):
                    if i + 2 < n:
                        pj_(i + 2)
                    if i + 1 < n:
                        post_a(i + 1, items[i + 1][2], rws[i + 1], banks[i + 1], items[i + 1][3])
                    if post_b is not None:
                        post_b(i, items[i][2], rws[i], items[i][3])

            def proj_loop(Wv, wk, tiles, post_a, post_b=None):
                proj_stream([(Wv, wk, t, None) for t in tiles],
                            lambda i, t, rows, bank, tag: post_a(i, t, rows, bank),
                            None if post_b is None else (lambda i, t, rows, tag: post_b(i, t, rows)))

            build_tables(GK, "GK")
            Wk_, wkk = W.get(P_k)

            def k_post_a(i, t, rows, bank):
                KRB = KRBS[i % 2]
                qk_chain(bank, rows, t, i % 2, KRB[0:rows, :].rearrange("p (h d) -> p h d", d=64), ["KRB%d" % (i % 2)])

            def k_post_b(i, t, rows):
                KRB = KRBS[i % 2]
                kk = "KRB%d" % (i % 2)
                S.op("act", lambda e: e.activation(
                    out=kd4[0:rows], in_=KRB[0:rows, :].rearrange("p (k n) -> p k n", n=64).unsqueeze(2).broadcast_to([rows, 4, 2, 64]),
                    func=AF.Copy), reads=[kk], writes=["KDUP"])
                if t == 8:
                    S.dma("sp", lambda e: e.dma_start(out=kp_o, in_=KRB[:, :]), "kp", reads=[kk], is_output=True)
                if t == 9:
                    for b in range(4):
                        S.dma("sp", lambda e, b=b: e.dma_start(out=ks_o[b, 120:128, :], in_=KRB[b * 8:(b + 1) * 8, :]),
                              ("ksn", b), reads=[kk], is_output=True)
                ktrans(None, rows, t * 128)
            proj_loop(Wk_, wkk, ALL10, k_post_a, k_post_b)
            W.done(P_k)
            ckpt(5.4)
            S.dma("sp", lambda e: e.dma_start(out=COS[:], in_=cos_t), "cos", writes=["COS"])
            S.dma("sp", lambda e: e.dma_start(out=SIN[:], in_=sin_t), "sin", writes=["SIN"])
            build_tables(GQ, "GQ")
            Wv_, wkv_ = W.get(P_v)

            def v_post(i, t, rows, bank):
                S.op("act", lambda e: e.activation(out=V[0:rows, t, :], in_=ps[0:rows, bank, 0:256], func=AF.Copy),
                     reads=[("ps", bank)], writes=[("V", t)])
                if t >= 8:
                    S.op("dve", lambda e: e.tensor_copy(out=KRBS[0][0:rows, :], in_=ps[0:rows, bank, 0:256]),
                         reads=[("ps", bank), ("V", t)], writes=["KRB0"])
                    if t == 8:
                        S.dma("sp", lambda e: e.dma_start(out=vp_o, in_=KRBS[0][:, :]), "vp", reads=["KRB0"], is_output=True)
                    else:
                        for b in range(4):
                            S.dma("sp", lambda e, b=b: e.dma_start(out=vs_o[b, 120:128, :], in_=KRBS[0][b * 8:(b + 1) * 8, :]),
                                  ("vsn", b), reads=["KRB0"], is_output=True)
            proj_loop(Wv_, wkv_, ALL10, v_post)
            W.done(P_v)

            ckpt(6)
            QT3 = AT3[:, 0:4, :]
            OT3 = AT3[:, 4:8, :]
            KT2all = [("KT2", c) for c in [t * 128 for t in range(10)] + [1184 + b * 128 for b in range(4)]] + ALLT
            ptc = [0]
            TILES1 = list(range(1, 10))
            for g in range(4):
                qp, ao = P_att[g]
                Wq0, wkq0 = W.get(qp[0])
                Wq1, wkq1 = W.get(qp[1])

                def q_post_a(i, t, rows, bank, qh):
                    cb = i % 2
                    qk_chain(bank, rows, t, cb, QRB[cb][0:rows, :].rearrange("p (h d) -> p h d", d=64), ["QRB%d" % cb])

                def q_post_b(i, t, rows, qh):
                    cb = i % 2
                    qb = QRB[cb]
                    qbk = "QRB%d" % cb
                    tb = 4 + (trc[0] % 2)
                    trc[0] += 1

                    def tr(e):
                        for pr in range(2):
                            ins = e.transpose(out=psb[:, tb, pr * 128:pr * 128 + rows], in_=qb[0:rows, pr * 128:(pr + 1) * 128],
                                              identity=ident[0:rows, 0:rows])
                        return ins
                    S.op("pe", tr, reads=[qbk, "ident"], writes=[("ps", tb)])
                    lc = t * 128 - 128
                    S.op("act", lambda e: e.activation(
                        out=QT3[:, qh * 2:qh * 2 + 2, lc:lc + rows],
                        in_=psb[:, tb, 0:256].rearrange("p (k n) -> p k n", n=128)[:, :, 0:rows], func=AF.Copy),
                        reads=[("ps", tb)], writes=[("A", qh * 2), ("A", qh * 2 + 1)])
                proj_stream([(Wq0, wkq0, t, 0) for t in TILES1] + [(Wq1, wkq1, t, 1) for t in TILES1], q_post_a, q_post_b)
                W.done(qp[0])
                W.done(qp[1])
                QK_ = [("A", c) for c in range(4)]
                OK_ = [("A", c) for c in range(4, 8)]
                bufs = {}
                for n in range(1, 9):
                    bufs[n] = ptc[0] % 2
                    ptc[0] += 1

                def score_exp(n, kbi, g=g):
                    buf = bufs[n]
                    P4 = PT[buf].rearrange("p (k r n) -> p k r n", k=2, r=2)
                    lc = (n - 1) * 128
                    kt = (n - 1, n)[kbi]
                    mi = kbi if (kbi == 1 or n > 1) else 2

                    def st(e):
                        for par in range(2):
                            bank = kbi * 2 + par
                            ph = slice(par * 64, par * 64 + 64)
                            ins = e.matmul(ps[:, bank, :], lhsT=KT2[ph, g, kt * 128:(kt + 1) * 128], rhs=QT3[ph, :, lc:lc + 128],
                                           start=True, stop=True)
                        return ins
                    S.op("pe", st, reads=KT2all + QK_, writes=[("ps", kbi * 2), ("ps", kbi * 2 + 1)])
                    S.op("act", lambda e: e.activation(
                        out=P4[:, kbi], in_=ps[:, kbi * 2:kbi * 2 + 2, :], func=AF.Exp, bias=NEGM, scale=0.125),
                        reads=[("ps", kbi * 2), ("ps", kbi * 2 + 1), "NEGM"], writes=[PTK[buf][kbi]])
                    pm = P4[:, kbi].rearrange("p r (a q) -> p (r a) q", q=128)
                    S.op("pool" if kbi == 0 else "dve", lambda e: e.tensor_tensor(
                        out=pm, in0=pm, in1=mask01[:, mi, :].unsqueeze(1).broadcast_to([128, 8, 128]), op=ALU.mult),
                        reads=[PTK[buf][kbi], "mask01"], writes=[PTK[buf][kbi]])

                def pv_norm(n, g=g):
                    buf = bufs[n]
                    P4 = PT[buf].rearrange("p (k r n) -> p k r n", k=2, r=2)
                    lc = (n - 1) * 128
                    bo = 4 + 2 * (n % 2)
                    bd = bo + 1

                    def pv(e):
                        for par in range(2):
                            ph = slice(par * 64, par * 64 + 64)
                            for kbi, kt in enumerate((n - 1, n)):
                                e.matmul(ps[ph, bo, :], lhsT=V[:, kt, g * 64:(g + 1) * 64], rhs=P4[:, kbi, par, :],
                                         start=(kbi == 0), stop=(kbi == 1), tile_position=(0, par * 64))
                        for par in range(2):
                            ph = slice(par * 64, par * 64 + 64)
                            for kbi in range(2):
                                ins = e.matmul(ps[ph, bd, :], lhsT=ones64[:, :], rhs=P4[:, kbi, par, :],
                                               start=(kbi == 0), stop=(kbi == 1), tile_position=(0, par * 64))
                        return ins
                    S.op("pe", pv, reads=PTK[buf] + [("V", n - 1), ("V", n), "ones64"], writes=[("ps", bo), ("ps", bd)])

                RDS = [(RD, RDK), (ATT[:, 2304:2816], ["AB0", "AB1"])]

                def norm_add(n, g=g):
                    rd, rdk = RDS[n % 2]
                    bd = 4 + 2 * (n % 2) + 1
                    S.op("dve", lambda e: e.tensor_tensor(
                        out=rd.rearrange("p (a q) -> p a q", q=128), in0=ps[:, bd, :].rearrange("p (a q) -> p a q", q=128),
                        in1=SE[:, 4 * g:4 * g + 4].unsqueeze(2).broadcast_to([128, 4, 128]), op=ALU.add),
                        reads=[("ps", bd), "SEa", "SEb"], writes=rdk)

                def norm_fin(n, g=g):
                    rd, rdk = RDS[n % 2]
                    lc = (n - 1) * 128
                    bo = 4 + 2 * (n % 2)
                    S.op("act", lambda e: e.activation(out=rd, in_=rd, func=AF.Ln), reads=rdk, writes=rdk)
                    S.op("act", lambda e: e.activation(out=rd, in_=rd, func=AF.Exp, scale=-1.0), reads=rdk, writes=rdk)
                    S.op("dve", lambda e: e.tensor_tensor(
                        out=OT3[:, :, lc:lc + 128], in0=ps[:, bo, :].rearrange("p (a q) -> p a q", q=128),
                        in1=rd.rearrange("p (a q) -> p a q", q=128), op=ALU.mult),
                        reads=[("ps", bo)] + rdk, writes=OK_)

                for i in range(1, 11):
                    if i <= 8:
                        score_exp(i, 0)
                        score_exp(i, 1)
                    if 1 <= i - 1 <= 8:
                        pv_norm(i - 1)
                        norm_add(i - 1)
                    if 1 <= i - 2 <= 8:
                        norm_fin(i - 2)
                buf = ptc[0] % 2
                ptc[0] += 1
                PTc = PT[buf][:, 0:256].rearrange("p (r n) -> p r n", r=2)
                PTn = PT[buf][0:32, 256:512].rearrange("p (r n) -> p r n", r=2)

                def s_mm(e, g=g):
                    for par in range(2):
                        ph = slice(par * 64, par * 64 + 64)
                        e.matmul(ps[:, par, 0:128], lhsT=ident[:, :],
                                 rhs=maskb[:, 0, 0:8].unsqueeze(1).broadcast_to([128, 16, 8]), start=True, stop=False)
                        for b in range(4):
                            e.matmul(ps[:, par, b * 32:(b + 1) * 32], lhsT=KT2[ph, g, 1184 + b * 128:1184 + (b + 1) * 128],
                                     rhs=QT3[ph, :, 1024 + b * 8:1024 + (b + 1) * 8], start=False, stop=(b == 3))
                    for par in range(2):
                        ph = slice(par * 64, par * 64 + 64)
                        e.matmul(ps[0:32, 2 + par, 0:128], lhsT=KT2[ph, g, 1152:1184],
                                 rhs=QT3[ph, :, 1024:1056].rearrange("p a (b t) -> p b a t", t=8), start=True, stop=False)
                        ins = e.matmul(ps[0:32, 2 + par, 0:128], lhsT=ident[0:32, 0:32],
                                       rhs=msb[:, :].rearrange("p (b t) -> p b t", t=8).unsqueeze(2).broadcast_to([32, 4, 4, 8]),
                                       start=False, stop=True)
                    return ins
                S.op("pe", s_mm, reads=KT2all + QK_ + ["ident", "maskb", "msb"], writes=[("ps", b) for b in range(4)])
                S.op("act", lambda e, PTc=PTc: e.activation(out=PTc, in_=ps[:, 0:2, 0:128], func=AF.Exp, bias=NEGM, scale=0.125),
                     reads=[("ps", 0), ("ps", 1), "NEGM"], writes=PTK[buf])
                S.op("act", lambda e, PTn=PTn: e.activation(out=PTn, in_=ps[0:32, 2:4, 0:128], func=AF.Exp, bias=NEGM[0:32, :], scale=0.125),
                     reads=[("ps", 2), ("ps", 3), "NEGM"], writes=PTK[buf])

                def s_pv(e, PTc=PTc, PTn=PTn, g=g):
                    for bank, use_v in ((4, True), (5, False)):
                        for par in range(2):
                            ph = slice(par * 64, par * 64 + 64)
                            lhs_n = V[0:32, 9, g * 64:(g + 1) * 64] if use_v else ones64[0:32, :]
                            e.matmul(ps[ph, bank, 0:128], lhsT=lhs_n, rhs=PTn[:, par, :], start=True, stop=False,
                                     tile_position=(0, par * 64))
                            for b in range(4):
                                lhs_c = Vc[:, b, g * 64:(g + 1) * 64] if use_v else ones64[:, :]
                                ins = e.matmul(ps[ph, bank, b * 32:(b + 1) * 32],
                                               lhsT=lhs_c, rhs=PTc[:, par, b * 32:(b + 1) * 32],
                                               start=False, stop=(b == 3), tile_position=(0, par * 64))
                    return ins
                S.op("pe", s_pv, reads=PTK[buf] + [("V", 9), "VC", "ones64"], writes=[("ps", 4), ("ps", 5)])
                rds = RD[:, 0:128].rearrange("p (b a t) -> p b a t", b=4, t=8)
                S.op("dve", lambda e, rds=rds, g=g: e.tensor_tensor(
                    out=rds, in0=ps[:, 5, 0:128].rearrange("p (b a t) -> p b a t", b=4, t=8),
                    in1=SE[:, 4 * g:4 * g + 4].unsqueeze(1).unsqueeze(3).broadcast_to([128, 4, 4, 8]), op=ALU.add),
                    reads=[("ps", 5), "SEa", "SEb"], writes=RDK)
                S.op("dve", lambda e: e.reciprocal(out=RD[:, 0:128], in_=RD[:, 0:128]), reads=RDK, writes=RDK)
                S.op("dve", lambda e, rds=rds: e.tensor_tensor(
                    out=OT3[:, :, 1024:1056].rearrange("p a (b t) -> p b a t", t=8),
                    in0=ps[:, 4, 0:128].rearrange("p (b a t) -> p b a t", b=4, t=8), in1=rds, op=ALU.mult),
                    reads=[("ps", 4)] + RDK, writes=OK_)
                if g == 3:
                    NP3 = NormPipe(3, TILES1, 32, HB_X0)
                for hf in range(2):
                    Wa, wka = W.get(ao[hf])
                    for t in TILES1:
                        rows = 128 if t < 9 else 32
                        lc = t * 128 - 128
                        for sub in range(2):
                            bank = 6 + (dn_cnt[0] % 2)
                            dn_cnt[0] += 1
                            xg = hf * 2 + sub

                            def mm(e, Wa=Wa, rows=rows, lc=lc, bank=bank, sub=sub):
                                for c in range(4):
                                    ins = e.matmul(ps[0:rows, bank, :], lhsT=OT3[:, c, lc:lc + rows],
                                                   rhs=Wa[:, c, sub * 512:(sub + 1) * 512], start=(c == 0), stop=(c == 3))
                                return ins
                            S.op("pe", mm, reads=wka + OK_, writes=[("ps", bank)])
                            S.op("dve", lambda e, t=t, rows=rows, bank=bank, xg=xg: e.tensor_tensor(
                                out=X[0:rows, t, xg * 512:(xg + 1) * 512], in0=X[0:rows, t, xg * 512:(xg + 1) * 512],
                                in1=ps[0:rows, bank, :], op=ALU.add), reads=[("ps", bank), ("X", t, xg)], writes=[("X", t, xg)])
                        if g == 3 and hf == 1:
                            NP3.tile_ready(t)
                    W.done(ao[hf])

            ckpt(7)
            TG1 = [(128, 640), (640, 1152), (1152, 1184)]
            if not skip_mlp1:
                mlp(NP3, P_mlp1, TILES1, 128, 32, TG1)

        try:
            record()
        except _Stop:
            pass
        TILES1 = list(range(1, 10))
        for g4 in range(4):
            for t in TILES1:
                rows = 128 if t < 9 else 32
                S.dma("sp", lambda e, t=t, rows=rows, g4=g4: e.dma_start(
                    out=y_o[(t - 1) * 128:(t - 1) * 128 + rows, g4 * 512:(g4 + 1) * 512], in_=X[0:rows, t, g4 * 512:(g4 + 1) * 512]),
                    ("x", t), reads=[("X", t, g4)], is_output=True)

        with nc.allow_low_precision("bf16 matmul operands with fp32 PSUM accumulation"):
            S.emit(st)
    return nc


_PROGRAM = None


def _rope_cos_sin(pos):
    half = 32
    try:
        import jax
        import jax.numpy as jnp
        cpu = jax.devices("cpu")[0]
        with jax.default_device(cpu):
            inv = 10000.0 ** (-jnp.arange(half, dtype=jnp.float32) / half)
            ang = jnp.asarray(pos, dtype=jnp.float32)[..., None] * inv
            return np.asarray(jnp.cos(ang), dtype=np.float32), np.asarray(jnp.sin(ang), dtype=np.float32)
    except Exception:
        inv = (np.float32(10000.0) ** (-(np.arange(half, dtype=np.float32)) / np.float32(half))).astype(np.float32)
        ang = (np.asarray(pos, np.float32)[..., None] * inv).astype(np.float32)
        return np.cos(ang).astype(np.float32), np.sin(ang).astype(np.float32)


def _tables(s, first):
    pos = np.zeros((128, 10), np.float32)
    r = np.arange(128)
    for t in range(9):
        pos[:, t] = s - 128 + 128 * t + r
    pos[:32, 9] = PAST + (r[:32] % 8)
    c, sn = _rope_cos_sin(pos)
    cos2 = np.concatenate([c, c], axis=-1)
    sinm = np.concatenate([-sn, sn], axis=-1)
    j = np.arange(128)[:, None]
    i = np.arange(128)[None, :]
    masks = np.zeros((128, 3, 128), np.float32)
    NEG = -30000.0
    masks[:, 0, :] = np.where(j > i, 0.0, NEG)
    masks[:, 1, :] = np.where(j <= i, 0.0, NEG)
    masks[:, 2, :] = NEG if first else np.where(j > i, 0.0, NEG)
    jj = np.arange(32)[:, None]
    qq = np.arange(32)[None, :]
    mask_s = np.where((jj // 8 == qq // 8) & ((jj % 8) <= (qq % 8)), 0.0, NEG).astype(np.float32)
    return np.ascontiguousarray(cos2), np.ascontiguousarray(sinm), masks, mask_s


def make_in_maps(x_prompt, x_sample, state_conv, cache_k_win, cache_v_win, ln_mix, ln_mlp,
                 w_conv_in, w_conv, w_conv_out, w_qkv, w_attn_out, q_norm, k_norm, sinks, w_up, w_down):
    f = lambda a: np.ascontiguousarray(np.asarray(a, dtype=np.float32))
    x_prompt, x_sample, state_conv = f(x_prompt), f(x_sample), f(state_conv)
    cache_k_win, cache_v_win = f(cache_k_win), f(cache_v_win)
    shared = {
        "ln": f(np.stack([np.asarray(ln_mix)[0], np.asarray(ln_mlp)[0], np.asarray(ln_mix)[1], np.asarray(ln_mlp)[1]])),
        "w_in": f(np.asarray(w_conv_in)[0]),
        "w_out": f(np.asarray(w_conv_out)[0]),
        "w_qkv": f(np.asarray(w_qkv)[0]),
        "w_ao": f(np.asarray(w_attn_out)[0]),
        "qn": f(np.asarray(q_norm)[0:1]),
        "kn": f(np.asarray(k_norm)[0:1]),
        "sinks": f(np.asarray(sinks)[0:1]),
        "w_up": f(w_up),
        "w_dn": f(w_down),
    }
    wc = f(np.asarray(w_conv)[0])
    in_maps = []
    for c in range(NCORES):
        bi, qi = c // 4, c % 4
        s = qi * 1024
        xt = np.zeros((NT, D), np.float32)
        if qi > 0:
            xt[0:128] = x_prompt[bi, s - 128:s]
            xt[1184:1186] = x_prompt[bi, s - 130:s - 128]
        xt[128:1152] = x_prompt[bi, s:s + 1024]
        xt[1152:1184] = x_sample[4 * c:4 * c + 4].reshape(32, D)
        swc = np.concatenate([state_conv[0, 4 * c:4 * c + 4].reshape(8, D), wc], axis=0)
        cos2, sinm, masks, mask_s = _tables(s, qi == 0)
        m = dict(shared)
        m.update({
            "xtok": xt, "swc": np.ascontiguousarray(swc),
            "ck": np.ascontiguousarray(cache_k_win[0, 4 * c:4 * c + 4].reshape(4, 128, 256)),
            "cv": np.ascontiguousarray(cache_v_win[0, 4 * c:4 * c + 4].reshape(4, 128, 256)),
            "cos_t": cos2, "sin_t": sinm, "masks": masks, "mask_s": mask_s,
        })
        in_maps.append(m)
    return in_maps


def assemble(R):
    y_prompt = np.zeros((2, 4096, D), np.float32)
    y_sample = np.zeros((32, 8, D), np.float32)
    ncp = np.zeros((1, 2, 2, D), np.float32)
    ncs = np.zeros((1, 32, 2, D), np.float32)
    kp = np.zeros((1, 2, 128, 4, 64), np.float32)
    vp = np.zeros((1, 2, 128, 4, 64), np.float32)
    ks = np.zeros((1, 32, 128, 4, 64), np.float32)
    vs = np.zeros((1, 32, 128, 4, 64), np.float32)
    for c in range(NCORES):
        if R[c] is None:
            continue
        bi, qi = c // 4, c % 4
        s = qi * 1024
        r = R[c]
        y_prompt[bi, s:s + 1024] = r["y"][0:1024]
        y_sample[4 * c:4 * c + 4] = r["y"][1024:1056].reshape(4, 8, D)
        ncs[0, 4 * c:4 * c + 4] = r["ncv"][2:10].reshape(4, 2, D)
        ks[0, 4 * c:4 * c + 4] = r["ks"].reshape(4, 128, 4, 64)
        vs[0, 4 * c:4 * c + 4] = r["vs"].reshape(4, 128, 4, 64)
        if qi == 3:
            ncp[0, bi] = r["ncv"][0:2]
            kp[0, bi] = r["kp"].reshape(128, 4, 64)
            vp[0, bi] = r["vp"].reshape(128, 4, 64)
    return (y_prompt, y_sample, ncp, ncs, kp, vp, ks, vs)


def kernel(**inputs):
    global _PROGRAM
    if _PROGRAM is None:
        _PROGRAM = build_program()
    in_maps = make_in_maps(**inputs)
    res = run_bass_kernel_spmd(_PROGRAM, in_maps, core_ids=list(range(NCORES)))
    return assemble(res.results)
```

```python
import numpy as np
from contextlib import ExitStack
import concourse.bass as bass
import concourse.mybir as mybir
from concourse.bass_utils import run_bass_kernel_spmd

F32 = mybir.dt.float32
BF16 = mybir.dt.bfloat16
AF = mybir.ActivationFunctionType
ALU = mybir.AluOpType
AX = mybir.AxisListType

D = 2048
DFF = 8192
NT = 1186
EPS = 1e-6
NCORES = 8
PAST = 16384


class _Op:
    __slots__ = ("eng", "fn", "deps", "signal", "key", "tick", "is_dma")

    def __init__(self, eng, fn, key, is_dma):
        self.eng = eng
        self.fn = fn
        self.deps = []
        self.signal = is_dma
        self.key = key
        self.tick = 0
        self.is_dma = is_dma


class Sched:
    ENGS = ("pe", "act", "dve", "pool", "sp")

    def __init__(self, nc, same_engine_sync=("act", "dve", "pool")):
        self.nc = nc
        self.streams = {e: [] for e in self.ENGS}
        self.last_writer = {}
        self.readers = {}
        self.same_sync = set(same_engine_sync)
        self.dma_counts = {}
        self.out_dmas = []

    def _add(self, op, reads, writes):
        deps = {}
        for r in reads:
            w = self.last_writer.get(r)
            if w is not None:
                deps[id(w)] = w
        for r in writes:
            w = self.last_writer.get(r)
            if w is not None:
                deps[id(w)] = w
            for rd in self.readers.get(r, ()):
                deps[id(rd)] = rd
        op.deps = list(deps.values())
        for r in reads:
            self.readers.setdefault(r, []).append(op)
        for r in writes:
            self.last_writer[r] = op
            self.readers[r] = []
        self.streams[op.eng].append(op)
        return op

    def op(self, eng, fn, reads=(), writes=()):
        return self._add(_Op(eng, fn, eng, False), reads, writes)

    def dma(self, eng, fn, slot, reads=(), writes=(), is_output=False):
        op = _Op(eng, fn, ("dma", slot), True)
        c = self.dma_counts.get(slot, 0) + 16
        self.dma_counts[slot] = c
        op.tick = c
        self._add(op, reads, writes)
        self.out_dmas.append(op)
        return op

    def finalize(self):
        fin = _Op("sp", None, "sp", False)
        fin.deps = list(self.out_dmas)
        self.streams["sp"].append(fin)
        for e in self.ENGS:
            for op in self.streams[e]:
                for d in op.deps:
                    if d.is_dma:
                        continue
                    if d.eng == op.eng and not op.is_dma and d.eng not in self.same_sync:
                        continue
                    d.signal = True
        for e in self.ENGS:
            t = 0
            for op in self.streams[e]:
                if op.is_dma:
                    continue
                if op.signal:
                    t += 1
                    op.tick = t

    def emit(self, stack):
        nc = self.nc
        self.finalize()
        sems = {}
        import os
        for i in range(int(os.environ.get("SEM_PAD", "0"))):
            stack.enter_context(nc.semaphore("pad%d" % i))
        for e in self.ENGS:
            sems[e] = stack.enter_context(nc.semaphore("s_" + e))
        for i, slot in enumerate(self.dma_counts):
            sems[("dma", slot)] = stack.enter_context(nc.semaphore("d%d" % i))
        block = stack.enter_context(nc.Block())
        sched = self

        def run(ename, eng):
            waited = {}
            for op in sched.streams[ename]:
                need = {}
                for d in op.deps:
                    if (not d.is_dma) and d.eng == ename and (not op.is_dma) and ename not in sched.same_sync:
                        continue
                    k = d.key
                    if need.get(k, 0) < d.tick:
                        need[k] = d.tick
                for k, t in need.items():
                    if waited.get(k, 0) >= t:
                        continue
                    eng.wait_ge(sems[k], t)
                    waited[k] = t
                if op.fn is None:
                    continue
                ins = op.fn(eng)
                if op.is_dma:
                    ins.then_inc(sems[op.key], 16)
                elif op.signal:
                    ins.then_inc(sems[ename], 1)

        @block.tensor
        def _(e):
            run("pe", e)

        @block.scalar
        def _(e):
            run("act", e)

        @block.vector
        def _(e):
            run("dve", e)

        @block.gpsimd
        def _(e):
            run("pool", e)

        @block.sync
        def _(e):
            run("sp", e)


UNIT = 2048
NUNITS = 6


class WStream:
    def __init__(self, S, ring):
        self.S = S
        self.ring = ring
        self.pieces = []
        self.views = {}
        self.next_issue = 0
        self.ptr = 0
        self.free = [True] * NUNITS
        self.units_of = {}
        self.hold_keys = []

    def add(self, src, nunits, a, b):
        self.pieces.append((src, nunits, a, b))
        return len(self.pieces) - 1

    def _try_issue(self):
        if self.next_issue >= len(self.pieces):
            return False
        src, nu, a, b = self.pieces[self.next_issue]
        p = self.ptr
        if p + nu > NUNITS:
            p = 0
        if not all(self.free[p:p + nu]):
            return False
        for u in range(p, p + nu):
            self.free[u] = False
        pid = self.next_issue
        self.units_of[pid] = (p, nu)
        view = self.ring[:, p * UNIT:p * UNIT + a * b].rearrange("p (a b) -> p a b", b=b)
        keys = [("ring", u) for u in range(p, p + nu)]
        self.views[pid] = (view, keys)
        extra = self.hold_keys if (2 <= pid < 6) else []
        self.S.dma("pool", lambda e, view=view, src=src: e.dma_start(out=view, in_=src),
                   ("ring", p), reads=extra, writes=keys)
        self.ptr = p + nu
        self.next_issue += 1
        return True

    def get(self, pid):
        while self._try_issue():
            pass
        assert pid in self.views, "ring deadlock: piece %d not issued" % pid
        return self.views[pid]

    def done(self, pid):
        p, nu = self.units_of[pid]
        for u in range(p, p + nu):
            self.free[u] = True
        while self._try_issue():
            pass


class _Stop(Exception):
    pass


def build_program(skip_mlp1=False, stage=99):
    nc = bass.Bass("TRN2", target_bir_lowering=False)

    def din(name, shape):
        return nc.dram_tensor(name, shape, F32, kind="ExternalInput").ap()

    def dout(name, shape):
        return nc.dram_tensor(name, shape, F32, kind="ExternalOutput").ap()

    xtok = din("xtok", [NT, D])
    swc = din("swc", [11, D])
    ck = din("ck", [4, 128, 256])
    cv = din("cv", [4, 128, 256])
    ln = din("ln", [4, D])
    w_in = din("w_in", [D, 3 * D])
    w_out = din("w_out", [D, D])
    w_qkv = din("w_qkv", [D, 2560])
    w_ao = din("w_ao", [D, D])
    qn = din("qn", [1, 64])
    kn = din("kn", [1, 64])
    sinks = din("sinks", [1, 32])
    w_up = din("w_up", [2, D, DFF])
    w_dn = din("w_dn", [2, DFF, D])
    cos_t = din("cos_t", [128, 10, 64])
    sin_t = din("sin_t", [128, 10, 64])
    masks = din("masks", [128, 3, 128])
    mask_s = din("mask_s", [32, 32])

    y_o = dout("y", [1056, D])
    ncv_o = dout("ncv", [10, D])
    kp_o = dout("kp", [128, 256])
    vp_o = dout("vp", [128, 256])
    ks_o = dout("ks", [4, 128, 256])
    vs_o = dout("vs", [4, 128, 256])

    with ExitStack() as st:
        def sb(name, shape, dt):
            return st.enter_context(nc.sbuf_tensor(name, shape, dt))

        X = sb("X", [128, 10, D], F32)
        HT = sb("HT", [128, 16 * NT], BF16)
        ACT_T = sb("ACT_T", [128, 8 * NT], BF16)
        RING = sb("RING", [128, NUNITS * UNIT], BF16)
        TMP = sb("TMP", [128, 3600], F32)
        ATT = sb("ATT", [128, 4352], F32)
        identf = sb("identf", [128, 128], F32)
        ident = sb("ident", [128, 128], BF16)
        ones64 = sb("ones64", [128, 64], BF16)
        maskb = sb("maskb", [128, 3, 128], BF16)
        mask01 = sb("mask01", [128, 3, 128], BF16)
        msb = sb("msb", [32, 32], BF16)
        COS = sb("COS", [128, 10, 64], F32)
        SIN = sb("SIN", [128, 10, 64], F32)
        SWT = sb("SWT", [128, 16, 11], F32)
        UK = sb("UK", [128, 16, 10], F32)
        SS = sb("SS", [128, 16], F32)
        GQ = sb("GQ", [128, 64], F32)
        GK = sb("GK", [128, 64], F32)
        SK = sb("SK", [128, 32], F32)
        SE0 = sb("SE0", [128, 32], F32)
        SE = sb("SE", [128, 16], F32)
        SM = sb("SM", [128, 8], F32)
        EPSB = sb("EPSB", [128, 1], F32)
        S4T = sb("S4T", [128, 8], F32)
        JUNK = sb("JUNK", [128, 256], BF16)
        ps = st.enter_context(nc.psum_tensor("ps", [128, 8, 512], F32))
        psb = ps.bitcast(BF16)

        HT3 = HT[:].rearrange("p (c n) -> p c n", n=NT)
        AT3 = ACT_T[:].rearrange("p (c n) -> p c n", n=NT)
        T0 = TMP[:, 0:1200]
        T1 = TMP[:, 1200:2400]
        T2 = TMP[:, 2400:3600]
        GB = TMP[:, 0:2048]
        KT2 = TMP[:].bitcast(BF16)[:, 0:4 * 1696].rearrange("p (k n) -> p k n", n=1696)
        ALLT = ["T0", "T1", "T1h", "T1s", "T1p", "T2", "T2s"]
        ATTb = ATT[:].bitcast(BF16)
        V = ATTb[:, 0:2560].rearrange("p (t n) -> p t n", n=256)
        Vc = ATTb[:, 2560:3584].rearrange("p (b n) -> p b n", n=256)
        XS = [ATT[:, 1792:2048], ATT[:, 2048:2304]]
        AB = [ATT[:, 2304:2560], ATT[:, 2560:2816]]
        BB = [ATT[:, 2816:3072], ATT[:, 3072:3328]]
        RD = ATT[:, 1792:2304]
        RDK = ["XS0", "XS1"]
        QRB = [ATTb[:, 6656:6912], ATTb[:, 6912:7168]]
        KDUP = ATTb[:, 7168:7680]
        KRBS = [ATT[:, 3840:4096], ATT[:, 4096:4352]]
        CKB = ATTb[:, 4608:5632].rearrange("p (b n) -> p b n", n=256)
        CKBK = ["AB0", "AB1"]
        X0b = X[:, 0, :].bitcast(BF16)
        PT = [X0b[:, 0:2048], X0b[:, 2048:4096]]
        PTK = [[("X", 0, 0), ("X", 0, 1)], [("X", 0, 2), ("X", 0, 3)]]
        HB_ATT = ([ATTb[:, 0:2048], ATTb[:, 2048:4096], ATTb[:, 4096:6144]],
                  [[("V", t) for t in range(8)], [("V", 8), ("V", 9), "VC", "XS0"], ["XS1", "AB0", "AB1", "BB0"]])
        HB_X0 = (PT, PTK)

        S = Sched(nc)
        W = WStream(S, RING[:])
        W.hold_keys = [("X", 9, 0)]

        def XK(t):
            return [("X", t, g) for g in range(4)]

        HK = [("H", t) for t in range(10)]

        def wv(src, c0, ncol):
            return src[:, c0:c0 + ncol].rearrange("(k p) n -> p k n", p=128)

        def wr(src, r0, nk, c0, ncol):
            return src[r0:r0 + nk * 128, c0:c0 + ncol].rearrange("(k p) n -> p k n", p=128)

        P_conv = []
        for gi in range(2):
            up = []
            for j in range(8):
                J = gi * 8 + j
                up.append((W.add(wv(w_in, 2048 + J * 128, 128), 1, 16, 128),
                           W.add(wv(w_in, 4096 + J * 128, 128), 1, 16, 128),
                           W.add(wv(w_in, J * 128, 128), 1, 16, 128)))
            dn = [W.add(wr(w_out, gi * 1024, 8, ng * 512, 512), 2, 8, 512) for ng in range(4)]
            P_conv.append((up, dn))

        def plan_mlp(l):
            out = []
            for gi in range(8):
                up = [W.add(wv(w_up[l], (gi * 8 + 2 * pi) * 128, 256), 2, 16, 256) for pi in range(4)]
                dn = [W.add(wr(w_dn[l], gi * 1024, 8, ng * 512, 512), 2, 8, 512) for ng in range(4)]
                out.append((up, dn))
            return out

        P_mlp0 = plan_mlp(0)
        P_k = W.add(wv(w_qkv, 2048, 256), 2, 16, 256)
        P_v = W.add(wv(w_qkv, 2304, 256), 2, 16, 256)
        P_att = []
        for g in range(4):
            qp = [W.add(wv(w_qkv, g * 512 + qh * 256, 256), 2, 16, 256) for qh in range(2)]
            ao = [W.add(wr(w_ao, g * 512, 4, hf * 1024, 1024), 2, 4, 1024) for hf in range(2)]
            P_att.append((qp, ao))
        P_mlp1 = plan_mlp(1)

        S.dma("sp", lambda e: e.dma_start(out=GB, in_=ln[0:1, :].partition_broadcast(128)), "gb", writes=["T0", "T1", "T1h"])
        for t in range(10):
            rows = 128 if t < 9 else 34
            S.dma("sp", lambda e, t=t, rows=rows: e.dma_start(out=X[0:rows, t, :], in_=xtok[t * 128:t * 128 + rows, :]),
                  ("x", t), writes=XK(t))
        S.dma("sp", lambda e: e.dma_start(out=X[64:75, 9, :], in_=swc), "swc", writes=["X9hi"])
        S.dma("sp", lambda e: e.dma_start(out=COS[:], in_=cos_t), "cos", writes=["COS"])
        S.dma("sp", lambda e: e.dma_start(out=SIN[:], in_=sin_t), "sin", writes=["SIN"])
        S.dma("sp", lambda e: e.dma_start(out=GQ[:], in_=qn.partition_broadcast(128)), "gq", writes=["GQ"])
        S.dma("sp", lambda e: e.dma_start(out=GK[:], in_=kn.partition_broadcast(128)), "gk", writes=["GK"])
        S.dma("sp", lambda e: e.dma_start(out=SK[:], in_=sinks.partition_broadcast(128)), "sk", writes=["SK"])
        S.dma("pool", lambda e: e.dma_start(out=maskb[:], in_=masks), "maskb", writes=["maskb"])
        S.dma("pool", lambda e: e.dma_start(out=msb[:], in_=mask_s), "msb", writes=["msb"])
        S.op("pool", lambda e: e.memset(identf[:], 1.0), writes=["identf"])
        S.op("pool", lambda e: e.affine_select(out=identf[:], in_=identf[:], pattern=[[-1, 128]],
                                               compare_op=ALU.is_equal, fill=0.0, base=0, channel_multiplier=1),
             reads=["identf"], writes=["identf"])
        S.op("dve", lambda e: e.tensor_copy(out=ident[:], in_=identf[:]), reads=["identf"], writes=["ident"])
        S.op("dve", lambda e: e.tensor_scalar(out=mask01[:], in0=maskb[:], scalar1=0.0, scalar2=None, op0=ALU.is_equal),
             reads=["maskb"], writes=["mask01"])
        S.op("dve", lambda e: e.memset(ones64[:], 1.0), writes=["ones64"])
        S.op("dve", lambda e: e.memset(EPSB[:], EPS), writes=["EPSB"])

        norm_cnt = [0]

        class NormPipe:
            def __init__(self, li, tiles, rows9, hbsel, gb_loaded=False):
                Hb, HbK = hbsel
                self.overlapped = False
                if not gb_loaded:
                    S.dma("sp", lambda e: e.dma_start(out=GB, in_=ln[li:li + 1, :].partition_broadcast(128)),
                          "gb", writes=["T0", "T1", "T1h"])
                self.info = []
                for t in tiles:
                    i = norm_cnt[0]
                    norm_cnt[0] += 1
                    self.info.append((t, 128 if t < 9 else rows9, Hb[i % len(Hb)], HbK[i % len(Hb)], i % 16, (i % 2) * 2))
                self.nbuf = len(Hb)
                self.idx = {t: k for k, t in enumerate(tiles)}
                self.na = 0
                self.nb = 0

            def _a(self, t, rows, hb, hk, col, b0):
                S.op("act", lambda e: e.activation(
                    out=hb[0:rows, :], in_=X[0:rows, t, :], func=AF.Square, accum_out=SS[0:rows, col:col + 1]),
                    reads=XK(t), writes=hk + [("SS", col)])
                S.op("act", lambda e: e.activation(
                    out=SS[0:rows, col:col + 1], in_=SS[0:rows, col:col + 1], func=AF.Sqrt,
                    bias=EPSB[0:rows, :], scale=1.0 / D), reads=[("SS", col), "EPSB"], writes=[("SS", col)])
                S.op("dve", lambda e: e.reciprocal(out=SS[0:rows, col:col + 1], in_=SS[0:rows, col:col + 1]),
                     reads=[("SS", col)], writes=[("SS", col)])
                S.op("dve", lambda e: e.scalar_tensor_tensor(
                    out=hb[0:rows, :], in0=X[0:rows, t, :], scalar=SS[0:rows, col:col + 1], in1=GB[0:rows, :],
                    op0=ALU.mult, op1=ALU.mult), reads=XK(t) + [("SS", col), "T0", "T1", "T1h"], writes=hk)

            def _b(self, t, rows, hb, hk, col, b0):
                def tr(e):
                    for c in range(16):
                        ins = e.transpose(out=psb[:, b0 + c // 8, (c % 8) * 128:(c % 8) * 128 + rows],
                                          in_=hb[0:rows, c * 128:(c + 1) * 128], identity=ident[0:rows, 0:rows])
                    return ins
                S.op("pe", tr, reads=hk + ["ident"], writes=[("ps", b0), ("ps", b0 + 1)])
                for half in range(2):
                    src = psb[:, b0 + half, :].rearrange("p (c n) -> p c n", n=128)[:, :, 0:rows]
                    dst = HT3[:, half * 8:(half + 1) * 8, t * 128:t * 128 + rows]
                    if half == 0 or self.overlapped:
                        S.op("act", lambda e, src=src, dst=dst: e.activation(out=dst, in_=src, func=AF.Copy),
                             reads=[("ps", b0 + half)], writes=[("H", t, half)])
                    else:
                        S.op("dve", lambda e, src=src, dst=dst: e.tensor_copy(out=dst, in_=src),
                             reads=[("ps", b0 + half)], writes=[("H", t, half)])

            def tile_ready(self, t):
                if t not in self.idx:
                    return
                assert self.idx[t] == self.na
                self.overlapped = True
                if self.nbuf >= 3:
                    self._a(*self.info[self.na])
                    self.na += 1
                    if self.na >= 3:
                        self._b(*self.info[self.nb])
                        self.nb += 1
                else:
                    if self.na >= 2:
                        self._b(*self.info[self.nb])
                        self.nb += 1
                    self._a(*self.info[self.na])
                    self.na += 1

            def finish(self):
                self.overlapped = False
                n = len(self.info)
                while self.nb < n:
                    if self.na < n and (self.na - self.nb) < 2:
                        self._a(*self.info[self.na])
                        self.na += 1
                        continue
                    self._b(*self.info[self.nb])
                    self.nb += 1

        def HKall(tiles):
            return [("H", t, h) for t in tiles for h in range(2)]

        def fm_mm(Wv, c_lo, bank0, tgs):
            def f(e):
                for k in range(16):
                    for gi, (c0, c1) in enumerate(tgs):
                        ins = e.matmul(ps[:, bank0 + gi, 0:c1 - c0], lhsT=Wv[:, k, c_lo:c_lo + 128],
                                       rhs=HT3[:, k, c0:c1], start=(k == 0), stop=(k == 15))
                return ins
            return f

        def flat(bank0, n):
            return ps[:, bank0:bank0 + 3, :].rearrange("p b n -> p (b n)")[:, 0:n]

        def BK(bank0):
            return [("ps", bank0), ("ps", bank0 + 1), ("ps", bank0 + 2)]

        dn_cnt = [0]

        def down_phase(pieces, nk, tiles, colbase, rows9, akeys, hook=None):
            def grp(Wv, wk, ngi, t):
                rows = 128 if t < 9 else rows9
                lc = t * 128 - colbase
                bank = 6 + (dn_cnt[0] % 2)
                dn_cnt[0] += 1

                def mk(c0, c1):
                    def mm(e):
                        for c in range(c0, c1):
                            ins = e.matmul(ps[0:rows, bank, :], lhsT=AT3[:, c, lc:lc + rows],
                                           rhs=Wv[:, c, :], start=(c == 0), stop=(c == nk - 1))
                        return ins
                    return mm

                def add():
                    S.op("dve", lambda e: e.tensor_tensor(
                        out=X[0:rows, t, ngi * 512:(ngi + 1) * 512], in0=X[0:rows, t, ngi * 512:(ngi + 1) * 512],
                        in1=ps[0:rows, bank, :], op=ALU.add), reads=[("ps", bank), ("X", t, ngi)], writes=[("X", t, ngi)])
                return mk, add, bank

            def one(Wv, wk, ngi, t):
                mk, add, bank = grp(Wv, wk, ngi, t)
                S.op("pe", mk(0, nk), reads=wk + akeys, writes=[("ps", bank)])
                add()

            def first_two(Wv, wk, ngi, t0, t1):
                mk0, add0, b0 = grp(Wv, wk, ngi, t0)
                mk1, add1, b1 = grp(Wv, wk, ngi, t1)
                S.op("pe", mk0(0, nk - 1), reads=wk + akeys[:nk - 1], writes=[("ps", b0)])
                S.op("pe", mk1(0, nk - 1), reads=wk + akeys[:nk - 1], writes=[("ps", b1)])
                S.op("pe", mk0(nk - 1, nk), reads=wk + akeys, writes=[("ps", b0)])
                add0()
                S.op("pe", mk1(nk - 1, nk), reads=wk + akeys, writes=[("ps", b1)])
                add1()

            first = pieces if hook is None else pieces[:-2]
            for ngi, pid in enumerate(first):
                Wv, wk = W.get(pid)
                rest = tiles
                if ngi == 0 and len(tiles) >= 2:
                    first_two(Wv, wk, ngi, tiles[0], tiles[1])
                    rest = tiles[2:]
                for t in rest:
                    one(Wv, wk, ngi, t)
                W.done(pid)
            if hook is not None:
                n0 = len(pieces) - 2
                Wa_, wka_ = W.get(pieces[n0])
                Wb_, wkb_ = W.get(pieces[n0 + 1])
                for t in tiles:
                    one(Wa_, wka_, n0, t)
                    one(Wb_, wkb_, n0 + 1, t)
                    hook(t)
                W.done(pieces[n0])
                W.done(pieces[n0 + 1])

        def ckpt(k):
            if k > stage:
                raise _Stop()

        def record():
            TG0 = [(0, 512), (512, 1024), (1024, NT)]
            ALL10 = list(range(10))
            NormPipe(0, ALL10, 34, HB_ATT, gb_loaded=True).finish()
            def swt_mm(e):
                for c in range(16):
                    ins = e.matmul(ps[:, 0, c * 11:(c + 1) * 11], lhsT=X[64:75, 9, c * 128:(c + 1) * 128],
                                   rhs=identf[64:75, 64:75], start=True, stop=True)
                return ins
            S.op("pe", swt_mm, reads=["X9hi", "identf"], writes=[("ps", 0)])
            S.op("act", lambda e: e.activation(out=SWT[:].rearrange("p c n -> p (c n)"), in_=ps[:, 0, 0:176], func=AF.Copy),
                 reads=[("ps", 0)], writes=["SWT"])
            ckpt(2)
            HA = HKall(ALL10)
            ubuf = T1
            ubs = T1[:, 1154:1194].rearrange("p (b n) -> p b n", n=10)
            t1 = T2
            csb = T0
            setc = [0]

            def nextset():
                s = (setc[0] % 2) * 3
                setc[0] += 1
                return s

            for gi in range(2):
                up, dn = P_conv[gi]
                for j in range(8):
                    J = gi * 8 + j
                    pc, pv, pb = up[j]
                    w0 = SWT[:, J, 8:9]
                    w1 = SWT[:, J, 9:10]
                    w2 = SWT[:, J, 10:11]
                    Wc, wkc = W.get(pc)
                    sA = nextset()
                    S.op("pe", fm_mm(Wc, 0, sA, TG0), reads=wkc + HA, writes=BK(sA))
                    W.done(pc)
                    S.op("act", lambda e, sA=sA: e.activation(out=csb[:, 0:NT], in_=flat(sA, NT), func=AF.Copy),
                         reads=BK(sA), writes=["T0"])
                    Wvv, wkv = W.get(pv)
                    sB = nextset()
                    S.op("pe", fm_mm(Wvv, 0, sB, TG0), reads=wkv + HA, writes=BK(sB))
                    W.done(pv)
                    S.op("dve", lambda e, sB=sB: e.tensor_tensor(out=ubuf[:, 2:1154], in0=csb[:, 0:1152], in1=flat(sB, NT)[:, 0:1152], op=ALU.mult),
                         reads=BK(sB) + ["T0"], writes=["T1"])
                    S.op("dve", lambda e, sB=sB: e.tensor_tensor(
                        out=ubs[:, :, 2:10], in0=csb[:, 1152:1184].rearrange("p (b n) -> p b n", n=8),
                        in1=flat(sB, NT)[:, 1152:1184].rearrange("p (b n) -> p b n", n=8), op=ALU.mult),
                        reads=BK(sB) + ["T0"], writes=["T1s"])
                    S.op("dve", lambda e, sB=sB: e.tensor_tensor(out=ubuf[:, 0:2], in0=csb[:, 1184:1186], in1=flat(sB, NT)[:, 1184:1186], op=ALU.mult),
                         reads=BK(sB) + ["T0"], writes=["T1h"])
                    S.op("pool", lambda e, J=J: e.tensor_copy(out=ubs[:, :, 0:2], in_=SWT[:, J, 0:8].rearrange("p (b n) -> p b n", n=2)),
                         reads=["SWT"], writes=["T1p"])
                    S.op("pool", lambda e, J=J: e.tensor_copy(out=UK[:, J, 0:2], in_=ubuf[:, 1152:1154]), reads=["T1"], writes=[("UK", J, 0)])
                    S.op("pool", lambda e, J=J: e.tensor_copy(out=UK[:, J, 2:10].rearrange("p (b n) -> p b n", n=2), in_=ubs[:, :, 8:10]),
                         reads=["T1s"], writes=[("UK", J, 1)])
                    UR = ["T1", "T1h"]
                    S.op("dve", lambda e, w0=w0: e.tensor_scalar(out=t1[:, 0:1152], in0=ubuf[:, 0:1152], scalar1=w0, scalar2=None, op0=ALU.mult),
                         reads=UR + ["SWT"], writes=["T2"])
                    S.op("dve", lambda e, w1=w1: e.scalar_tensor_tensor(out=t1[:, 0:1152], in0=ubuf[:, 1:1153], scalar=w1, in1=t1[:, 0:1152], op0=ALU.mult, op1=ALU.add),
                         reads=UR + ["T2"], writes=["T2"])
                    S.op("dve", lambda e, w2=w2: e.scalar_tensor_tensor(out=t1[:, 0:1152], in0=ubuf[:, 2:1154], scalar=w2, in1=t1[:, 0:1152], op0=ALU.mult, op1=ALU.add),
                         reads=UR + ["T2"], writes=["T2"])
                    t1s = t1[:, 1152:1184].rearrange("p (b n) -> p b n", n=8)
                    USR = ["T1s", "T1p"]
                    S.op("dve", lambda e, w0=w0, t1s=t1s: e.tensor_scalar(out=t1s, in0=ubs[:, :, 0:8], scalar1=w0, scalar2=None, op0=ALU.mult),
                         reads=USR + ["SWT"], writes=["T2s"])
                    S.op("dve", lambda e, w1=w1, t1s=t1s: e.scalar_tensor_tensor(out=t1s, in0=ubs[:, :, 1:9], scalar=w1, in1=t1s, op0=ALU.mult, op1=ALU.add),
                         reads=USR + ["T2s"], writes=["T2s"])
                    S.op("dve", lambda e, w2=w2, t1s=t1s: e.scalar_tensor_tensor(out=t1s, in0=ubs[:, :, 2:10], scalar=w2, in1=t1s, op0=ALU.mult, op1=ALU.add),
                         reads=USR + ["T2s"], writes=["T2s"])
                    Wb, wkb = W.get(pb)
                    sC = nextset()
                    S.op("pe", fm_mm(Wb, 0, sC, TG0), reads=wkb + HA, writes=BK(sC))
                    W.done(pb)
                    S.op("dve", lambda e, sC=sC, j=j: e.tensor_tensor(out=AT3[:, j, 0:1184], in0=flat(sC, NT)[:, 0:1184], in1=t1[:, 0:1184], op=ALU.mult),
                         reads=BK(sC) + ["T2", "T2s"], writes=[("A", j)])
                if gi == 0:
                    down_phase(dn, 8, ALL10, 0, 32, [("A", c) for c in range(8)])
                else:
                    def uk_mm(e):
                        for c in range(16):
                            ins = e.matmul(ps[0:10, c // 4, (c % 4) * 128:(c % 4 + 1) * 128], lhsT=UK[:, c, :], rhs=identf[:, :],
                                           start=True, stop=True)
                        return ins
                    S.op("pe", uk_mm, reads=[("UK", J, h) for J in range(16) for h in range(2)] + ["identf"],
                         writes=[("ps", b) for b in range(4)])
                    S.op("act", lambda e: e.activation(out=TMP[0:10, 0:2048], in_=ps[0:10, 0:4, :].rearrange("p b n -> p (b n)"), func=AF.Copy),
                         reads=[("ps", b) for b in range(4)], writes=["T0", "T1", "T1h"])
                    S.dma("sp", lambda e: e.dma_start(out=ncv_o, in_=TMP[0:10, 0:2048]), "ncv", reads=["T0", "T1", "T1h"], is_output=True)


                    NP1 = NormPipe(1, ALL10, 32, HB_ATT)
                    down_phase(dn, 8, ALL10, 0, 32, [("A", c) for c in range(8)], hook=NP1.tile_ready)
            ckpt(3)

            def mlp(NP, plan, tiles, colbase, rows9, tgs, next_norm=None):
                NP.finish()
                NPn = None
                HA_ = HKall(tiles)
                ncols = tgs[-1][1] - tgs[0][0]
                rcnt = 0
                for gi in range(8):
                    up, dn = plan[gi]
                    for pi in range(4):
                        Wv, wk = W.get(up[pi])
                        for cl in range(2):
                            c = pi * 2 + cl
                            sA = nextset()
                            S.op("pe", fm_mm(Wv, cl * 128, sA, tgs), reads=wk + HA_, writes=BK(sA))
                            rt = (T0, T1)[rcnt % 2]
                            rk = (["T0"], ["T1", "T1h", "T1s", "T1p"])[rcnt % 2]
                            rcnt += 1
                            S.op("act", lambda e, sA=sA, rt=rt: e.activation(out=rt[:, 0:ncols], in_=flat(sA, ncols), func=AF.Relu),
                                 reads=BK(sA), writes=rk)
                            S.op("pool", lambda e, rt=rt, c=c: e.tensor_tensor(out=AT3[:, c, 0:ncols], in0=rt[:, 0:ncols], in1=rt[:, 0:ncols], op=ALU.mult),
                                 reads=rk, writes=[("A", c)])
                        W.done(up[pi])
                    if gi == 7 and next_norm is not None:
                        NPn = NormPipe(*next_norm)
                        down_phase(dn, 8, tiles, colbase, rows9, [("A", c) for c in range(8)], hook=NPn.tile_ready)
                    else:
                        down_phase(dn, 8, tiles, colbase, rows9, [("A", c) for c in range(8)])
                return NPn

            ckpt(4)
            NP2 = mlp(NP1, P_mlp0, ALL10, 0, 32, TG0, next_norm=(2, ALL10, 32, HB_ATT))
            ckpt(5)

            NP2.finish()
            HA1 = HKall(ALL10)

            S.op("dve", lambda e: e.tensor_reduce(out=SM[:, 0:1], in_=GQ[:], axis=AX.X, op=ALU.max, apply_absolute_value=True), reads=["GQ"], writes=["SM0"])
            S.op("dve", lambda e: e.tensor_reduce(out=SM[:, 1:2], in_=GK[:], axis=AX.X, op=ALU.max, apply_absolute_value=True), reads=["GK"], writes=["SM1"])
            S.op("dve", lambda e: e.tensor_reduce(out=SM[:, 2:3], in_=SK[:], axis=AX.X, op=ALU.max), reads=["SK"], writes=["SM2"])
            S.op("dve", lambda e: e.tensor_tensor(out=SM[:, 3:4], in0=SM[:, 0:1], in1=SM[:, 1:2], op=ALU.mult), reads=["SM0", "SM1"], writes=["SM3"])
            S.op("dve", lambda e: e.scalar_tensor_tensor(out=SM[:, 4:5], in0=SM[:, 3:4], scalar=8.0, in1=SM[:, 2:3], op0=ALU.mult, op1=ALU.max),
                 reads=["SM3", "SM2"], writes=["SM4"])
            S.op("dve", lambda e: e.tensor_scalar(out=SM[:, 5:6], in0=SM[:, 4:5], scalar1=-1.0, scalar2=None, op0=ALU.mult), reads=["SM4"], writes=["NEGM"])
            NEGM = SM[:, 5:6]
            S.op("act", lambda e: e.activation(out=SE0[:], in_=SK[:], func=AF.Exp, bias=NEGM, scale=1.0), reads=["SK", "NEGM"], writes=["SE0"])
            SE0v = SE0[:].rearrange("p (a b) -> p a b", b=2)
            S.op("dve", lambda e: e.tensor_copy(out=SE[0:64, :], in_=SE0v[0:64, :, 0]), reads=["SE0"], writes=["SEa"])
            S.op("dve", lambda e: e.tensor_copy(out=SE[64:128, :], in_=SE0v[64:128, :, 1]), reads=["SE0"], writes=["SEb"])

            ckpt(5.1)
            S.dma("pool", lambda e: e.dma_start(out=CKB, in_=ck.rearrange("b k n -> k b n")), "ckb", writes=CKBK)
            S.dma("pool", lambda e: e.dma_start(out=Vc, in_=cv.rearrange("b k n -> k b n")), "vc", writes=["VC"])
            S.dma("sp", lambda e: e.dma_start(out=ks_o[:, 0:120, :], in_=ck[:, 8:128, :]), "ksw", is_output=True)
            S.dma("sp", lambda e: e.dma_start(out=vs_o[:, 0:120, :], in_=cv[:, 8:128, :]), "vsw", is_output=True)
            ckpt(5.2)
            kd4 = KDUP.rearrange("p (k d n) -> p k d n", d=2, n=64)
            trc = [0]

            def ktrans(src_rows, rows, dstcols):
                bank = 4 + (trc[0] % 2)
                trc[0] += 1

                def tr(e):
                    for kv in range(4):
                        ins = e.transpose(out=psb[:, bank, kv * 128:kv * 128 + rows], in_=KDUP[0:rows, kv * 128:(kv + 1) * 128],
                                          identity=ident[0:rows, 0:rows])
                    return ins
                S.op("pe", tr, reads=["KDUP", "ident"], writes=[("ps", bank)])
                S.op("act", lambda e: e.activation(
                    out=KT2[:, :, dstcols:dstcols + rows],
                    in_=psb[:, bank, 0:512].rearrange("p (k n) -> p k n", n=128)[:, :, 0:rows], func=AF.Copy),
                    reads=[("ps", bank)] + ALLT, writes=[("KT2", dstcols)])

            for b in range(4):
                S.op("dve", lambda e, b=b: e.tensor_copy(
                    out=kd4, in_=CKB[:, b, :].rearrange("p (k n) -> p k n", n=64).unsqueeze(2).broadcast_to([128, 4, 2, 64])),
                    reads=CKBK, writes=["KDUP"])
                ktrans(None, 128, 1184 + b * 128)

            ckpt(5.3)
            def build_tables(G, gkey):
                S.op("dve", lambda e: e.tensor_tensor(out=COS[:], in0=COS[:], in1=G[:].unsqueeze(1).broadcast_to([128, 10, 64]), op=ALU.mult),
                     reads=["COS", gkey], writes=["COS"])
                S.op("dve", lambda e: e.tensor_tensor(out=SIN[:, :, 0:32], in0=SIN[:, :, 0:32],
                                                      in1=G[:, 32:64].unsqueeze(1).broadcast_to([128, 10, 32]), op=ALU.mult),
                     reads=["SIN", gkey], writes=["SIN"])
                S.op("dve", lambda e: e.tensor_tensor(out=SIN[:, :, 32:64], in0=SIN[:, :, 32:64],
                                                      in1=G[:, 0:32].unsqueeze(1).broadcast_to([128, 10, 32]), op=ALU.mult),
                     reads=["SIN", gkey], writes=["SIN"])

            def qk_chain(bank, rows, t, cb, out_ap, out_keys):
                xs = XS[cb][0:rows, :]
                xs3 = xs.rearrange("p (h d) -> p h d", d=64)
                a3 = AB[cb][0:rows, :].rearrange("p (h d) -> p h d", d=64)
                b3 = BB[cb][0:rows, :].rearrange("p (h d) -> p h d", d=64)
                s4 = S4T[0:rows, cb * 4:cb * 4 + 4]
                xk, ak, bk, sk_ = "XS%d" % cb, "AB%d" % cb, "BB%d" % cb, "S4%d" % cb
                S.op("act", lambda e: e.activation(out=xs, in_=ps[0:rows, bank, 0:256], func=AF.Copy), reads=[("ps", bank)], writes=[xk])
                for h in range(4):
                    S.op("act", lambda e, h=h: e.activation(out=JUNK[0:rows, h * 64:(h + 1) * 64], in_=xs[:, h * 64:(h + 1) * 64],
                                                            func=AF.Square, accum_out=s4[:, h:h + 1]),
                         reads=[xk], writes=[("J", h), (sk_, h)])
                S.op("act", lambda e: e.activation(out=s4, in_=s4, func=AF.Sqrt, bias=EPSB[0:rows, :], scale=1.0 / 64),
                     reads=[(sk_, h) for h in range(4)] + ["EPSB"], writes=[sk_])
                S.op("dve", lambda e: e.tensor_tensor(out=a3, in0=xs3, in1=COS[0:rows, t, :].unsqueeze(1).broadcast_to([rows, 4, 64]), op=ALU.mult),
                     reads=[xk, "COS"], writes=[ak])
                S.op("dve", lambda e: e.tensor_tensor(out=b3[:, :, 0:32], in0=xs3[:, :, 32:64],
                                                      in1=SIN[0:rows, t, 0:32].unsqueeze(1).broadcast_to([rows, 4, 32]), op=ALU.mult),
                     reads=[xk, "SIN"], writes=[bk + "a"])
                S.op("dve", lambda e: e.tensor_tensor(out=b3[:, :, 32:64], in0=xs3[:, :, 0:32],
                                                      in1=SIN[0:rows, t, 32:64].unsqueeze(1).broadcast_to([rows, 4, 32]), op=ALU.mult),
                     reads=[xk, "SIN"], writes=[bk + "b"])
                S.op("dve", lambda e: e.reciprocal(out=s4, in_=s4), reads=[sk_], writes=[sk_])
                S.op("pool", lambda e: e.tensor_tensor(out=AB[cb][0:rows, :], in0=AB[cb][0:rows, :], in1=BB[cb][0:rows, :], op=ALU.add),
                     reads=[ak, bk + "a", bk + "b"], writes=[ak])
                S.op("pool", lambda e: e.tensor_tensor(out=out_ap, in0=a3, in1=s4.unsqueeze(2).broadcast_to([rows, 4, 64]), op=ALU.mult),
                     reads=[ak, sk_], writes=out_keys)

            pj = [0]

            def tm_proj(Wv, wk, t, rows):
                bank = 6 + (pj[0] % 2)
                pj[0] += 1

                def mm(e):
                    for k in range(16):
                        ins = e.matmul(ps[0:rows, bank, 0:256], lhsT=HT3[:, k, t * 128:t * 128 + rows], rhs=Wv[:, k, :],
                                       start=(k == 0), stop=(k == 15))
                    return ins
                S.op("pe", mm, reads=wk + [("H", t, 0), ("H", t, 1)], writes=[("ps", bank)])
                return bank

            def proj_stream(items, post_a, post_b=None):
                n = len(items)
                rws = [128 if it[2] < 9 else 32 for it in items]
                banks = [None] * n

                def pj_(i):
                    banks[i] = tm_proj(items[i][0], items[i][1], items[i][2], rws[i])
                pj_(0)
                if n > 1:
                    pj_(1)
                post_a(0, items[0][2], rws[0], banks[0], items[0][3])
                for i in range(n):
                    if i + 2 < n:
                        pj_(i + 2)
                    if i + 1 < n:
                        post_a(i + 1, items[i + 1][2], rws[i + 1], banks[i + 1], items[i + 1][3])
                    if post_b is not None:
                        post_b(i, items[i][2], rws[i], items[i][3])

            def proj_loop(Wv, wk, tiles, post_a, post_b=None):
                proj_stream([(Wv, wk, t, None) for t in tiles],
                            lambda i, t, rows, bank, tag: post_a(i, t, rows, bank),
                            None if post_b is None else (lambda i, t, rows, tag: post_b(i, t, rows)))

            build_tables(GK, "GK")
            Wk_, wkk = W.get(P_k)

            def k_post_a(i, t, rows, bank):
                KRB = KRBS[i % 2]
                qk_chain(bank, rows, t, i % 2, KRB[0:rows, :].rearrange("p (h d) -> p h d", d=64), ["KRB%d" % (i % 2)])

            def k_post_b(i, t, rows):
                KRB = KRBS[i % 2]
                kk = "KRB%d" % (i % 2)
                S.op("act", lambda e: e.activation(
                    out=kd4[0:rows], in_=KRB[0:rows, :].rearrange("p (k n) -> p k n", n=64).unsqueeze(2).broadcast_to([rows, 4, 2, 64]),
                    func=AF.Copy), reads=[kk], writes=["KDUP"])
                if t == 8:
                    S.dma("sp", lambda e: e.dma_start(out=kp_o, in_=KRB[:, :]), "kp", reads=[kk], is_output=True)
                if t == 9:
                    for b in range(4):
                        S.dma("sp", lambda e, b=b: e.dma_start(out=ks_o[b, 120:128, :], in_=KRB[b * 8:(b + 1) * 8, :]),
                              ("ksn", b), reads=[kk], is_output=True)
                ktrans(None, rows, t * 128)
            proj_loop(Wk_, wkk, ALL10, k_post_a, k_post_b)
            W.done(P_k)
            ckpt(5.4)
            S.dma("sp", lambda e: e.dma_start(out=COS[:], in_=cos_t), "cos", writes=["COS"])
            S.dma("sp", lambda e: e.dma_start(out=SIN[:], in_=sin_t), "sin", writes=["SIN"])
            build_tables(GQ, "GQ")
            Wv_, wkv_ = W.get(P_v)

            def v_post(i, t, rows, bank):
                S.op("act", lambda e: e.activation(out=V[0:rows, t, :], in_=ps[0:rows, bank, 0:256], func=AF.Copy),
                     reads=[("ps", bank)], writes=[("V", t)])
                if t >= 8:
                    S.op("dve", lambda e: e.tensor_copy(out=KRBS[0][0:rows, :], in_=ps[0:rows, bank, 0:256]),
                         reads=[("ps", bank), ("V", t)], writes=["KRB0"])
                    if t == 8:
                        S.dma("sp", lambda e: e.dma_start(out=vp_o, in_=KRBS[0][:, :]), "vp", reads=["KRB0"], is_output=True)
                    else:
                        for b in range(4):
                            S.dma("sp", lambda e, b=b: e.dma_start(out=vs_o[b, 120:128, :], in_=KRBS[0][b * 8:(b + 1) * 8, :]),
                                  ("vsn", b), reads=["KRB0"], is_output=True)
            proj_loop(Wv_, wkv_, ALL10, v_post)
            W.done(P_v)

            ckpt(6)
            QT3 = AT3[:, 0:4, :]
            OT3 = AT3[:, 4:8, :]
            KT2all = [("KT2", c) for c in [t * 128 for t in range(10)] + [1184 + b * 128 for b in range(4)]] + ALLT
            ptc = [0]
            TILES1 = list(range(1, 10))
            for g in range(4):
                qp, ao = P_att[g]
                Wq0, wkq0 = W.get(qp[0])
                Wq1, wkq1 = W.get(qp[1])

                def q_post_a(i, t, rows, bank, qh):
                    cb = i % 2
                    qk_chain(bank, rows, t, cb, QRB[cb][0:rows, :].rearrange("p (h d) -> p h d", d=64), ["QRB%d" % cb])

                def q_post_b(i, t, rows, qh):
                    cb = i % 2
                    qb = QRB[cb]
                    qbk = "QRB%d" % cb
                    tb = 4 + (trc[0] % 2)
                    trc[0] += 1

                    def tr(e):
                        for pr in range(2):
                            ins = e.transpose(out=psb[:, tb, pr * 128:pr * 128 + rows], in_=qb[0:rows, pr * 128:(pr + 1) * 128],
                                              identity=ident[0:rows, 0:rows])
                        return ins
                    S.op("pe", tr, reads=[qbk, "ident"], writes=[("ps", tb)])
                    lc = t * 128 - 128
                    S.op("act", lambda e: e.activation(
                        out=QT3[:, qh * 2:qh * 2 + 2, lc:lc + rows],
                        in_=psb[:, tb, 0:256].rearrange("p (k n) -> p k n", n=128)[:, :, 0:rows], func=AF.Copy),
                        reads=[("ps", tb)], writes=[("A", qh * 2), ("A", qh * 2 + 1)])
                proj_stream([(Wq0, wkq0, t, 0) for t in TILES1] + [(Wq1, wkq1, t, 1) for t in TILES1], q_post_a, q_post_b)
                W.done(qp[0])
                W.done(qp[1])
                QK_ = [("A", c) for c in range(4)]
                OK_ = [("A", c) for c in range(4, 8)]
                bufs = {}
                for n in range(1, 9):
                    bufs[n] = ptc[0] % 2
                    ptc[0] += 1

                def score_exp(n, kbi, g=g):
                    buf = bufs[n]
                    P4 = PT[buf].rearrange("p (k r n) -> p k r n", k=2, r=2)
                    lc = (n - 1) * 128
                    kt = (n - 1, n)[kbi]
                    mi = kbi if (kbi == 1 or n > 1) else 2

                    def st(e):
                        for par in range(2):
                            bank = kbi * 2 + par
                            ph = slice(par * 64, par * 64 + 64)
                            ins = e.matmul(ps[:, bank, :], lhsT=KT2[ph, g, kt * 128:(kt + 1) * 128], rhs=QT3[ph, :, lc:lc + 128],
                                           start=True, stop=(kbi == 0))
                            if kbi == 1:
                                ins = e.matmul(ps[:, bank, :], lhsT=ident[:, :],
                                               rhs=maskb[:, mi, :].unsqueeze(1).broadcast_to([128, 4, 128]), start=False, stop=True)
                        return ins
                    S.op("pe", st, reads=KT2all + QK_ + ["ident", "maskb"], writes=[("ps", kbi * 2), ("ps", kbi * 2 + 1)])
                    S.op("act", lambda e: e.activation(
                        out=P4[:, kbi], in_=ps[:, kbi * 2:kbi * 2 + 2, :], func=AF.Exp, bias=NEGM, scale=0.125),
                        reads=[("ps", kbi * 2), ("ps", kbi * 2 + 1), "NEGM"], writes=[PTK[buf][kbi]])
                    if kbi == 0:
                        pm = P4[:, kbi].rearrange("p r (a q) -> p (r a) q", q=128)
                        S.op("pool", lambda e: e.tensor_tensor(
                            out=pm, in0=pm, in1=mask01[:, mi, :].unsqueeze(1).broadcast_to([128, 8, 128]), op=ALU.mult),
                            reads=[PTK[buf][kbi], "mask01"], writes=[PTK[buf][kbi]])

                def pv_norm(n, g=g):
                    buf = bufs[n]
                    P4 = PT[buf].rearrange("p (k r n) -> p k r n", k=2, r=2)
                    lc = (n - 1) * 128
                    bo = 4 + 2 * (n % 2)
                    bd = bo + 1

                    def pv(e):
                        for par in range(2):
                            ph = slice(par * 64, par * 64 + 64)
                            for kbi, kt in enumerate((n - 1, n)):
                                e.matmul(ps[ph, bo, :], lhsT=V[:, kt, g * 64:(g + 1) * 64], rhs=P4[:, kbi, par, :],
                                         start=(kbi == 0), stop=(kbi == 1), tile_position=(0, par * 64))
                        for par in range(2):
                            ph = slice(par * 64, par * 64 + 64)
                            for kbi in range(2):
                                ins = e.matmul(ps[ph, bd, :], lhsT=ones64[:, :], rhs=P4[:, kbi, par, :],
                                               start=(kbi == 0), stop=(kbi == 1), tile_position=(0, par * 64))
                        return ins
                    S.op("pe", pv, reads=PTK[buf] + [("V", n - 1), ("V", n), "ones64"], writes=[("ps", bo), ("ps", bd)])

                RDS = [(RD, RDK), (ATT[:, 2304:2816], ["AB0", "AB1"])]

                def norm_add(n, g=g):
                    rd, rdk = RDS[n % 2]
                    bd = 4 + 2 * (n % 2) + 1
                    S.op("dve", lambda e: e.tensor_tensor(
                        out=rd.rearrange("p (a q) -> p a q", q=128), in0=ps[:, bd, :].rearrange("p (a q) -> p a q", q=128),
                        in1=SE[:, 4 * g:4 * g + 4].unsqueeze(2).broadcast_to([128, 4, 128]), op=ALU.add),
                        reads=[("ps", bd), "SEa", "SEb"], writes=rdk)

                def norm_fin(n, g=g):
                    rd, rdk = RDS[n % 2]
                    lc = (n - 1) * 128
                    bo = 4 + 2 * (n % 2)
                    S.op("act", lambda e: e.activation(out=rd, in_=rd, func=AF.Ln), reads=rdk, writes=rdk)
                    S.op("act", lambda e: e.activation(out=rd, in_=rd, func=AF.Exp, scale=-1.0), reads=rdk, writes=rdk)
                    S.op("dve", lambda e: e.tensor_tensor(
                        out=OT3[:, :, lc:lc + 128], in0=ps[:, bo, :].rearrange("p (a q) -> p a q", q=128),
                        in1=rd.rearrange("p (a q) -> p a q", q=128), op=ALU.mult),
                        reads=[("ps", bo)] + rdk, writes=OK_)

                for i in range(1, 11):
                    if i <= 8:
                        score_exp(i, 0)
                        score_exp(i, 1)
                    if 1 <= i - 1 <= 8:
                        pv_norm(i - 1)
                        norm_add(i - 1)
                    if 1 <= i - 2 <= 8:
                        norm_fin(i - 2)
                buf = ptc[0] % 2
                ptc[0] += 1
                PTc = PT[buf][:, 0:256].rearrange("p (r n) -> p r n", r=2)
                PTn = PT[buf][0:32, 256:512].rearrange("p (r n) -> p r n", r=2)

                def s_mm(e, g=g):
                    for par in range(2):
                        ph = slice(par * 64, par * 64 + 64)
                        e.matmul(ps[:, par, 0:128], lhsT=ident[:, :],
                                 rhs=maskb[:, 0, 0:8].unsqueeze(1).broadcast_to([128, 16, 8]), start=True, stop=False)
                        for b in range(4):
                            e.matmul(ps[:, par, b * 32:(b + 1) * 32], lhsT=KT2[ph, g, 1184 + b * 128:1184 + (b + 1) * 128],
                                     rhs=QT3[ph, :, 1024 + b * 8:1024 + (b + 1) * 8], start=False, stop=(b == 3))
                    for par in range(2):
                        ph = slice(par * 64, par * 64 + 64)
                        e.matmul(ps[0:32, 2 + par, 0:128], lhsT=KT2[ph, g, 1152:1184],
                                 rhs=QT3[ph, :, 1024:1056].rearrange("p a (b t) -> p b a t", t=8), start=True, stop=False)
                        ins = e.matmul(ps[0:32, 2 + par, 0:128], lhsT=ident[0:32, 0:32],
                                       rhs=msb[:, :].rearrange("p (b t) -> p b t", t=8).unsqueeze(2).broadcast_to([32, 4, 4, 8]),
                                       start=False, stop=True)
                    return ins
                S.op("pe", s_mm, reads=KT2all + QK_ + ["ident", "maskb", "msb"], writes=[("ps", b) for b in range(4)])
                S.op("act", lambda e, PTc=PTc: e.activation(out=PTc, in_=ps[:, 0:2, 0:128], func=AF.Exp, bias=NEGM, scale=0.125),
                     reads=[("ps", 0), ("ps", 1), "NEGM"], writes=PTK[buf])
                S.op("act", lambda e, PTn=PTn: e.activation(out=PTn, in_=ps[0:32, 2:4, 0:128], func=AF.Exp, bias=NEGM[0:32, :], scale=0.125),
                     reads=[("ps", 2), ("ps", 3), "NEGM"], writes=PTK[buf])

                def s_pv(e, PTc=PTc, PTn=PTn, g=g):
                    for bank, use_v in ((4, True), (5, False)):
                        for par in range(2):
                            ph = slice(par * 64, par * 64 + 64)
                            lhs_n = V[0:32, 9, g * 64:(g + 1) * 64] if use_v else ones64[0:32, :]
                            e.matmul(ps[ph, bank, 0:128], lhsT=lhs_n, rhs=PTn[:, par, :], start=True, stop=False,
                                     tile_position=(0, par * 64))
                            for b in range(4):
                                lhs_c = Vc[:, b, g * 64:(g + 1) * 64] if use_v else ones64[:, :]
                                ins = e.matmul(ps[ph, bank, b * 32:(b + 1) * 32],
                                               lhsT=lhs_c, rhs=PTc[:, par, b * 32:(b + 1) * 32],
                                               start=False, stop=(b == 3), tile_position=(0, par * 64))
                    return ins
                S.op("pe", s_pv, reads=PTK[buf] + [("V", 9), "VC", "ones64"], writes=[("ps", 4), ("ps", 5)])
                rds = RD[:, 0:128].rearrange("p (b a t) -> p b a t", b=4, t=8)
                S.op("dve", lambda e, rds=rds, g=g: e.tensor_tensor(
                    out=rds, in0=ps[:, 5, 0:128].rearrange("p (b a t) -> p b a t", b=4, t=8),
                    in1=SE[:, 4 * g:4 * g + 4].unsqueeze(1).unsqueeze(3).broadcast_to([128, 4, 4, 8]), op=ALU.add),
                    reads=[("ps", 5), "SEa", "SEb"], writes=RDK)
                S.op("dve", lambda e: e.reciprocal(out=RD[:, 0:128], in_=RD[:, 0:128]), reads=RDK, writes=RDK)
                S.op("dve", lambda e, rds=rds: e.tensor_tensor(
                    out=OT3[:, :, 1024:1056].rearrange("p a (b t) -> p b a t", t=8),
                    in0=ps[:, 4, 0:128].rearrange("p (b a t) -> p b a t", b=4, t=8), in1=rds, op=ALU.mult),
                    reads=[("ps", 4)] + RDK, writes=OK_)
                if g == 3:
                    NP3 = NormPipe(3, TILES1, 32, HB_X0)
                for hf in range(2):
                    Wa, wka = W.get(ao[hf])
                    for t in TILES1:
                        rows = 128 if t < 9 else 32
                        lc = t * 128 - 128
                        for sub in range(2):
                            bank = 6 + (dn_cnt[0] % 2)
                            dn_cnt[0] += 1
                            xg = hf * 2 + sub

                            def mm(e, Wa=Wa, rows=rows, lc=lc, bank=bank, sub=sub):
                                for c in range(4):
                                    ins = e.matmul(ps[0:rows, bank, :], lhsT=OT3[:, c, lc:lc + rows],
                                                   rhs=Wa[:, c, sub * 512:(sub + 1) * 512], start=(c == 0), stop=(c == 3))
                                return ins
                            S.op("pe", mm, reads=wka + OK_, writes=[("ps", bank)])
                            S.op("dve", lambda e, t=t, rows=rows, bank=bank, xg=xg: e.tensor_tensor(
                                out=X[0:rows, t, xg * 512:(xg + 1) * 512], in0=X[0:rows, t, xg * 512:(xg + 1) * 512],
                                in1=ps[0:rows, bank, :], op=ALU.add), reads=[("ps", bank), ("X", t, xg)], writes=[("X", t, xg)])
                        if g == 3 and hf == 1:
                            NP3.tile_ready(t)
                    W.done(ao[hf])

            ckpt(7)
            TG1 = [(128, 640), (640, 1152), (1152, 1184)]
            if not skip_mlp1:
                mlp(NP3, P_mlp1, TILES1, 128, 32, TG1)

        try:
            record()
        except _Stop:
            pass
        TILES1 = list(range(1, 10))
        for g4 in range(4):
            for t in TILES1:
                rows = 128 if t < 9 else 32
                S.dma("sp", lambda e, t=t, rows=rows, g4=g4: e.dma_start(
                    out=y_o[(t - 1) * 128:(t - 1) * 128 + rows, g4 * 512:(g4 + 1) * 512], in_=X[0:rows, t, g4 * 512:(g4 + 1) * 512]),
                    ("x", t), reads=[("X", t, g4)], is_output=True)

        with nc.allow_low_precision("bf16 matmul operands with fp32 PSUM accumulation"):
            S.emit(st)
    return nc


_PROGRAM = None


def _rope_cos_sin(pos):
    half = 32
    try:
        import jax
        import jax.numpy as jnp
        cpu = jax.devices("cpu")[0]
        with jax.default_device(cpu):
            inv = 10000.0 ** (-jnp.arange(half, dtype=jnp.float32) / half)
            ang = jnp.asarray(pos, dtype=jnp.float32)[..., None] * inv
            return np.asarray(jnp.cos(ang), dtype=np.float32), np.asarray(jnp.sin(ang), dtype=np.float32)
    except Exception:
        inv = (np.float32(10000.0) ** (-(np.arange(half, dtype=np.float32)) / np.float32(half))).astype(np.float32)
        ang = (np.asarray(pos, np.float32)[..., None] * inv).astype(np.float32)
        return np.cos(ang).astype(np.float32), np.sin(ang).astype(np.float32)


def _tables(s, first):
    pos = np.zeros((128, 10), np.float32)
    r = np.arange(128)
    for t in range(9):
        pos[:, t] = s - 128 + 128 * t + r
    pos[:32, 9] = PAST + (r[:32] % 8)
    c, sn = _rope_cos_sin(pos)
    cos2 = np.concatenate([c, c], axis=-1)
    sinm = np.concatenate([-sn, sn], axis=-1)
    j = np.arange(128)[:, None]
    i = np.arange(128)[None, :]
    masks = np.zeros((128, 3, 128), np.float32)
    NEG = -30000.0
    masks[:, 0, :] = np.where(j > i, 0.0, NEG)
    masks[:, 1, :] = np.where(j <= i, 0.0, NEG)
    masks[:, 2, :] = NEG if first else np.where(j > i, 0.0, NEG)
    jj = np.arange(32)[:, None]
    qq = np.arange(32)[None, :]
    mask_s = np.where((jj // 8 == qq // 8) & ((jj % 8) <= (qq % 8)), 0.0, NEG).astype(np.float32)
    return np.ascontiguousarray(cos2), np.ascontiguousarray(sinm), masks, mask_s


def make_in_maps(x_prompt, x_sample, state_conv, cache_k_win, cache_v_win, ln_mix, ln_mlp,
                 w_conv_in, w_conv, w_conv_out, w_qkv, w_attn_out, q_norm, k_norm, sinks, w_up, w_down):
    f = lambda a: np.ascontiguousarray(np.asarray(a, dtype=np.float32))
    x_prompt, x_sample, state_conv = f(x_prompt), f(x_sample), f(state_conv)
    cache_k_win, cache_v_win = f(cache_k_win), f(cache_v_win)
    shared = {
        "ln": f(np.stack([np.asarray(ln_mix)[0], np.asarray(ln_mlp)[0], np.asarray(ln_mix)[1], np.asarray(ln_mlp)[1]])),
        "w_in": f(np.asarray(w_conv_in)[0]),
        "w_out": f(np.asarray(w_conv_out)[0]),
        "w_qkv": f(np.asarray(w_qkv)[0]),
        "w_ao": f(np.asarray(w_attn_out)[0]),
        "qn": f(np.asarray(q_norm)[0:1]),
        "kn": f(np.asarray(k_norm)[0:1]),
        "sinks": f(np.asarray(sinks)[0:1]),
        "w_up": f(w_up),
        "w_dn": f(w_down),
    }
    wc = f(np.asarray(w_conv)[0])
    in_maps = []
    for c in range(NCORES):
        bi, qi = c // 4, c % 4
        s = qi * 1024
        xt = np.zeros((NT, D), np.float32)
        if qi > 0:
            xt[0:128] = x_prompt[bi, s - 128:s]
            xt[1184:1186] = x_prompt[bi, s - 130:s - 128]
        xt[128:1152] = x_prompt[bi, s:s + 1024]
        xt[1152:1184] = x_sample[4 * c:4 * c + 4].reshape(32, D)
        swc = np.concatenate([state_conv[0, 4 * c:4 * c + 4].reshape(8, D), wc], axis=0)
        cos2, sinm, masks, mask_s = _tables(s, qi == 0)
        m = dict(shared)
        m.update({
            "xtok": xt, "swc": np.ascontiguousarray(swc),
            "ck": np.ascontiguousarray(cache_k_win[0, 4 * c:4 * c + 4].reshape(4, 128, 256)),
            "cv": np.ascontiguousarray(cache_v_win[0, 4 * c:4 * c + 4].reshape(4, 128, 256)),
            "cos_t": cos2, "sin_t": sinm, "masks": masks, "mask_s": mask_s,
        })
        in_maps.append(m)
    return in_maps


def assemble(R):
    y_prompt = np.zeros((2, 4096, D), np.float32)
    y_sample = np.zeros((32, 8, D), np.float32)
    ncp = np.zeros((1, 2, 2, D), np.float32)
    ncs = np.zeros((1, 32, 2, D), np.float32)
    kp = np.zeros((1, 2, 128, 4, 64), np.float32)
    vp = np.zeros((1, 2, 128, 4, 64), np.float32)
    ks = np.zeros((1, 32, 128, 4, 64), np.float32)
    vs = np.zeros((1, 32, 128, 4, 64), np.float32)
    for c in range(NCORES):
        if R[c] is None:
            continue
        bi, qi = c // 4, c % 4
        s = qi * 1024
        r = R[c]
        y_prompt[bi, s:s + 1024] = r["y"][0:1024]
        y_sample[4 * c:4 * c + 4] = r["y"][1024:1056].reshape(4, 8, D)
        ncs[0, 4 * c:4 * c + 4] = r["ncv"][2:10].reshape(4, 2, D)
        ks[0, 4 * c:4 * c + 4] = r["ks"].reshape(4, 128, 4, 64)
        vs[0, 4 * c:4 * c + 4] = r["vs"].reshape(4, 128, 4, 64)
        if qi == 3:
            ncp[0, bi] = r["ncv"][0:2]
            kp[0, bi] = r["kp"].reshape(128, 4, 64)
            vp[0, bi] = r["vp"].reshape(128, 4, 64)
    return (y_prompt, y_sample, ncp, ncs, kp, vp, ks, vs)


def kernel(**inputs):
    global _PROGRAM
    if _PROGRAM is None:
        _PROGRAM = build_program()
    in_maps = make_in_maps(**inputs)
    res = run_bass_kernel_spmd(_PROGRAM, in_maps, core_ids=list(range(NCORES)))
    return assemble(res.results)
```
